# Optimizing a Trainium2 kernel written in Bass

```python
import jax, jax.numpy as jnp
from jax import lax
import numpy as np

D_MODEL = 2048
BATCH = 2
SEQ = 8192
DEPTH = 2

HEAD_DIM = 128
ROT_DIM = HEAD_DIM // 4
ROPE_THETA = 500000.0
NORM_EPS = 1e-6
D_FF = 4 * D_MODEL
D_PLE = 256
N_A_LAYERS = DEPTH // 2
N_B_LAYERS = DEPTH - N_A_LAYERS
QBLOCK = 128
NEG_INF = -1e30
FORCE_SCORE = 1e9
TINY = 1e-20

DILATED_GROUPS = ((128, 1), (512, 4), (2048, 16))
HEADS_PER_GROUP_A = D_MODEL // (2 * HEAD_DIM)

NSA_HEADS = D_MODEL // HEAD_DIM
NSA_KV_GROUPS = 4
CMP_LEN = 32
CMP_STRIDE = 16
CMP_HIDDEN = 2 * HEAD_DIM
SLC_LEN = 64
SLC_TOP_N = 16
WIN_LEN = 512
N_NSA_BRANCH = 3

kernel_name = "yoco_dilated_nsa_hybrid"


def rmsnorm(x, g):
    xf = x.astype(jnp.float32)
    y = xf * lax.rsqrt(jnp.mean(xf * xf, axis=-1, keepdims=True) + NORM_EPS)
    return (y * g.astype(jnp.float32)).astype(x.dtype)


def rope_partial(x, pos):
    half = ROT_DIM // 2
    inv = ROPE_THETA ** (-jnp.arange(half, dtype=jnp.float32) * (2.0 / ROT_DIM))
    ang = pos.astype(jnp.float32)[:, None] * inv[None, :]
    cos = jnp.cos(ang)[:, None, :]
    sin = jnp.sin(ang)[:, None, :]
    xf = x.astype(jnp.float32)
    x1 = xf[..., :half]
    x2 = xf[..., half:ROT_DIM]
    out = jnp.concatenate([x1 * cos - x2 * sin, x2 * cos + x1 * sin, xf[..., ROT_DIM:]], axis=-1)
    return out.astype(x.dtype)


def dilated_mixture_attention(xn, w_in, w_out):
    B, S, _ = xn.shape
    G, Hg, dh = len(DILATED_GROUPS), HEADS_PER_GROUP_A, HEAD_DIM
    nb = S // QBLOCK
    pos = jnp.arange(S)
    qkv = (xn @ w_in).reshape(B, S, 3, G * Hg, dh)
    q = (rope_partial(qkv[:, :, 0], pos) * (dh ** -0.5)).reshape(B, S, G, Hg, dh)
    k = rope_partial(qkv[:, :, 1], pos).reshape(B, S, G, Hg, dh)
    v = qkv[:, :, 2].reshape(B, S, G, Hg, dh)
    q_blocks = q.reshape(B, nb, QBLOCK, G, Hg, dh).transpose(1, 0, 2, 3, 4, 5)

    def block(args):
        bi, qb = args
        t = bi * QBLOCK + jnp.arange(QBLOCK)
        outs, lses = [], []
        for g, (window, dil) in enumerate(DILATED_GROUPS):
            n_keys = window // dil + 1
            idx = t[:, None] - dil * jnp.arange(n_keys)[None, :]
            valid = idx >= 0
            idx = jnp.maximum(idx, 0)
            kg = jnp.take(k[:, :, g], idx, axis=1)
            vg = jnp.take(v[:, :, g], idx, axis=1)
            s = jnp.einsum('bqhd,bqnhd->bhqn', qb[:, :, g], kg, preferred_element_type=jnp.float32)
            s = jnp.where(valid[None, None], s, NEG_INF)
            lse = jax.nn.logsumexp(s, axis=-1)
            pr = jnp.exp(s - lse[..., None])
            outs.append(jnp.einsum('bhqn,bqnhd->bqhd', pr.astype(vg.dtype), vg).astype(jnp.float32))
            lses.append(lse)
        wts = jax.nn.softmax(jnp.stack(lses), axis=0)
        wts = wts.transpose(0, 1, 3, 2)[..., None]
        o = jnp.sum(wts * jnp.stack(outs), axis=0)
        return o.reshape(B, QBLOCK, Hg * dh).astype(xn.dtype)

    o = lax.map(block, (jnp.arange(nb), q_blocks))
    o = o.transpose(1, 0, 2, 3).reshape(B, S, Hg * dh)
    return o @ w_out


def compress_blocks(blocks, pe, w1, w2):
    B, n, L, G, dh = blocks.shape
    z = (blocks + pe[None, None, :, None, :]).transpose(0, 1, 3, 2, 4).reshape(B, n, G, L * dh)
    return jax.nn.gelu(z @ w1) @ w2


def nsa_shared_kv(hn, w_kv, cmp_pe_k, cmp_w1_k, cmp_w2_k, cmp_pe_v, cmp_w1_v, cmp_w2_v):
    B, S, _ = hn.shape
    G, dh = NSA_KV_GROUPS, HEAD_DIM
    pos = jnp.arange(S)
    kv = (hn @ w_kv).reshape(B, S, 6, G, dh)
    k_cmp_raw, v_cmp_raw = kv[:, :, 0], kv[:, :, 1]
    k_slc = rope_partial(kv[:, :, 2], pos)
    v_slc = kv[:, :, 3]
    k_win = rope_partial(kv[:, :, 4], pos)
    v_win = kv[:, :, 5]
    n_cmp = (S - CMP_LEN) // CMP_STRIDE + 1
    starts = jnp.arange(n_cmp) * CMP_STRIDE
    idx = starts[:, None] + jnp.arange(CMP_LEN)[None, :]
    k_cmp = compress_blocks(jnp.take(k_cmp_raw, idx, axis=1), cmp_pe_k, cmp_w1_k, cmp_w2_k)
    k_cmp = rope_partial(k_cmp, starts + CMP_LEN - 1)
    v_cmp = compress_blocks(jnp.take(v_cmp_raw, idx, axis=1), cmp_pe_v, cmp_w1_v, cmp_w2_v)
    n_slc = S // SLC_LEN
    k_slc_b = k_slc.reshape(B, n_slc, SLC_LEN, G, dh).transpose(0, 3, 1, 2, 4)
    v_slc_b = v_slc.reshape(B, n_slc, SLC_LEN, G, dh).transpose(0, 3, 1, 2, 4)
    pad = ((0, 0), (WIN_LEN, 0), (0, 0), (0, 0))
    return (k_cmp, v_cmp, k_slc_b, v_slc_b, jnp.pad(k_win, pad), jnp.pad(v_win, pad))


def nsa_attention(xn, w_qg, w_out, k_cmp, v_cmp, k_slc_b, v_slc_b, k_win_pad, v_win_pad):
    B, S, _ = xn.shape
    H, G, dh = NSA_HEADS, NSA_KV_GROUPS, HEAD_DIM
    hpg = H // G
    nb = S // QBLOCK
    n_cmp = k_cmp.shape[1]
    n_slc = k_slc_b.shape[2]
    top_n = min(SLC_TOP_N, n_slc)
    pos = jnp.arange(S)
    qg = xn @ w_qg
    q = (rope_partial(qg[..., :H * dh].reshape(B, S, H, dh), pos) * (dh ** -0.5)).reshape(B, S, G, hpg, dh)
    gates = jax.nn.sigmoid(qg[..., H * dh:].astype(jnp.float32)).reshape(B, S, N_NSA_BRANCH, G, hpg)
    q_blocks = q.reshape(B, nb, QBLOCK, G, hpg, dh).transpose(1, 0, 2, 3, 4, 5)
    g_blocks = gates.reshape(B, nb, QBLOCK, N_NSA_BRANCH, G, hpg).transpose(1, 0, 2, 3, 4, 5)

    cmp_start = jnp.arange(n_cmp) * CMP_STRIDE
    cmp_end = cmp_start + CMP_LEN - 1
    slc_start = jnp.arange(n_slc) * SLC_LEN
    overlap = jnp.clip(jnp.minimum(cmp_start[:, None] + CMP_LEN, slc_start[None, :] + SLC_LEN)
                       - jnp.maximum(cmp_start[:, None], slc_start[None, :]), 0, None)
    overlap = overlap.astype(jnp.float32) / CMP_LEN
    bix = jnp.arange(B)[:, None, None, None]
    gix = jnp.arange(G)[None, None, :, None]

    def block(args):
        bi, qb, gb = args
        t = bi * QBLOCK + jnp.arange(QBLOCK)
        s_c = jnp.einsum('bqghd,bcgd->bqghc', qb, k_cmp, preferred_element_type=jnp.float32)
        vc = (cmp_end[None, :] <= t[:, None])[None, :, None, None, :]
        s_c = jnp.where(vc, s_c, NEG_INF)
        e = jnp.where(vc, jnp.exp(s_c - jnp.max(s_c, axis=-1, keepdims=True)), 0.0)
        p_c = e / jnp.maximum(jnp.sum(e, axis=-1, keepdims=True), TINY)
        o_cmp = jnp.einsum('bqghc,bcgd->bqghd', p_c.astype(v_cmp.dtype), v_cmp).astype(jnp.float32)
        imp = jnp.einsum('bqghc,cj->bqgj', p_c, overlap)
        jidx = jnp.arange(n_slc)[None, :]
        cur = (t // SLC_LEN)[:, None]
        forced = ((jidx == 0) | (jidx == cur) | (jidx == cur - 1))[None, :, None, :]
        causal = (jidx * SLC_LEN <= t[:, None])[None, :, None, :]
        score = jnp.where(forced, FORCE_SCORE, jnp.where(causal, imp, NEG_INF))
        _, sel = lax.top_k(score, top_n)
        ks = k_slc_b[bix, gix, sel]
        vs = v_slc_b[bix, gix, sel].reshape(B, QBLOCK, G, top_n * SLC_LEN, dh)
        tok = sel[..., None] * SLC_LEN + jnp.arange(SLC_LEN)
        vsm = (tok <= t[None, :, None, None, None]).reshape(B, QBLOCK, G, 1, top_n * SLC_LEN)
        s_s = jnp.einsum('bqghd,bqgnld->bqghnl', qb, ks, preferred_element_type=jnp.float32)
        s_s = jnp.where(vsm, s_s.reshape(B, QBLOCK, G, hpg, top_n * SLC_LEN), NEG_INF)
        p_s = jax.nn.softmax(s_s, axis=-1)
        o_slc = jnp.einsum('bqghk,bqgkd->bqghd', p_s.astype(vs.dtype), vs).astype(jnp.float32)
        t0 = bi * QBLOCK
        kw = lax.dynamic_slice_in_dim(k_win_pad, t0, WIN_LEN + QBLOCK, axis=1)
        vw = lax.dynamic_slice_in_dim(v_win_pad, t0, WIN_LEN + QBLOCK, axis=1)
        kp = t0 - WIN_LEN + jnp.arange(WIN_LEN + QBLOCK)
        dist = t[:, None] - kp[None, :]
        vwm = ((dist >= 0) & (dist < WIN_LEN) & (kp[None, :] >= 0))[None, :, None, None, :]
        s_w = jnp.einsum('bqghd,bkgd->bqghk', qb, kw, preferred_element_type=jnp.float32)
        p_w = jax.nn.softmax(jnp.where(vwm, s_w, NEG_INF), axis=-1)
        o_win = jnp.einsum('bqghk,bkgd->bqghd', p_w.astype(vw.dtype), vw).astype(jnp.float32)
        o = (gb[:, :, 0, ..., None] * o_cmp + gb[:, :, 1, ..., None] * o_slc
             + gb[:, :, 2, ..., None] * o_win)
        return o.reshape(B, QBLOCK, H * dh).astype(xn.dtype)

    o = lax.map(block, (jnp.arange(nb), q_blocks, g_blocks))
    o = o.transpose(1, 0, 2, 3).reshape(B, S, H * dh)
    return o @ w_out


def setup_inputs(seed: int = 0) -> dict:
    key = jax.random.key(seed)
    ks = jax.random.split(key, 24)
    f32 = jnp.float32

    def nrm(k, shape, fan_in, scale=1.0):
        return jax.random.normal(k, shape, f32) * (scale * fan_in ** -0.5)

    def gain(k, shape):
        return 1.0 + 0.02 * jax.random.normal(k, shape, f32)

    GA = len(DILATED_GROUPS)
    a_in_cols = 3 * GA * HEADS_PER_GROUP_A * HEAD_DIM
    a_out_rows = HEADS_PER_GROUP_A * HEAD_DIM
    b_q_cols = NSA_HEADS * HEAD_DIM + N_NSA_BRANCH * NSA_HEADS
    kv_cols = 6 * NSA_KV_GROUPS * HEAD_DIM
    return {
        "x": jax.random.normal(ks[0], (BATCH, SEQ, D_MODEL), f32),
        "p": jax.random.normal(ks[1], (DEPTH, BATCH, SEQ, D_PLE), f32),
        "a_w_in": nrm(ks[2], (N_A_LAYERS, D_MODEL, a_in_cols), D_MODEL),
        "a_w_out": nrm(ks[3], (N_A_LAYERS, a_out_rows, D_MODEL), a_out_rows),
        "b_w_qg": nrm(ks[4], (N_B_LAYERS, D_MODEL, b_q_cols), D_MODEL),
        "b_w_out": nrm(ks[5], (N_B_LAYERS, NSA_HEADS * HEAD_DIM, D_MODEL), NSA_HEADS * HEAD_DIM),
        "kv_norm_g": gain(ks[6], (D_MODEL,)),
        "w_kv_shared": nrm(ks[7], (D_MODEL, kv_cols), D_MODEL),
        "cmp_pe_k": 0.1 * jax.random.normal(ks[8], (CMP_LEN, HEAD_DIM), f32),
        "cmp_w1_k": nrm(ks[9], (CMP_LEN * HEAD_DIM, CMP_HIDDEN), CMP_LEN * HEAD_DIM),
        "cmp_w2_k": nrm(ks[10], (CMP_HIDDEN, HEAD_DIM), CMP_HIDDEN, 2.0),
        "cmp_pe_v": 0.1 * jax.random.normal(ks[11], (CMP_LEN, HEAD_DIM), f32),
        "cmp_w1_v": nrm(ks[12], (CMP_LEN * HEAD_DIM, CMP_HIDDEN), CMP_LEN * HEAD_DIM),
        "cmp_w2_v": nrm(ks[13], (CMP_HIDDEN, HEAD_DIM), CMP_HIDDEN, 2.0),
        "attn_norm_g": gain(ks[14], (DEPTH, D_MODEL)),
        "mlp_norm_g": gain(ks[15], (DEPTH, D_MODEL)),
        "mlp_w1": nrm(ks[16], (DEPTH, D_MODEL, D_FF), D_MODEL),
        "mlp_w2": nrm(ks[17], (DEPTH, D_FF, D_MODEL), D_FF, 0.5),
        "ple_norm_g": gain(ks[18], (DEPTH, D_MODEL)),
        "ple_w_gate": nrm(ks[19], (DEPTH, D_MODEL, D_MODEL), D_MODEL),
        "ple_w_proj": nrm(ks[20], (DEPTH, D_PLE, D_MODEL), D_PLE, 0.5),
        "final_norm_g": gain(ks[21], (D_MODEL,)),
    }


def reference(x, p, a_w_in, a_w_out, b_w_qg, b_w_out, kv_norm_g, w_kv_shared,
              cmp_pe_k, cmp_w1_k, cmp_w2_k, cmp_pe_v, cmp_w1_v, cmp_w2_v,
              attn_norm_g, mlp_norm_g, mlp_w1, mlp_w2, ple_norm_g, ple_w_gate, ple_w_proj,
              final_norm_g):
    h = x
    shared = None
    for i in range(DEPTH):
        hn = rmsnorm(h, attn_norm_g[i])
        if i < N_A_LAYERS:
            h = h + dilated_mixture_attention(hn, a_w_in[i], a_w_out[i])
        else:
            if i == N_A_LAYERS:
                shared = nsa_shared_kv(rmsnorm(h, kv_norm_g), w_kv_shared,
                                       cmp_pe_k, cmp_w1_k, cmp_w2_k, cmp_pe_v, cmp_w1_v, cmp_w2_v)
            j = i - N_A_LAYERS
            h = h + nsa_attention(hn, b_w_qg[j], b_w_out[j], *shared)
        hn = rmsnorm(h, mlp_norm_g[i])
        h = h + jnp.square(jax.nn.relu(hn @ mlp_w1[i])) @ mlp_w2[i]
        gate = jax.nn.sigmoid(rmsnorm(h, ple_norm_g[i]) @ ple_w_gate[i])
        h = h + (p[i] @ ple_w_proj[i]) * gate
    return rmsnorm(h, final_norm_g)
```

```python
import numpy as np
import ml_dtypes
import concourse.bass as bass
import concourse.mybir as mybir
from concourse.bass_utils import run_bass_kernel_spmd

F32 = mybir.dt.float32
BF16 = mybir.dt.bfloat16
AF = mybir.ActivationFunctionType
ALU = mybir.AluOpType

ENGS = ("pe", "act", "dve", "pool", "sp")
NDMA_SEMS = 12


class Buf:
    __slots__ = ("name", "last_w", "readers", "exclusive")

    def __init__(self, name, exclusive=False):
        self.name = name
        self.last_w = None
        self.readers = []
        self.exclusive = exclusive


class Op:
    __slots__ = ("eng", "emit", "deps", "needs_inc", "event", "waits", "is_dma")

    def __init__(self, eng, emit, is_dma=False):
        self.eng = eng
        self.emit = emit
        self.deps = []
        self.needs_inc = is_dma
        self.event = None
        self.waits = []
        self.is_dma = is_dma


class Prog:
    def __init__(self, nc):
        self.nc = nc
        self.ops = []
        self.nbuf = 0

    def buf(self, name=None, exclusive=False):
        self.nbuf += 1
        return Buf(name or f"b{self.nbuf}", exclusive)

    def op(self, eng, emit, reads=(), writes=(), is_dma=False, extra_deps=()):
        o = Op(eng, emit, is_dma)
        for d in extra_deps:
            o.deps.append(d)
            d.needs_inc = True
        xr = [b for b in reads if b.exclusive]
        if xr:
            reads = [b for b in reads if not b.exclusive]
            writes = list(writes) + [b for b in xr if b not in writes]
        deps = {}
        for b in reads:
            if b.last_w is not None:
                deps[id(b.last_w)] = b.last_w
        for b in writes:
            if b.last_w is not None:
                deps[id(b.last_w)] = b.last_w
            for r in b.readers:
                deps[id(r)] = r
        for d in deps.values():
            if d.eng == "pe" and eng == "pe" and not d.is_dma and not is_dma:
                continue
            o.deps.append(d)
            d.needs_inc = True
        for b in writes:
            b.last_w = o
            b.readers = []
        for b in reads:
            if b.last_w is not o:
                if not is_dma:
                    b.readers = [r for r in b.readers if r.is_dma or r.eng != eng]
                b.readers.append(o)
        self.ops.append(o)
        return o

    def dma(self, q, out, in_, reads=(), writes=(), extra_deps=()):
        return self.op(q, lambda e: e.dma_start(out=out, in_=in_), reads, writes, is_dma=True, extra_deps=extra_deps)

    def finalize(self, final_waits=()):
        nc = self.nc
        sems = {e: nc.alloc_semaphore(f"s_{e}") for e in ("pe", "act", "dve", "pool")}
        dsems = {q: [nc.alloc_semaphore(f"d_{q}{i}") for i in range(NDMA_SEMS)]
                 for q in ("sp", "pool", "act")}
        cnt = {e: 0 for e in sems}
        dcnt = {q: 0 for q in dsems}
        per_eng = {e: [] for e in ENGS}
        for o in self.ops:
            if o.is_dma:
                i = dcnt[o.eng]
                dcnt[o.eng] += 1
                sem = dsems[o.eng][i % NDMA_SEMS]
                o.event = (sem, 16 * (i // NDMA_SEMS + 1))
                if i >= NDMA_SEMS:
                    o.waits.append((sem, 16 * (i // NDMA_SEMS)))
            elif o.needs_inc:
                cnt[o.eng] += 1
                o.event = (sems[o.eng], cnt[o.eng])
            per_eng[o.eng].append(o)
        waited = {e: {} for e in ENGS}
        for o in self.ops:
            w = waited[o.eng]
            for s, v in o.waits:
                if w.get(id(s), 0) < v:
                    w[id(s)] = v
            for d in o.deps:
                s, v = d.event
                if w.get(id(s), 0) >= v:
                    continue
                w[id(s)] = v
                o.waits.append((s, v))
        self._per_eng = per_eng
        fin = {}
        for o in final_waits:
            sm, v = o.event
            if id(sm) not in fin or fin[id(sm)][1] < v:
                fin[id(sm)] = (sm, v)
        self._fin = list(fin.values())
        self.stats = {e: len(per_eng[e]) for e in ENGS}

    def emit(self):
        nc = self.nc
        per_eng = self._per_eng
        fin = self._fin

        def replay(eng_obj, ops, final=False):
            for o in ops:
                for s, v in o.waits:
                    eng_obj.wait_ge(s, v)
                ins = o.emit(eng_obj)
                if o.event is not None:
                    ins.then_inc(o.event[0], 16 if o.is_dma else 1)
            if final:
                for s, v in fin:
                    eng_obj.wait_ge(s, v)

        with nc.Block() as block:
            @block.tensor
            def _(e):
                replay(e, per_eng["pe"])

            @block.scalar
            def _(e):
                replay(e, per_eng["act"])

            @block.vector
            def _(e):
                replay(e, per_eng["dve"])

            @block.gpsimd
            def _(e):
                replay(e, per_eng["pool"])

            @block.sync
            def _(e):
                replay(e, per_eng["sp"], final=True)


class Rot:
    def __init__(self, items):
        self.items = list(items)
        self.i = 0

    def next(self):
        it = self.items[self.i % len(self.items)]
        self.i += 1
        return it


D = 2048
KC = 16
S = 8192
B = 2
NR = 4
TL = 2048
NT = 16
TB = 512
NB = TL // TB
DH = 128
EPS = 1e-6
SCALE = DH ** -0.5
NEG = -30000.0
DFF = 8192
DPLE = 256
DEBUG = False
HA = 24
GROUPS_A = ((1, 1), (4, 4), (16, 16))


class KB:
    def __init__(self, bf16_bank=None, wslot_elems=8192):
        self.wse = wslot_elems
        self.nc = bass.Bass("TRN2", target_bir_lowering=False)
        nc = self.nc
        self.P = Prog(nc)
        self.ps = [nc.alloc_psum_tensor(f"ps{i}", [128, 1024], BF16) if i == bf16_bank else
                   nc.alloc_psum_tensor(f"ps{i}", [128, 512], F32) for i in range(8)]
        self.psb = [self.P.buf(f"ps{i}", exclusive=True) for i in range(8)]
        self.wslots = Rot([self.sb(f"wslot{i}", [128, wslot_elems], BF16) for i in range(3)])
        self.out_dmas = []
        self.wcache = {}
        self.wcache_on = False
        self.phase_stores = []
        self.phaseb = self.P.buf("phase")
        self.dq = Rot(["sp", "act"])

    def sb(self, name, shape, dt):
        return self.nc.alloc_sbuf_tensor(name, list(shape), dt), self.P.buf(name)

    def din(self, name, shape, dt=F32):
        return self.nc.dram_tensor(name, list(shape), dt, kind="ExternalInput").ap()

    def dout(self, name, shape, dt=F32):
        return self.nc.dram_tensor(name, list(shape), dt, kind="ExternalOutput").ap()

    def load_const(self, name, shape, dt=F32, cast=False):
        d = self.din(name, shape, F32)
        t, b = self.sb(name + "_sb", shape, BF16 if cast else dt)
        self.P.dma("pool" if cast else "sp", t[:], d, writes=[b])
        return t, b

    def dump(self, name, in_ap, rbuf, shape, dt=F32):
        d = self.dout(name, shape, dt)
        self.store(d, in_ap, rbuf)

    def store(self, out_ap, in_ap, rbuf):
        o = self.P.dma("sp", out_ap, in_ap, reads=[rbuf] + list(getattr(self, "store_reads", [])))
        self.out_dmas.append(o)
        self.phase_stores.append(o)
        return o

    def phase_barrier(self):
        self.P.op("dve", lambda e: e.memset(self.fence_t[0][:, :], 0.0), [], [self.phaseb, self.fence_t[1]],
                  extra_deps=list(self.phase_stores))
        self.phase_stores = []

    def load_transpose(self, src, row0, ncol_chunks, dst, dstb, ident, identb, stage, ntiles=4, dst_dt_bf16=False):
        P = self.P
        for tile in range(ntiles):
            st, stb = stage.next()
            P.dma("sp", st[:, 0:ncol_chunks * 128], src[row0 + tile * 128: row0 + (tile + 1) * 128, :], writes=[stb])
            for k0 in range(0, ncol_chunks, 4):
                nk = min(4, ncol_chunks - k0)
                pi = self.trb.next()
                ps, psb = self.ps[pi], self.psb[pi]
                for j in range(nk):
                    P.op("pe", lambda e, ps=ps, st=st, j=j, k0=k0: e.transpose(
                        ps[:, j * 128:(j + 1) * 128], st[:, (k0 + j) * 128:(k0 + j + 1) * 128], ident[:]),
                        [stb, identb], [psb])
                P.op("act", lambda e, ps=ps, nk=nk, k0=k0, tile=tile: e.copy(
                    dst[:, k0:k0 + nk, tile * 128:(tile + 1) * 128],
                    ps[:, 0:nk * 128].rearrange("p (a b) -> p a b", a=nk)), [psb], [dstb])

    def rmsnorm(self, hT, hTb, g, gb, out, outb, T=TB, nkc=KC):
        P = self.P
        pi = self.statb
        ps, psb = self.ps[pi], self.psb[pi]
        for kc in range(nkc):
            sq, sqb = self.sq.next()
            if kc % 2 == 0:
                P.op("act", lambda e, sq=sq, kc=kc: e.activation(sq[:, :T], hT[:, kc, :T], AF.Square), [hTb], [sqb])
            else:
                P.op("dve", lambda e, sq=sq, kc=kc: e.tensor_tensor(sq[:, :T], hT[:, kc, :T], hT[:, kc, :T], ALU.mult), [hTb], [sqb])
            P.op("pe", lambda e, sq=sq, kc=kc: e.matmul(ps[:, :T], self.ones[:], sq[:, :T],
                                                         start=(kc == 0), stop=(kc == nkc - 1)),
                 [sqb, self.onesb], [psb])
        rs, rsb = self.rstd
        if getattr(self, "fast_rstd", False):
            P.op("act", lambda e: e.activation(rs[:, :T], ps[:, :T], AF.Ln, bias=EPS, scale=1.0 / (nkc * 128)), [psb], [rsb])
            P.op("act", lambda e: e.activation(rs[:, :T], rs[:, :T], AF.Exp, scale=-0.5), [rsb], [rsb])
        else:
            P.op("act", lambda e: e.activation(rs[:, :T], ps[:, :T], AF.Sqrt, bias=EPS, scale=1.0 / (nkc * 128)), [psb], [rsb])
            P.op("dve", lambda e: e.reciprocal(rs[:, :T], rs[:, :T]), [rsb], [rsb])
        for kc in range(nkc):
            P.op("dve", lambda e, kc=kc: e.scalar_tensor_tensor(
                out[:, kc, :T], hT[:, kc, :T], g[:, kc:kc + 1], rs[:, :T], ALU.mult, ALU.mult),
                [hTb, gb, rsb], [outb])

    def wsrc(self, key, i, nelem, slv, src_fp32):
        P = self.P
        if not getattr(self, "wcache_on", False):
            return None
        ent = self.wcache.setdefault(key, {})
        if i not in ent:
            self.wc_n = getattr(self, "wc_n", 0) + 1
            d = self.nc.dram_tensor(f"wc{self.wc_n}", [128, nelem], BF16).ap()
            ent[i] = (d, P.buf(f"wc_{key}_{i}"))
            return ("first", ent[i])
        return ("cached", ent[i])

    def wload(self, key, i, sl, slb, nelem, slv, src):
        P = self.P
        st = self.wsrc(key, i, nelem, slv, src)
        if st is None:
            P.dma("pool", slv, src, writes=[slb])
        elif st[0] == "first":
            d, db = st[1]
            P.dma("pool", slv, src, writes=[slb])
            P.dma("pool", d, sl[:, 0:nelem], reads=[slb], writes=[db])
        else:
            d, db = st[1]
            P.dma("pool", sl[:, 0:nelem], d, reads=[db], writes=[slb])

    def fm_groups(self, K, ncols):
        kct = K // 128
        if kct <= 16:
            kcs, nkq = kct, 1
            cps = min(ncols, (self.wse // kct) // 128 * 128)
        else:
            kcs, nkq = 16, kct // 16
            cps = min(ncols, self.wse // 16)
        return kcs, nkq, [(g0, min(cps, ncols - g0), kq) for g0 in range(0, ncols, cps) for kq in range(nkq)]

    def tm_groups(self, K, ncols):
        kct = K // 128
        cps = min(ncols, self.wse // kct)
        return kct, [(g0, min(cps, ncols - g0)) for g0 in range(0, ncols, cps)]

    def precast_list(self, w, K, c0, ncols, mode):
        wv = w.rearrange("(kc p) n -> p kc n", p=128)
        out = []
        if mode == "fm":
            kcs, nkq, groups = self.fm_groups(K, ncols)
            for i, (g0, gc, kq) in enumerate(groups):
                out.append(((w.tensor.name, "fm", c0, ncols), i, kcs * gc, kcs,
                            wv[:, kq * 16: kq * 16 + kcs, c0 + g0: c0 + g0 + gc]))
        else:
            kct, groups = self.tm_groups(K, ncols)
            for i, (g0, gc) in enumerate(groups):
                out.append(((w.tensor.name, "tm", c0, ncols), i, kct * gc, kct, wv[:, :, c0 + g0: c0 + g0 + gc]))
        return out

    def precast_issue(self, item):
        key, i, nelem, a, src = item
        ent = self.wcache.setdefault(key, {})
        assert i not in ent
        self.wc_n = getattr(self, "wc_n", 0) + 1
        d = self.nc.dram_tensor(f"wc{self.wc_n}", [128, nelem], BF16).ap()
        db = self.P.buf(f"wcb{self.wc_n}")
        ent[i] = (d, db)
        self.P.dma("pool", d.rearrange("p (a b) -> p a b", a=a), src, writes=[db])

    def linear_fm(self, w, K, c0, ncols, xT, xTb, T, epilogue, banks):
        P = self.P
        wv = w.rearrange("(kc p) n -> p kc n", p=128)
        kcs, nkq, groups = self.fm_groups(K, ncols)
        brot = Rot(banks)
        loads = []

        def issue(i):
            g0, gc, kq = groups[i]
            sl, slb = self.wslots.next()
            slv = sl[:, 0:kcs * gc].rearrange("p (a b) -> p a b", a=kcs)
            self.wload((w.tensor.name, "fm", c0, ncols), i, sl, slb, kcs * gc, slv,
                       wv[:, kq * 16: kq * 16 + kcs, c0 + g0: c0 + g0 + gc])
            loads.append((slv, slb))
        for i in range(min(2, len(groups))):
            issue(i)
        gbanks = None
        for i, (g0, gc, kq) in enumerate(groups):
            if i + 2 < len(groups):
                issue(i + 2)
            slv, slb = loads[i]
            nch = gc // 128
            if nkq > 1 and kq == 0:
                gbanks = [brot.next() for _ in range(nch)]
            for ch in range(nch):
                pi = gbanks[ch] if nkq > 1 else brot.next()
                ps, psb = self.ps[pi], self.psb[pi]
                for kc in range(kcs):
                    P.op("pe", lambda e, ps=ps, slv=slv, kc=kc, ch=ch, kq=kq: e.matmul(
                        ps[:, :T], slv[:, kc, ch * 128:(ch + 1) * 128], xT[:, kq * 16 + kc, :T],
                        start=(kq == 0 and kc == 0), stop=(kq == nkq - 1 and kc == kcs - 1)),
                        [slb, xTb], [psb])
                if kq == nkq - 1:
                    epilogue(ps, psb, (g0 // 128) + ch)
        self.rope_flush()

    def linear_tm(self, w, K, c0, ncols, xT, xTb, ntiles, epilogue, banks, tiles=None):
        P = self.P
        wv = w.rearrange("(kc p) n -> p kc n", p=128)
        kct, groups = self.tm_groups(K, ncols)
        brot = Rot(banks)
        loads = []

        def issue(i):
            g0, gc = groups[i]
            sl, slb = self.wslots.next()
            slv = sl[:, 0:kct * gc].rearrange("p (a b) -> p a b", a=kct)
            self.wload((w.tensor.name, "tm", c0, ncols), i, sl, slb, kct * gc, slv, wv[:, :, c0 + g0: c0 + g0 + gc])
            loads.append((slv, slb))
        for i in range(min(2, len(groups))):
            issue(i)
        for i, (g0, gc) in enumerate(groups):
            if i + 2 < len(groups):
                issue(i + 2)
            slv, slb = loads[i]
            for tile in (tiles if tiles is not None else range(ntiles)):
                pi = brot.next()
                ps, psb = self.ps[pi], self.psb[pi]
                for kc in range(kct):
                    P.op("pe", lambda e, ps=ps, slv=slv, kc=kc, tile=tile, gc=gc: e.matmul(
                        ps[:, :gc], xT[:, kc, tile * 128:(tile + 1) * 128], slv[:, kc, :],
                        start=(kc == 0), stop=(kc == kct - 1)), [slb, xTb], [psb])
                epilogue(ps, psb, tile, g0, gc)

    def rope_head(self, ps, psb, cosT, sinT, tabbs, t0, T, dst_dram):
        P = self.P
        xb, xbb = self.xbf.next()
        P.op("act", lambda e: e.copy(xb[:, :T], ps[:, :T]), [psb], [xbb])

        def part_b():
            pi = self.ropeb
            ps2, ps2b = self.ps[pi], self.psb[pi]
            P.op("pe", lambda e: e.matmul(ps2[0:32, :T], self.pm[0:32, 0:32], xb[0:32, :T], start=True, stop=True),
                 [xbb, self.pmb], [ps2b])
            t1, t1b = self.rt1.next()
            t2, t2b = self.rt2.next()
            P.op("dve", lambda e: e.tensor_tensor(t1[0:32, :T], ps[0:32, :T], cosT[0:32, t0:t0 + T], ALU.mult), [psb] + tabbs, [t1b])
            P.op("dve", lambda e: e.tensor_tensor(t2[0:32, :T], ps2[0:32, :T], sinT[0:32, t0:t0 + T], ALU.mult), [ps2b] + tabbs, [t2b])
            P.op("dve", lambda e: e.tensor_tensor(xb[0:32, :T], t1[0:32, :T], t2[0:32, :T], ALU.add), [t1b, t2b, xbb], [xbb])
            self.store(dst_dram, xb[:, :T], xbb)
        prev = getattr(self, "_rope_pending", None)
        self._rope_pending = part_b
        if prev is not None:
            prev()

    def rope_flush(self):
        prev = getattr(self, "_rope_pending", None)
        self._rope_pending = None
        if prev is not None:
            prev()

    def common_consts(self):
        self.ident, self.identb = self.load_const("ident", [128, 128])
        self.identbf, self.identbfb = self.load_const("identbf", [128, 128], cast=True)
        self.ones, self.onesb = self.load_const("ones", [128, 128], cast=True)
        self.pm, self.pmb = self.load_const("pm", [32, 32], cast=True)

    def finish(self):
        self.P.finalize(final_waits=self.out_dmas)
        self.P.emit()
        return self.nc


def host_consts():
    pm = np.zeros((32, 32), np.float32)
    for m in range(16):
        pm[m + 16, m] = -1.0
        pm[m, m + 16] = 1.0
    return dict(ident=np.eye(128, dtype=np.float32), identbf=np.eye(128, dtype=np.float32),
                ones=np.ones((128, 128), np.float32), pm=pm)


def rope_tables(pos):
    half = 16
    inv = (500000.0 ** (-np.arange(half, dtype=np.float32) * np.float32(2.0 / 32))).astype(np.float32)
    ang = pos.astype(np.float32)[None, :] * inv[:, None]
    c = np.cos(ang).astype(np.float32)
    s = np.sin(ang).astype(np.float32)
    return np.concatenate([c, c], 0), np.concatenate([s, s], 0)


def local_positions(r):
    n = np.arange(NT)[:, None]
    l = np.arange(128)[None, :]
    return ((4 * n + r) * 128 + l).reshape(-1)


def gain_layout(g):
    return np.ascontiguousarray(g.reshape(KC, 128).T)


def build_p1(nb=NB):
    k = KB()
    nc, P = k.nc, k.P
    x = k.din("x", [TL, D])
    w_in = k.din("w_in", [D, 3 * HA * DH])
    qT_o = k.dout("qT", [HA, DH, TL], BF16)
    kT_o = k.dout("kT", [HA, DH, TL], BF16)
    v_o = k.dout("v", [TL, HA * DH], BF16)
    k.common_consts()
    g, gb = k.load_const("g_attn", [128, KC])
    cosT, tabb = k.load_const("cosT", [32, TL])
    sinT, tabb2 = k.load_const("sinT", [32, TL])
    hT, hTb = k.sb("hT", [128, KC, TB], F32)
    hnT, hnTb = k.sb("hnT", [128, KC, TB], BF16)
    stage = Rot([k.sb(f"stage{i}", [128, D], F32) for i in range(2)])
    k.sq = Rot([k.sb(f"sq{i}", [128, TB], BF16) for i in range(2)])
    k.rstd = k.sb("rstd", [128, TB], F32)
    k.xbf = Rot([k.sb(f"xbf{i}", [128, TB], BF16) for i in range(3)])
    k.rt1 = Rot([k.sb(f"rt1_{i}", [32, TB], F32) for i in range(2)])
    k.rt2 = Rot([k.sb(f"rt2_{i}", [32, TB], F32) for i in range(2)])
    vsb = Rot([k.sb(f"vsb{i}", [128, 512], BF16) for i in range(3)])
    k.trb = Rot([6, 7])
    k.statb = 5
    k.ropeb = 4
    mmb = [0, 1, 2, 3]
    tabs = [tabb, tabb2]
    for tb in range(nb):
        t0 = tb * TB
        k.load_transpose(x, t0, KC, hT, hTb, k.ident, k.identb, stage)
        k.rmsnorm(hT, hTb, g, gb, hnT, hnTb)

        def ep_qk(ps, psb, ch, t0=t0):
            head = ch
            dst = (qT_o if head < HA else kT_o)[head % HA, :, t0:t0 + TB]
            k.rope_head(ps, psb, cosT, sinT, tabs, t0, TB, dst)
        k.linear_fm(w_in, D, 0, 2 * HA * DH, hnT, hnTb, TB, ep_qk, mmb)

        def ep_v(ps, psb, tile, c0, gc, t0=t0):
            vs, vsbuf = vsb.next()
            P.op("act", lambda e: e.copy(vs[:, :gc], ps[:, :gc]), [psb], [vsbuf])
            k.store(v_o[t0 + tile * 128: t0 + (tile + 1) * 128, c0:c0 + gc], vs[:, :gc], vsbuf)
        k.linear_tm(w_in, D, 2 * HA * DH, HA * DH, hnT, hnTb, TB // 128, ep_v, mmb)
    return k.finish()


def masks_A():
    lk = np.arange(128)[:, None]
    lq = np.arange(128)[None, :]
    out = []
    for dil in (1, 4, 16):
        res = ((lq - lk) % dil) == 0
        for kind in ("first", "mid", "last"):
            if kind == "first":
                ok = res & (lq >= lk)
            elif kind == "mid":
                ok = res
            else:
                ok = res & (lq <= lk)
            out.append(np.where(ok, 0.0, NEG).astype(np.float32))
    return np.stack(out)


def rel_arrange(arrs, r, tok_axis, pad_value=0.0):
    outs = []
    for rho in range(4):
        src = arrs[(r - rho) % 4]
        shift = 1 if r >= rho else 2
        shp = list(src.shape)
        shp[tok_axis] = 17 * 128
        o = np.zeros(shp, src.dtype)
        ntok = (17 - shift) * 128
        sl_o = [slice(None)] * len(shp)
        sl_i = [slice(None)] * len(shp)
        sl_o[tok_axis] = slice(shift * 128, 17 * 128)
        sl_i[tok_axis] = slice(0, ntok)
        o[tuple(sl_o)] = src[tuple(sl_i)]
        outs.append(o)
    return np.stack(outs)


class AttnRes:
    pass


class Region:
    def __init__(self, k, big, nelems):
        self.k, self.big, self.n, self.off = k, big, nelems, 0

    def take(self, name, shape, dt=BF16):
        ne = int(np.prod(shape[1:])) * (2 if dt == F32 else 1)
        assert self.off + ne <= self.n, (name, self.off, ne, self.n)
        ap = self.big[:, self.off:self.off + ne]
        self.off += ne
        if dt == F32:
            ap = ap.bitcast(F32)
        if len(shape) == 3:
            ap = ap.rearrange("p (a b) -> p a b", a=shape[1])
        elif len(shape) == 4:
            ap = ap.rearrange("p (a b c) -> p a b c", a=shape[1], b=shape[2])
        return ap, self.k.P.buf(name)


def fence(k, bufs):
    k.P.op("dve", lambda e: e.memset(k.fence_t[0][:, :], 0.0), [], list(bufs) + [k.fence_t[1]])


def attn_setup(k, reg, ntile_slot, nq_heads):
    a = AttnRes()
    a.kslots = Rot([reg.take(f"kslot{i}", [128, 4, ntile_slot * 128]) for i in range(3)])
    a.vslots = Rot([reg.take(f"vslot{i}", [128, ntile_slot, 4, 129]) for i in range(3)])
    a.pT = Rot([k.sb(f"pT{i}", [128, 512], BF16) for i in range(3)])
    a.qs = Rot([reg.take(f"qs{i}", [128, nq_heads, 128]) for i in range(2)])
    a.region_bufs = [b for _, b in a.kslots.items + a.vslots.items + a.qs.items]
    a.rz = k.sb("rz", [128, 4], F32)
    a.otok = Rot([k.sb(f"otok{i}", [128, 4, 128], BF16) for i in range(2)])
    a.sbanks = Rot([0, 1])
    a.acc = (4, 5)
    return a


class Pipe:
    def __init__(self, depth=2, nslots=3, issue=None, nchunks=0):
        self.depth, self.nslots, self.issue_fn, self.nchunks = depth, nslots, issue, nchunks
        self.pending = []
        self.next_issue = 0
        self.open_steps = {}

    def _can_issue(self, j):
        old = j - self.nslots
        return old < 0 or self.open_steps.get(old, 0) == 0

    def _issue_ahead(self, upto):
        while self.next_issue < self.nchunks and self.next_issue <= upto and self._can_issue(self.next_issue):
            self.issue_fn(self.next_issue)
            self.next_issue += 1

    def begin_chunk(self, j):
        while self.next_issue <= j:
            if self._can_issue(self.next_issue):
                self.issue_fn(self.next_issue)
                self.next_issue += 1
            else:
                self._flush_one()
        self._issue_ahead(j + self.nslots - 1)

    def _flush_one(self):
        pv, first_hook, last_hook, chunk = self.pending.pop(0)
        if first_hook is not None:
            first_hook()
        pv()
        if last_hook is not None:
            last_hook()
        if chunk is not None:
            self.open_steps[chunk] -= 1

    def flush(self):
        while self.pending:
            self._flush_one()

    def step(self, score_fn, first_hook=None, last_hook=None, chunk=None):
        pv = score_fn()
        if chunk is not None:
            self.open_steps[chunk] = self.open_steps.get(chunk, 0) + 1
        self.pending.append((pv, first_hook, last_hook, chunk))
        while len(self.pending) > self.depth:
            self._flush_one()
        if chunk is not None and self.issue_fn is not None:
            self._issue_ahead(chunk + self.nslots - 1)


def attn_step(k, a, kslot, kslotb, s, shared_k, qs, qsb, qheads, vslot, vslotb, mask_mm, extra=None, vsel=None, defer=False):
    P = k.P
    pi = a.sbanks.next()
    ps, psb = k.ps[pi], k.psb[pi]
    last = mask_mm is None
    if shared_k:
        q0 = qheads[0]
        P.op("pe", lambda e: e.matmul(ps[:, :], kslot[:, s * 128:(s + 1) * 128],
                                      qs[:, q0:q0 + 4, :].rearrange("p a b -> p (a b)"),
                                      start=True, stop=last, skip_group_check=True), [kslotb, qsb], [psb])
    else:
        for h in range(4):
            P.op("pe", lambda e, h=h: e.matmul(ps[:, h * 128:(h + 1) * 128], kslot[:, h, s * 128:(s + 1) * 128],
                                               qs[:, qheads[h], :], start=(h == 0), stop=(last and h == 3),
                                               skip_group_check=True), [kslotb, qsb], [psb])
    if mask_mm is not None:
        lt, rh, mb = mask_mm
        P.op("pe", lambda e: e.matmul(ps[:, :].rearrange("p (a b) -> p a b", a=4), lt, rh, start=False, stop=True,
                                      skip_group_check=True), list(mb), [psb])
    pT, pTb = a.pT.next()
    P.op("act", lambda e: e.activation(pT[:, :], ps[:, :], AF.Exp, scale=SCALE), [psb], [pTb])

    def pv_part():
        for h in range(4):
            acc = k.ps[a.acc[h // 2]]
            accb = k.psb[a.acc[h // 2]]
            vap = vslot[:, s, h, :] if vsel is None else vsel(h)
            P.op("pe", lambda e, h=h, acc=acc, vap=vap: e.matmul(acc[:, (h % 2) * 129:(h % 2) * 129 + 129],
                                                                 pT[:, h * 128:(h + 1) * 128], vap,
                                                                 start=False, stop=False, skip_group_check=True),
                 [pTb, vslotb], [accb])
        if extra is not None:
            extra(pT, pTb)
    if defer:
        return pv_part
    pv_part()


def attn_zero_acc(k, a):
    for pi in a.acc:
        k.P.op("dve", lambda e, pi=pi: e.memset(k.ps[pi][:, 0:258], 0.0), [], [k.psb[pi]])


def attn_rz(k, a, tiny=None):
    P = k.P
    rz, rzb = a.rz
    for j, pi in enumerate(a.acc):
        zin = k.ps[pi][:, 0:258].rearrange("p (a b) -> p a b", a=2)[:, :, 128]
        if tiny is None:
            P.op("dve", lambda e, zin=zin, j=j: e.reciprocal(rz[:, 2 * j:2 * j + 2], zin), [k.psb[pi]], [rzb])
        else:
            P.op("dve", lambda e, zin=zin, j=j: e.tensor_scalar_max(rz[:, 2 * j:2 * j + 2], zin, tiny), [k.psb[pi]], [rzb])
            P.op("dve", lambda e, j=j: e.reciprocal(rz[:, 2 * j:2 * j + 2], rz[:, 2 * j:2 * j + 2]), [rzb], [rzb])
    return rz, rzb


def transpose_heads(k, a, otok, otokb, dst_ap_fn, dstb, bank):
    P = k.P
    ps, psb = k.ps[bank], k.psb[bank]
    psv = ps
    for h in range(4):
        P.op("pe", lambda e, h=h: e.transpose(psv[:, h * 128:(h + 1) * 128], otok[:, h, :], k.identbf[:]),
             [otokb, k.identbfb], [psb])
    P.op("act", lambda e: e.copy(dst_ap_fn(), psv[:, 0:512].rearrange("p (a b) -> p a b", a=4)), [psb], [dstb])


def mlp_ple(k, li, hT, hTb, hnT, hnTb, big, bigb, gains, w1, w2, wgate, wproj, p_in, t0, stage_rot, mmb, allreg):
    P = k.P
    g_mlp, g_mlpb, g_ple, g_pleb = gains
    h1T = big[:, 0:64 * TB].rearrange("p (a b) -> p a b", a=64)
    k.rmsnorm(hT, hTb, g_mlp, g_mlpb, hnT, hnTb)

    def ep_w1(ps, psb, ch):
        r, rb = k.relu_t.next()
        P.op("act", lambda e: e.activation(r[:, :], ps[:, :], AF.Relu), [psb], [rb])
        P.op("dve", lambda e: e.tensor_tensor(h1T[:, ch, :], r[:, :], r[:, :], ALU.mult), [rb], [bigb])
    k.linear_fm(w1, D, 0, DFF, hnT, hnTb, TB, ep_w1, mmb)

    def ep_add(ps, psb, ch):
        P.op("dve", lambda e: e.tensor_tensor(hT[:, ch, :], hT[:, ch, :], ps[:, :], ALU.add), [psb, hTb], [hTb])
    k.linear_fm(w2, DFF, 0, D, h1T, bigb, TB, ep_add, mmb)
    fence(k, allreg)
    k.rmsnorm(hT, hTb, g_ple, g_pleb, hnT, hnTb)
    gateT = big[:, 0:KC * TB].rearrange("p (a b) -> p a b", a=KC)

    def ep_gate(ps, psb, ch):
        P.op("act", lambda e: e.activation(gateT[:, ch, :], ps[:, :], AF.Sigmoid), [psb], [bigb])
    k.linear_fm(wgate, D, 0, D, hnT, hnTb, TB, ep_gate, mmb)
    pT, pTb = k.pT_ple
    k.load_transpose(p_in, t0, 2, pT, pTb, k.ident, k.identb, stage_rot)

    def ep_proj(ps, psb, ch):
        r, rb = k.relu_t.next()
        P.op("dve", lambda e: e.tensor_tensor(r[:, :], ps[:, :], gateT[:, ch, :], ALU.mult), [psb, bigb], [rb])
        P.op("dve", lambda e: e.tensor_tensor(hT[:, ch, :], hT[:, ch, :], r[:, :], ALU.add), [rb, hTb], [hTb])
    k.linear_fm(wproj, DPLE, 0, D, pT, pTb, TB, ep_proj, mmb)


def common_scratch(k):
    k.sq = Rot([k.sb(f"sq{i}", [128, TB], BF16) for i in range(4)])
    k.rstd = k.sb("rstd", [128, TB], F32)
    k.xbf = Rot([k.sb(f"xbf{i}", [128, TB], BF16) for i in range(3)])
    k.rt1 = Rot([k.sb(f"rt1_{i}", [32, TB], F32) for i in range(1)])
    k.rt2 = Rot([k.sb(f"rt2_{i}", [32, TB], F32) for i in range(1)])
    k.relu_t = Rot([k.sb(f"relu{i}", [128, TB], F32) for i in range(2)])
    k.fence_t = k.sb("fence_t", [128, 2], F32)
    k.pT_ple = k.sb("pT_ple", [128, 2, TB], BF16)
    k.trb = Rot([7])
    k.statb = 7
    k.ropeb = 7


def build_p2(nb=NB, do_attn=True, do_mlp=True, do_proj=True):
    k = KB(bf16_bank=6)
    nc, P = k.nc, k.P
    x = k.din("x", [TL, D])
    p_in = k.din("p0", [TL, DPLE])
    qT_in = k.din("qT", [HA, DH, TL], BF16)
    kT_rel = k.din("kT_rel", [4, HA, DH, 17 * 128], BF16)
    v1_rel = k.din("v1_rel", [4, 17 * 128, HA, 129], BF16)
    w_out = k.din("a_w_out", [8 * DH, D])
    w1 = k.din("mlp_w1", [D, DFF])
    w2 = k.din("mlp_w2", [DFF, D])
    wgate = k.din("ple_w_gate", [D, D])
    wproj = k.din("ple_w_proj", [DPLE, D])
    w_qg = k.din("b_w_qg", [D, 2096])
    w_kv = k.din("w_kv", [D, 3072])
    cos_d = k.din("cosT", [32, TL])
    sin_d = k.din("sinT", [32, TL])
    h_o = k.dout("h_out", [KC, 128, TL], F32)
    qTB_o = k.dout("qTB", [16, DH, TL], BF16)
    gates_o = k.dout("gates", [TL, 48], F32)
    kvT_o = k.dout("kvT", [4, 4, DH, TL], BF16)
    vtm_o = k.dout("vtm", [2, TL, 512], BF16)
    k.common_consts()
    g_mlp, g_mlpb = k.load_const("g_mlp", [128, KC])
    g_ple, g_pleb = k.load_const("g_ple", [128, KC])
    g_attn1, g_attn1b = k.load_const("g_attn1", [128, KC])
    g_kv, g_kvb = k.load_const("g_kv", [128, KC])
    maskA, maskAb = k.load_const("maskA", [128, 9, 128], cast=True)
    hT, hTb = k.sb("hT", [128, KC, TB], F32)
    hnT, hnTb = k.sb("hnT", [128, KC, TB], BF16)
    big, bigb = k.sb("big", [128, 64 * TB], BF16)
    oT, oTb = k.sb("oT", [128, 8, TB], BF16)
    cosT, cosTb = k.sb("cosT_sb", [32, TB], F32)
    sinT, sinTb = k.sb("sinT_sb", [32, TB], F32)
    gsb = Rot([k.sb(f"gsb{i}", [128, 48], F32) for i in range(2)])
    vsb = Rot([k.sb(f"vsb{i}", [128, 512], BF16) for i in range(3)])
    common_scratch(k)
    reg = Region(k, big, 64 * TB)
    a = attn_setup(k, reg, 5, HA)
    reg.off = 24576
    stage = Rot([reg.take(f"stage{i}", [128, D], F32) for i in range(2)])
    allreg = a.region_bufs + [b for _, b in stage.items] + [bigb]
    mmb = [0, 1, 2, 3]
    gains = (g_mlp, g_mlpb, g_ple, g_pleb)
    for tb in range(nb):
        t0 = tb * TB
        fence(k, allreg)
        k.load_transpose(x, t0, KC, hT, hTb, k.ident, k.identb, stage)
        if do_attn:
            for qt in range(4):
                n = tb * 4 + qt
                qs, qsb = a.qs.next()
                P.dma("sp", qs[:, :, :], qT_in[:, :, n * 128:(n + 1) * 128].rearrange("h d t -> d h t"), writes=[qsb])
                for quad in range(2):
                    attn_zero_acc(k, a)
                    for g, (dil, nk) in enumerate(GROUPS_A):
                        for rho in range(4):
                            offs = [j for j in range(nk + 1) if j % 4 == rho and n - j // 4 >= 0]
                            if not offs:
                                continue
                            a_lo, a_hi = offs[0] // 4, offs[-1] // 4
                            m_lo, m_hi = n - a_hi + 1, n - a_lo + 1
                            cnt = m_hi - m_lo + 1
                            h0 = g * 8 + quad * 4
                            ks, ksb = a.kslots.next()
                            vs, vsb_ = a.vslots.next()
                            P.dma("sp", ks[:, :, 0:cnt * 128],
                                  kT_rel[rho, h0:h0 + 4, :, m_lo * 128:(m_hi + 1) * 128].rearrange("h d t -> d h t"),
                                  writes=[ksb])
                            P.dma("act", vs[:, 0:cnt, :, :],
                                  v1_rel[rho, m_lo * 128:(m_hi + 1) * 128, h0:h0 + 4, :].rearrange("(m l) h c -> l m h c", l=128),
                                  writes=[vsb_])
                            for j in offs:
                                aa = j // 4
                                s = (n - aa + 1) - m_lo
                                kind = 0 if j == 0 else (2 if j == nk else 1)
                                mt = maskA[:, g * 3 + kind, :].unsqueeze(1).to_broadcast([128, 4, 128])
                                attn_step(k, a, ks, ksb, s, False, qs, qsb, [h0 + h for h in range(4)], vs, vsb_,
                                          (k.identbf[:], mt, [k.identbfb, maskAb]))
                    rz, rzb = attn_rz(k, a)
                    if DEBUG and n == 0 and quad == 0:
                        dbg, dbgb = k.sb("dbg", [128, 258], F32)
                        P.op("dve", lambda e: e.tensor_copy(dbg[:, :], k.ps[4][:, 0:258]), [k.psb[4]], [dbgb])
                        k.dump("dbg_acc", dbg[:, :], dbgb, [128, 258])
                        k.dump("dbg_rz", rz[:, :], rzb, [128, 4])
                    ot, otb = a.otok.next()
                    for h in range(4):
                        acc = k.ps[a.acc[h // 2]]
                        P.op("dve", lambda e, h=h, acc=acc, ot=ot, rz=rz: e.tensor_scalar(
                            ot[:, h, :], acc[:, (h % 2) * 129:(h % 2) * 129 + 128], rz[:, h:h + 1], None, ALU.mult),
                            [k.psb[a.acc[h // 2]], rzb], [otb])
                    if DEBUG and n == 0 and quad == 0:
                        k.dump("dbg_ot", ot[:, :, :], otb, [128, 4, 128], BF16)
                    transpose_heads(k, a, ot, otb,
                                    lambda quad=quad, qt=qt: oT[:, quad * 4:quad * 4 + 4, qt * 128:(qt + 1) * 128], oTb, 6)

            def ep_add(ps, psb, ch):
                P.op("dve", lambda e: e.tensor_tensor(hT[:, ch, :], hT[:, ch, :], ps[:, :], ALU.add), [psb, hTb], [hTb])
            k.linear_fm(w_out, 8 * DH, 0, D, oT, oTb, TB, ep_add, mmb)
        if do_mlp:
            fence(k, allreg)
            mlp_ple(k, 0, hT, hTb, hnT, hnTb, big, bigb, gains, w1, w2, wgate, wproj, p_in, t0, stage, mmb, allreg)
        k.store(h_o[:, :, t0:t0 + TB].rearrange("k p t -> p k t"), hT[:, :, :], hTb)
        if do_proj:
            P.dma("sp", cosT[:, :], cos_d[:, t0:t0 + TB], writes=[cosTb])
            P.dma("sp", sinT[:, :], sin_d[:, t0:t0 + TB], writes=[sinTb])
            tabs = [cosTb, sinTb]
            k.rmsnorm(hT, hTb, g_attn1, g_attn1b, hnT, hnTb)

            def ep_q(ps, psb, ch, t0=t0):
                k.rope_head(ps, psb, cosT, sinT, tabs, 0, TB, qTB_o[ch, :, t0:t0 + TB])
            k.linear_fm(w_qg, D, 0, 16 * DH, hnT, hnTb, TB, ep_q, mmb)

            def ep_g(ps, psb, tile, c0, gc, t0=t0):
                gs, gsbuf = gsb.next()
                P.op("act", lambda e: e.activation(gs[:, :gc], ps[:, :gc], AF.Sigmoid), [psb], [gsbuf])
                k.store(gates_o[t0 + tile * 128:t0 + (tile + 1) * 128, :], gs[:, :gc], gsbuf)
            k.linear_tm(w_qg, D, 16 * DH, 48, hnT, hnTb, TB // 128, ep_g, mmb)
            k.rmsnorm(hT, hTb, g_kv, g_kvb, hnT, hnTb)

            def mk_ep_fm(slot, rope, t0=t0):
                def ep(ps, psb, ch):
                    dst = kvT_o[slot, ch, :, t0:t0 + TB]
                    if rope:
                        k.rope_head(ps, psb, cosT, sinT, tabs, 0, TB, dst)
                    else:
                        xb, xbb = k.xbf.next()
                        P.op("act", lambda e: e.copy(xb[:, :], ps[:, :]), [psb], [xbb])
                        k.store(dst, xb[:, :], xbb)
                return ep

            def mk_ep_tm(slot, t0=t0):
                def ep(ps, psb, tile, c0, gc):
                    vs, vsbuf = vsb.next()
                    P.op("act", lambda e: e.copy(vs[:, :gc], ps[:, :gc]), [psb], [vsbuf])
                    k.store(vtm_o[slot, t0 + tile * 128:t0 + (tile + 1) * 128, c0:c0 + gc], vs[:, :gc], vsbuf)
                return ep
            k.linear_fm(w_kv, D, 0, 512, hnT, hnTb, TB, mk_ep_fm(0, False), mmb)
            k.linear_fm(w_kv, D, 512, 512, hnT, hnTb, TB, mk_ep_fm(1, False), mmb)
            k.linear_fm(w_kv, D, 1024, 512, hnT, hnTb, TB, mk_ep_fm(2, True), mmb)
            k.linear_tm(w_kv, D, 1536, 512, hnT, hnTb, TB // 128, mk_ep_tm(0), mmb)
            k.linear_fm(w_kv, D, 2048, 512, hnT, hnTb, TB, mk_ep_fm(3, True), mmb)
            k.linear_tm(w_kv, D, 2560, 512, hnT, hnTb, TB // 128, mk_ep_tm(1), mmb)
    return k.finish()


def nsa_host_consts(r):
    n_cmp, n_slc = 511, 128
    j_of_jj = np.full(128, -1, np.int64)
    for rho in range(4):
        for m in range(1, 17):
            for half in range(2):
                j = 8 * (m - 1) + 2 * (r - rho) + half
                if 0 <= j < n_slc:
                    j_of_jj[32 * rho + 2 * (m - 1) + half] = j
    c = np.arange(512)
    cmp_start = c * 16
    ov = np.zeros((512, 128), np.float32)
    for jj in range(128):
        j = j_of_jj[jj]
        if j < 0:
            continue
        o = np.clip(np.minimum(cmp_start + 32, j * 64 + 64) - np.maximum(cmp_start, j * 64), 0, None) / 32.0
        ov[:, jj] = o
    ov[511] = 0.0
    ov_perm = ov.reshape(4, 128, 128).transpose(1, 0, 2)
    scoreM = np.zeros((NT, 128, 128), np.float32)
    scoreA = np.zeros((NT, 128, 128), np.float32)
    cmpmask = np.zeros((NT, 128, 4, 128), np.float32)
    l = np.arange(128)
    for n in range(NT):
        t = (4 * n + r) * 128 + l
        cur = t // 64
        for jj in range(128):
            j = j_of_jj[jj]
            if j < 0:
                scoreA[n, :, jj] = -1e30
                continue
            forced = (j == 0) | (j == cur) | (j == cur - 1)
            causal = (j * 64 <= t)
            scoreM[n, :, jj] = np.where(forced | ~causal, 0.0, 1.0)
            scoreA[n, :, jj] = np.where(forced, 1e9, np.where(causal, 0.0, -1e30))
        cend = (np.arange(512) * 16 + 31).reshape(4, 128)
        valid = cend.T[:, :, None] <= t[None, None, :]
        valid[127, 3, :] = False
        cmpmask[n] = np.where(valid, 0.0, NEG)
    lk = np.arange(128)[:, None]
    lq = np.arange(128)[None, :]
    maskB = np.stack([np.where(lq >= lk, 0.0, NEG), np.where(lq < lk, 0.0, NEG)]).astype(np.float32)
    estd = (np.arange(8192)[None, :] // 64 == np.arange(128)[:, None]).astype(np.float32)
    cpos = np.arange(512) * 16 + 31
    cosC, sinC = rope_tables(cpos)
    bf = ml_dtypes.bfloat16
    return dict(ov_perm=np.ascontiguousarray(ov_perm).astype(bf), scoreM=scoreM, scoreA=scoreA,
                cmpmask=cmpmask.astype(bf), maskB=np.ascontiguousarray(maskB.transpose(1, 0, 2)),
                estd=estd.astype(bf), cosC=cosC, sinC=sinC)


def gelu_tanh(k, out_ap, outb, ps, psb, bias_ap, biasb, n):
    P = k.P
    x, xb = k.g_x
    u, ub = k.g_u
    P.op("act", lambda e: e.activation(x[:, :n], ps[:, :n], AF.Identity, bias=bias_ap), [psb, biasb], [xb])
    P.op("act", lambda e: e.activation(u[:, :n], x[:, :n], AF.Square), [xb], [ub])
    P.op("dve", lambda e: e.tensor_scalar(u[:, :n], u[:, :n], 0.044715, 1.0, ALU.mult, ALU.add), [ub], [ub])
    P.op("dve", lambda e: e.tensor_tensor(u[:, :n], u[:, :n], x[:, :n], ALU.mult), [ub, xb], [ub])
    P.op("act", lambda e: e.activation(u[:, :n], u[:, :n], AF.Sigmoid, scale=1.5957691216057308), [ub], [ub])
    P.op("dve", lambda e: e.tensor_tensor(out_ap, u[:, :n], x[:, :n], ALU.mult), [ub, xb], [outb])


def build_p3(nb=NB, do_mlp=True, do_attn=True):
    k = KB(bf16_bank=6, wslot_elems=4096)
    nc, P = k.nc, k.P
    h_in = k.din("h_in", [KC, 128, TL])
    p_in = k.din("p1", [TL, DPLE])
    qT_in = k.din("qTB", [16, DH, TL], BF16)
    gates_in = k.din("gates", [TL, 48])
    rawT = k.din("rawT", [2, 4, DH, S], BF16)
    kslc_rel = k.din("kslc_rel", [4, 4, DH, 17 * 128], BF16)
    vslc_rel = k.din("vslc_rel", [4, 17 * 128, 4, 129], BF16)
    kwin_rel = k.din("kwin_rel", [4, 4, DH, 17 * 128], BF16)
    vwin_rel = k.din("vwin_rel", [4, 17 * 128, 4, 129], BF16)
    cw1 = [k.din("cmp_w1_k", [32 * DH, 256]), k.din("cmp_w1_v", [32 * DH, 256])]
    cw2 = [k.din("cmp_w2_k", [256, DH]), k.din("cmp_w2_v", [256, DH])]
    cpe = [k.din("cmp_pe_k", [32, DH]), k.din("cmp_pe_v", [32, DH])]
    w_out = k.din("b_w_out", [D, D])
    w1 = k.din("mlp_w1", [D, DFF])
    w2 = k.din("mlp_w2", [DFF, D])
    wgate = k.din("ple_w_gate", [D, D])
    wproj = k.din("ple_w_proj", [DPLE, D])
    scoreM_d = k.din("scoreM", [NT, 128, 128])
    scoreA_d = k.din("scoreA", [NT, 128, 128])
    cmpmask_d = k.din("cmpmask", [NT, 128, 4, 128], BF16)
    estd_d = k.din("estd", [128, 8192], BF16)
    ovp_d = k.din("ov_perm", [128, 4, 128], BF16)
    out_o = k.dout("out", [TL, D], F32)
    k.common_consts()
    g_mlp, g_mlpb = k.load_const("g_mlp", [128, KC])
    g_ple, g_pleb = k.load_const("g_ple", [128, KC])
    g_fin, g_finb = k.load_const("g_fin", [128, KC])
    maskB, maskBb = k.load_const("maskB", [128, 2, 128], cast=True)
    cosC, cosCb = k.load_const("cosC", [32, 512])
    sinC, sinCb = k.load_const("sinC", [32, 512])
    estd, estdb = k.sb("estd_sb", [128, 8192], BF16)
    P.dma("sp", estd[:, :], estd_d, writes=[estdb])
    ovp, ovpb = k.sb("ovp_sb", [128, 4, 128], BF16)
    P.dma("sp", ovp[:, :, :], ovp_d, writes=[ovpb])
    hT, hTb = k.sb("hT", [128, KC, TB], F32)
    hnT, hnTb = k.sb("hnT", [128, KC, TB], BF16)
    oT, oTb = hnT, hnTb
    big, bigb = k.sb("big", [128, 64 * TB], BF16)
    kcmpT, kcmpTb = k.sb("kcmpT", [128, 4, 512], BF16)
    vcmp1, vcmp1b = k.sb("vcmp1", [128, 4, 4, 129], BF16)
    common_scratch(k)
    k.g_x = k.sb("g_x", [128, 512], F32)
    k.g_u = k.sb("g_u", [128, 512], F32)
    reg = Region(k, big, 64 * TB)
    a = AttnRes()
    a.kslots = Rot([reg.take(f"kslot{i}", [128, 17 * 128]) for i in range(3)])
    a.vslots = Rot([reg.take(f"vslot{i}", [128, 17, 129]) for i in range(3)])
    a.qs = Rot([reg.take(f"qs{i}", [128, 16, 128]) for i in range(2)])
    a.pT = Rot([k.sb(f"pT{i}", [128, 512], BF16) for i in range(3)])
    a.rz = k.sb("rz", [128, 4], F32)
    a.otok = Rot([k.sb(f"otok{i}", [128, 4, 128], BF16) for i in range(2)])
    a.sbanks = Rot([0, 1])
    a.acc = (4, 5)
    a.region_bufs = [b for _, b in a.kslots.items + a.vslots.items + a.qs.items]
    assert reg.off <= 24576, reg.off
    reg.off = 24576
    stage = Rot([reg.take(f"stage{i}", [128, D], F32) for i in range(2)])
    reg2 = Region(k, big, 64 * TB)
    w1sb, w1sbb = reg2.take("w1sb", [128, 32, 256])
    rawsb = Rot([reg2.take(f"rawsb{i}", [128, S]) for i in range(2)])
    hidT, hidTb = k.sb("hidT", [128, 2, 512], BF16)
    w2sb, w2sbb = k.sb("w2sb", [128, 2, 128], BF16)
    peT, peTb = k.sb("peT", [128, 32], BF16)
    biasT, biasTb = k.sb("biasT", [128, 2], F32)
    allreg = a.region_bufs + [b for _, b in stage.items] + [bigb, w1sbb] + [b for _, b in rawsb.items]
    mmb = [0, 1, 2, 3]
    gains = (g_mlp, g_mlpb, g_ple, g_pleb)
    gsc, gscb = k.sb("gates_sb", [128, 48], F32)
    sM, sMb = k.sb("sM", [128, 128], F32)
    sA, sAb = k.sb("sA", [128, 128], F32)
    cmk, cmkb = k.sb("cmk", [128, 4, 128], BF16)
    imp, impb = k.sb("imp", [128, 128], F32)
    scr, scrb = k.sb("scr", [128, 128], F32)
    m8, m8b = k.sb("m8", [128, 16], F32)
    selb, selbb = k.sb("selb", [128, 128], BF16)
    selT, selTb = k.sb("selT", [128, 128], BF16)
    otot, ototb = k.sb("otot", [128, 4, 128], F32)
    sc4, sc4b = k.sb("sc4", [128, 4], F32)
    tabsC = [cosCb, sinCb]

    P.op("dve", lambda e: e.memset(vcmp1[:, :, :, 128:129], 1.0), [], [vcmp1b])
    P.op("dve", lambda e: e.memset(hidT[:, :, :], 0.0), [], [hidTb])
    P.op("dve", lambda e: e.memset(kcmpT[:, :, :], 0.0), [], [kcmpTb])
    for kv in range(2):
        fence(k, [w1sbb, w2sbb, peTb])
        P.dma("pool", w1sb[:, :, :], cw1[kv].rearrange("(l d) h -> d l h", d=128), writes=[w1sbb])
        P.dma("pool", w2sb[:, :, :], cw2[kv].rearrange("(c p) d -> p c d", p=128), writes=[w2sbb])
        P.op("pool", lambda e, kv=kv: e.dma_start(out=peT[:, :], in_=cpe[kv].rearrange("l d -> d l"),
                                                  allow_slow_non_contiguous=True), [], [peTb], is_dma=True)
        ps7, ps7b = k.ps[7], k.psb[7]
        for hc in range(2):
            for l in range(32):
                P.op("pe", lambda e, hc=hc, l=l: e.matmul(ps7[:, hc:hc + 1], w1sb[:, l, hc * 128:(hc + 1) * 128],
                                                          peT[:, l:l + 1], start=(l == 0), stop=(l == 31),
                                                          skip_group_check=True), [w1sbb, peTb], [ps7b])
        P.op("dve", lambda e: e.tensor_copy(biasT[:, :], ps7[:, 0:2]), [ps7b], [biasTb])
        for g in range(4):
            rs, rsb = rawsb.next()
            P.dma("sp", rs[:, :], rawT[kv, g, :, :], writes=[rsb])
            for hc in range(2):
                pi = mmb[hc]
                ps, psb = k.ps[pi], k.psb[pi]
                for l in range(32):
                    P.op("pe", lambda e, ps=ps, rs=rs, hc=hc, l=l: e.matmul(
                        ps[:, 0:511], w1sb[:, l, hc * 128:(hc + 1) * 128], rs[:, l:l + 16 * 510 + 1:16],
                        start=(l == 0), stop=(l == 31)), [w1sbb, rsb], [psb])
                gelu_tanh(k, hidT[:, hc, 0:511], hidTb, ps, psb, biasT[:, hc:hc + 1], biasTb, 511)
            if kv == 0:
                ps, psb = k.ps[2], k.psb[2]
                for hc in range(2):
                    P.op("pe", lambda e, ps=ps, hc=hc: e.matmul(ps[:, 0:512], w2sb[:, hc, :], hidT[:, hc, :],
                                                                start=(hc == 0), stop=(hc == 1)), [w2sbb, hidTb], [psb])
                xb, xbb = k.xbf.next()
                P.op("act", lambda e, ps=ps, xb=xb: e.copy(xb[:, :], ps[:, :]), [psb], [xbb])
                ps2, ps2b = k.ps[7], k.psb[7]
                P.op("pe", lambda e, xb=xb: e.matmul(ps2[0:32, :], k.pm[0:32, 0:32], xb[0:32, :], start=True, stop=True),
                     [xbb, k.pmb], [ps2b])
                t1, t1b = k.rt1.next()
                t2, t2b = k.rt2.next()
                P.op("dve", lambda e, ps=ps, t1=t1: e.tensor_tensor(t1[0:32, :], ps[0:32, :], cosC[0:32, :], ALU.mult), [psb] + tabsC, [t1b])
                P.op("dve", lambda e, t2=t2: e.tensor_tensor(t2[0:32, :], ps2[0:32, :], sinC[0:32, :], ALU.mult), [ps2b] + tabsC, [t2b])
                P.op("dve", lambda e, xb=xb, t1=t1, t2=t2: e.tensor_tensor(xb[0:32, :], t1[0:32, :], t2[0:32, :], ALU.add), [t1b, t2b, xbb], [xbb])
                P.op("act", lambda e, xb=xb, g=g: e.copy(kcmpT[:, g, 0:511], xb[:, 0:511]), [xbb], [kcmpTb])
            else:
                for ct in range(4):
                    ps, psb = k.ps[2 + ct % 2], k.psb[2 + ct % 2]
                    for hc in range(2):
                        P.op("pe", lambda e, ps=ps, hc=hc, ct=ct: e.matmul(ps[:, 0:128], hidT[:, hc, ct * 128:(ct + 1) * 128],
                                                                           w2sb[:, hc, :], start=(hc == 0), stop=(hc == 1)),
                             [w2sbb, hidTb], [psb])
                    P.op("act", lambda e, ps=ps, ct=ct, g=g: e.copy(vcmp1[:, ct, g, 0:128], ps[:, 0:128]), [psb], [vcmp1b])

    for tb in range(nb):
        t0 = tb * TB
        fence(k, allreg)
        P.dma("sp", hT[:, :, :], h_in[:, :, t0:t0 + TB].rearrange("k p t -> p k t"), writes=[hTb])
        if do_attn:
            for qt in range(4):
                n = tb * 4 + qt
                qs, qsb = a.qs.next()
                P.dma("sp", qs[:, :, :], qT_in[:, :, n * 128:(n + 1) * 128].rearrange("h d t -> d h t"), writes=[qsb])
                P.dma("sp", gsc[:, :], gates_in[n * 128:(n + 1) * 128, :], writes=[gscb])
                P.dma("sp", sM[:, :], scoreM_d[n], writes=[sMb])
                P.dma("sp", sA[:, :], scoreA_d[n], writes=[sAb])
                P.dma("sp", cmk[:, :, :], cmpmask_d[n], writes=[cmkb])
                for g in range(4):
                    nsa_group(k, a, n, g, qs, qsb, gsc, gscb, sM, sMb, sA, sAb, cmk, cmkb, kcmpT, kcmpTb, vcmp1, vcmp1b,
                              ovp, ovpb, estd, estdb, maskB, maskBb, imp, impb, scr, scrb, m8, m8b, selb, selbb, selT, selTb,
                              otot, ototb, sc4, sc4b, kslc_rel, vslc_rel, kwin_rel, vwin_rel, oT, oTb, qt)

            def ep_add(ps, psb, ch):
                P.op("dve", lambda e: e.tensor_tensor(hT[:, ch, :], hT[:, ch, :], ps[:, :], ALU.add), [psb, hTb], [hTb])
            k.linear_fm(w_out, D, 0, D, oT, oTb, TB, ep_add, mmb)
        if do_mlp:
            fence(k, allreg)
            mlp_ple(k, 1, hT, hTb, hnT, hnTb, big, bigb, gains, w1, w2, wgate, wproj, p_in, t0, stage, mmb, allreg)
        fence(k, allreg)
        yT = big[:, 0:2 * KC * TB].bitcast(F32).rearrange("p (a b) -> p a b", a=KC)
        final_norm_store(k, hT, hTb, g_fin, g_finb, yT, bigb, stage, out_o, t0)
    return k.finish()


def final_norm_store(k, hT, hTb, g, gb, yT, yTb, stage, out_o, t0):
    P = k.P
    T = TB
    pi = k.statb
    ps, psb = k.ps[pi], k.psb[pi]
    for kc in range(KC):
        sq, sqb = k.sq.next()
        P.op("act", lambda e, sq=sq, kc=kc: e.activation(sq[:, :T], hT[:, kc, :T], AF.Square), [hTb], [sqb])
        P.op("pe", lambda e, sq=sq, kc=kc: e.matmul(ps[:, :T], k.ones[:], sq[:, :T], start=(kc == 0), stop=(kc == KC - 1)),
             [sqb, k.onesb], [psb])
    rs, rsb = k.rstd
    P.op("act", lambda e: e.activation(rs[:, :T], ps[:, :T], AF.Sqrt, bias=EPS, scale=1.0 / D), [psb], [rsb])
    P.op("dve", lambda e: e.reciprocal(rs[:, :T], rs[:, :T]), [rsb], [rsb])
    for kc in range(KC):
        P.op("dve", lambda e, kc=kc: e.scalar_tensor_tensor(yT[:, kc, :T], hT[:, kc, :T], g[:, kc:kc + 1], rs[:, :T],
                                                            ALU.mult, ALU.mult), [hTb, gb, rsb], [yTb])
    for tile in range(4):
        st, stb = stage.next()
        for k0 in range(0, KC, 4):
            pj = [2, 3][(k0 // 4) % 2]
            pt, ptb = k.ps[pj], k.psb[pj]
            for j in range(4):
                P.op("pe", lambda e, pt=pt, j=j, k0=k0, tile=tile: e.transpose(
                    pt[:, j * 128:(j + 1) * 128], yT[:, k0 + j, tile * 128:(tile + 1) * 128], k.ident[:]),
                    [yTb, k.identb], [ptb])
            P.op("act", lambda e, pt=pt, st=st, k0=k0: e.copy(st[:, k0 * 128:(k0 + 4) * 128], pt[:, :]), [ptb], [stb])
        k.store(out_o[t0 + tile * 128:t0 + (tile + 1) * 128, :], st[:, :], stb)


def nsa_group(k, a, n, g, qs, qsb, gsc, gscb, sM, sMb, sA, sAb, cmk, cmkb, kcmpT, kcmpTb, vcmp1, vcmp1b,
              ovp, ovpb, estd, estdb, maskB, maskBb, imp, impb, scr, scrb, m8, m8b, selb, selbb, selT, selTb,
              otot, ototb, sc4, sc4b, kslc_rel, vslc_rel, kwin_rel, vwin_rel, oT, oTb, qt):
    P = k.P
    qheads = [4 * g + h for h in range(4)]
    ip, ipb = k.ps[2], k.psb[2]

    def gate_scale(branch, tiny):
        rz, rzb = attn_rz(k, a, tiny=tiny)
        c0 = branch * 16 + 4 * g
        P.op("dve", lambda e: e.tensor_tensor(sc4[:, :], rz[:, :], gsc[:, c0:c0 + 4], ALU.mult), [rzb, gscb], [sc4b])
        return rz, rzb

    def accum_otot(first):
        for h in range(4):
            acc = k.ps[a.acc[h // 2]]
            accb = k.psb[a.acc[h // 2]]
            src = acc[:, (h % 2) * 129:(h % 2) * 129 + 128]
            if first:
                P.op("dve", lambda e, h=h, src=src: e.tensor_scalar(otot[:, h, :], src, sc4[:, h:h + 1], None, ALU.mult),
                     [accb, sc4b], [ototb])
            else:
                P.op("dve", lambda e, h=h, src=src: e.scalar_tensor_tensor(otot[:, h, :], src, sc4[:, h:h + 1], otot[:, h, :],
                                                                           ALU.mult, ALU.add), [accb, sc4b, ototb], [ototb])

    attn_zero_acc(k, a)
    P.op("dve", lambda e: e.memset(ip[:, :], 0.0), [], [ipb])
    nct = n // 4 + 1
    for ct in range(nct):
        def extra(pT, pTb, ct=ct):
            for h in range(4):
                P.op("pe", lambda e, h=h: e.matmul(ip[:, h * 128:(h + 1) * 128], pT[:, h * 128:(h + 1) * 128], ovp[:, ct, :],
                                                   start=False, stop=False, skip_group_check=True), [pTb, ovpb], [ipb])
        mt = cmk[:, ct, :].unsqueeze(1).to_broadcast([128, 4, 128])
        attn_step(k, a, kcmpT[:, g, :], kcmpTb, ct, True, qs, qsb, qheads, None, vcmp1b,
                  (k.identbf[:], mt, [k.identbfb, cmkb]), extra=extra, vsel=lambda h, ct=ct: vcmp1[:, ct, g, :])
    rz, rzb = gate_scale(0, 1e-20)
    P.op("dve", lambda e: e.tensor_scalar(imp[:, :], ip[:, 0:128], rz[:, 0:1], None, ALU.mult), [ipb, rzb], [impb])
    for h in range(1, 4):
        P.op("dve", lambda e, h=h: e.scalar_tensor_tensor(imp[:, :], ip[:, h * 128:(h + 1) * 128], rz[:, h:h + 1], imp[:, :],
                                                          ALU.mult, ALU.add), [ipb, rzb, impb], [impb])
    accum_otot(True)
    P.op("dve", lambda e: e.tensor_tensor(imp[:, :], imp[:, :], sM[:, :], ALU.mult), [impb, sMb], [impb])
    P.op("dve", lambda e: e.tensor_tensor(imp[:, :], imp[:, :], sA[:, :], ALU.add), [impb, sAb], [impb])
    P.op("dve", lambda e: e.max(m8[:, 0:8], imp[:, :]), [impb], [m8b])
    P.op("dve", lambda e: e.match_replace(scr[:, :], m8[:, 0:8], imp[:, :], -3.0e38), [impb, m8b], [scrb])
    P.op("dve", lambda e: e.max(m8[:, 8:16], scr[:, :]), [scrb], [m8b])
    P.op("dve", lambda e: e.tensor_scalar(scr[:, :], imp[:, :], m8[:, 15:16], 1.0, ALU.is_ge, ALU.subtract), [impb, m8b], [scrb])
    P.op("dve", lambda e: e.tensor_scalar(selb[:, :], scr[:, :], -NEG, None, ALU.mult), [scrb], [selbb])
    p6, p6b = k.ps[6], k.psb[6]
    P.op("pe", lambda e: e.transpose(p6[:, 0:128], selb[:, :], k.identbf[:]), [selbb, k.identbfb], [p6b])
    P.op("act", lambda e: e.copy(selT[:, :], p6[:, 0:128]), [p6b], [selTb])
    attn_zero_acc(k, a)
    for rho in range(4):
        cnt = n + 1
        ks, ksb = a.kslots.next()
        vs, vsb_ = a.vslots.next()
        P.dma("sp", ks[:, 0:cnt * 128], kslc_rel[rho, g, :, 128:(cnt + 1) * 128], writes=[ksb])
        P.dma("act", vs[:, 0:cnt, :], vslc_rel[rho, 128:(cnt + 1) * 128, g, :].rearrange("(m l) c -> l m c", l=128),
              writes=[vsb_])
        for m in range(1, n + 2):
            s = m - 1
            if rho == 0 and m == n + 1:
                mm = (k.identbf[:], maskB[:, 0, :].unsqueeze(1).to_broadcast([128, 4, 128]), [k.identbfb, maskBb])
            else:
                e0 = 128 * (16 * rho + m - 1)
                mm = (estd[:, e0:e0 + 128], selT[:, :].unsqueeze(1).to_broadcast([128, 4, 128]), [estdb, selTb])
            attn_step(k, a, ks, ksb, s, True, qs, qsb, qheads, None, vsb_, mm, vsel=lambda h, vs=vs, s=s: vs[:, s, :])
    gate_scale(1, None)
    accum_otot(False)
    attn_zero_acc(k, a)
    for rho in range(4):
        ms = [n + 1] if rho else ([n, n + 1] if n >= 1 else [n + 1])
        m_lo, cnt = ms[0], len(ms)
        ks, ksb = a.kslots.next()
        vs, vsb_ = a.vslots.next()
        P.dma("sp", ks[:, 0:cnt * 128], kwin_rel[rho, g, :, m_lo * 128:(m_lo + cnt) * 128], writes=[ksb])
        P.dma("act", vs[:, 0:cnt, :], vwin_rel[rho, m_lo * 128:(m_lo + cnt) * 128, g, :].rearrange("(m l) c -> l m c", l=128),
              writes=[vsb_])
        for m in ms:
            s = m - m_lo
            if rho == 0 and m == n + 1:
                mm = (k.identbf[:], maskB[:, 0, :].unsqueeze(1).to_broadcast([128, 4, 128]), [k.identbfb, maskBb])
            elif rho == 0:
                mm = (k.identbf[:], maskB[:, 1, :].unsqueeze(1).to_broadcast([128, 4, 128]), [k.identbfb, maskBb])
            else:
                mm = None
            attn_step(k, a, ks, ksb, s, True, qs, qsb, qheads, None, vsb_, mm, vsel=lambda h, vs=vs, s=s: vs[:, s, :])
    gate_scale(2, None)
    accum_otot(False)
    ot, otb = a.otok.next()
    P.op("act", lambda e: e.copy(ot[:, :, :], otot[:, :, :]), [ototb], [otb])
    transpose_heads(k, a, ot, otb, lambda: oT[:, 4 * g:4 * g + 4, qt * 128:(qt + 1) * 128], oTb, 6)


_BF = ml_dtypes.bfloat16
_CACHE = {}


def _prog(name, fn):
    if name not in _CACHE:
        _CACHE[name] = fn()
    return _CACHE[name]


def p1_maps(I):
    hc = host_consts()
    maps = []
    for c in range(8):
        b, r = c // 4, c % 4
        pos = local_positions(r)
        cosT, sinT = rope_tables(pos)
        maps.append(dict(x=np.ascontiguousarray(I["x"][b][pos]), w_in=I["a_w_in"][0],
                         g_attn=gain_layout(I["attn_norm_g"][0]), cosT=cosT, sinT=sinT, **hc))
    return maps


def with_ones(v, nh):
    v = np.asarray(v).reshape(TL, nh, DH)
    return np.concatenate([v, np.ones((TL, nh, 1), _BF)], -1)


def p2_maps(I, r1):
    hc = host_consts()
    mA = np.ascontiguousarray(masks_A().transpose(1, 0, 2))
    maps = []
    for c in range(8):
        b, r = c // 4, c % 4
        pos = local_positions(r)
        cosT, sinT = rope_tables(pos)
        ks = [np.asarray(r1[4 * b + rr]["kT"]) for rr in range(4)]
        vs = [with_ones(r1[4 * b + rr]["v"], HA) for rr in range(4)]
        maps.append(dict(
            x=np.ascontiguousarray(I["x"][b][pos]), p0=np.ascontiguousarray(I["p"][0][b][pos]),
            qT=np.asarray(r1[c]["qT"]), kT_rel=rel_arrange(ks, r, 2), v1_rel=rel_arrange(vs, r, 0),
            a_w_out=I["a_w_out"][0], mlp_w1=I["mlp_w1"][0], mlp_w2=I["mlp_w2"][0], ple_w_gate=I["ple_w_gate"][0],
            ple_w_proj=I["ple_w_proj"][0], b_w_qg=I["b_w_qg"][0], w_kv=I["w_kv_shared"], cosT=cosT, sinT=sinT,
            g_mlp=gain_layout(I["mlp_norm_g"][0]), g_ple=gain_layout(I["ple_norm_g"][0]),
            g_attn1=gain_layout(I["attn_norm_g"][1]), g_kv=gain_layout(I["kv_norm_g"]), maskA=mA, **hc))
    return maps


def p3_maps(I, r2):
    hc = host_consts()
    maps = []
    for c in range(8):
        b, r = c // 4, c % 4
        pos = local_positions(r)
        kvT = [np.asarray(r2[4 * b + rr]["kvT"]) for rr in range(4)]
        vtm = [np.asarray(r2[4 * b + rr]["vtm"]) for rr in range(4)]
        raw = np.empty((2, 4, DH, S), _BF)
        for rr in range(4):
            gpos = local_positions(rr)
            raw[:, :, :, gpos] = kvT[rr][0:2]
        kslc = rel_arrange([kvT[rr][2] for rr in range(4)], r, 2)
        kwin = rel_arrange([kvT[rr][3] for rr in range(4)], r, 2)
        vslc = rel_arrange([with_ones(vtm[rr][0], 4) for rr in range(4)], r, 0)
        vwin = rel_arrange([with_ones(vtm[rr][1], 4) for rr in range(4)], r, 0)
        maps.append(dict(
            h_in=np.asarray(r2[c]["h_out"]), p1=np.ascontiguousarray(I["p"][1][b][pos]), qTB=np.asarray(r2[c]["qTB"]),
            gates=np.asarray(r2[c]["gates"]), rawT=raw, kslc_rel=kslc, vslc_rel=vslc, kwin_rel=kwin, vwin_rel=vwin,
            cmp_w1_k=I["cmp_w1_k"], cmp_w1_v=I["cmp_w1_v"], cmp_w2_k=I["cmp_w2_k"], cmp_w2_v=I["cmp_w2_v"],
            cmp_pe_k=I["cmp_pe_k"], cmp_pe_v=I["cmp_pe_v"], b_w_out=I["b_w_out"][0], mlp_w1=I["mlp_w1"][1],
            mlp_w2=I["mlp_w2"][1], ple_w_gate=I["ple_w_gate"][1], ple_w_proj=I["ple_w_proj"][1],
            g_mlp=gain_layout(I["mlp_norm_g"][1]), g_ple=gain_layout(I["ple_norm_g"][1]),
            g_fin=gain_layout(I["final_norm_g"]), **nsa_host_consts(r), **hc))
    return maps


def kernel_unfused(**inputs):
    I = {k_: np.asarray(v) for k_, v in inputs.items()}
    cores = list(range(8))
    r1 = run_bass_kernel_spmd(_prog("p1", build_p1), p1_maps(I), core_ids=cores).results
    r2 = run_bass_kernel_spmd(_prog("p2", build_p2), p2_maps(I, r1), core_ids=cores).results
    del r1
    r3 = run_bass_kernel_spmd(_prog("p3", build_p3), p3_maps(I, r2), core_ids=cores).results
    out = np.empty((B, S, D), np.float32)
    for c in range(8):
        b, r = c // 4, c % 4
        out[b][local_positions(r)] = np.asarray(r3[c]["out"])
    return out


TP = 8192
NBP = TP // TB


def private_positions(r):
    pi = np.arange(64)[:, None]
    l = np.arange(128)[None, :]
    return ((pi + r - 3) * 128 + l).reshape(-1)


def build_fused(nbp=NBP, nb3=NB, phases=(1, 2, 3)):
    k = KB(bf16_bank=6, wslot_elems=4096)
    k.wcache_on = True
    k.fast_rstd = True
    nc, P = k.nc, k.P
    xp = k.din("xp", [TP, D])
    p0p = k.din("p0p", [TP, DPLE])
    p1o = k.din("p1o", [TL, DPLE])
    w_in = k.din("a_w_in", [D, 3 * HA * DH])
    a_w_out = k.din("a_w_out", [8 * DH, D])
    w1 = [k.din("mlp_w1_0", [D, DFF]), k.din("mlp_w1_1", [D, DFF])]
    w2 = [k.din("mlp_w2_0", [DFF, D]), k.din("mlp_w2_1", [DFF, D])]
    wgate = [k.din("ple_w_gate_0", [D, D]), k.din("ple_w_gate_1", [D, D])]
    wproj = [k.din("ple_w_proj_0", [DPLE, D]), k.din("ple_w_proj_1", [DPLE, D])]
    w_qg = k.din("b_w_qg", [D, 2096])
    w_kv = k.din("w_kv", [D, 3072])
    b_w_out = k.din("b_w_out", [D, D])
    cw1 = [k.din("cmp_w1_k", [32 * DH, 256]), k.din("cmp_w1_v", [32 * DH, 256])]
    cw2 = [k.din("cmp_w2_k", [256, DH]), k.din("cmp_w2_v", [256, DH])]
    cpe = [k.din("cmp_pe_k", [32, DH]), k.din("cmp_pe_v", [32, DH])]
    cos_d = k.din("cosP", [32, TP])
    sin_d = k.din("sinP", [32, TP])
    scoreM_d = k.din("scoreM", [NT, 128, 128])
    scoreA_d = k.din("scoreA", [NT, 128, 128])
    cmpmask_d = k.din("cmpmask", [NT, 128, 4, 128], BF16)
    estd_d = k.din("estd", [128, 8192], BF16)
    ovp_d = k.din("ov_perm", [128, 4, 128], BF16)
    out_o = k.dout("out", [TL, D], F32)
    def scratch(name, shape, dt=BF16):
        return nc.dram_tensor(name, list(shape), dt).ap()
    qT_s = scratch("qT_s", [HA, DH, TP])
    kT_s = scratch("kT_s", [HA, DH, TP])
    v1_s = scratch("v1_s", [TP, HA, 129])
    h_own = scratch("h_own", [KC, 128, TL], F32)
    qTB_s = scratch("qTB_s", [16, DH, TL])
    gates_s = scratch("gates_s", [TL, 48], F32)
    kvT_s = scratch("kvT_s", [4, 4, DH, TP])
    vtm1_s = scratch("vtm1_s", [2, TP, 4, 129])
    xT_s = scratch("xT_s", [KC, 128, TP], F32)
    k.common_consts()
    g_attn0, g_attn0b = k.load_const("g_attn0", [128, KC])
    g_mlp0, g_mlp0b = k.load_const("g_mlp0", [128, KC])
    g_ple0, g_ple0b = k.load_const("g_ple0", [128, KC])
    g_attn1, g_attn1b = k.load_const("g_attn1", [128, KC])
    g_kv, g_kvb = k.load_const("g_kv", [128, KC])
    g_mlp1, g_mlp1b = k.load_const("g_mlp1", [128, KC])
    g_ple1, g_ple1b = k.load_const("g_ple1", [128, KC])
    g_fin, g_finb = k.load_const("g_fin", [128, KC])
    validt, validtb = k.load_const("validt", [128, 64], cast=True)
    maskA, maskAb = k.load_const("maskA", [128, 9, 128], cast=True)
    maskB, maskBb = k.load_const("maskB", [128, 2, 128], cast=True)
    cosC, cosCb = k.load_const("cosC", [32, 512])
    sinC, sinCb = k.load_const("sinC", [32, 512])
    ovp, ovpb = k.sb("ovp_sb", [128, 4, 128], BF16)
    P.dma("sp", ovp[:, :, :], ovp_d, writes=[ovpb])
    hT, hTb = k.sb("hT", [128, KC, TB], F32)
    hnT, hnTb = k.sb("hnT", [128, KC, TB], BF16)
    oT, oTb = hnT, hnTb
    big, bigb = k.sb("big", [128, 64 * TB], BF16)
    cosT, cosTb = k.sb("cosT_sb", [32, TB], F32)
    sinT, sinTb = k.sb("sinT_sb", [32, TB], F32)
    kcmpT, kcmpTb = k.sb("kcmpT", [128, 4, 512], BF16)
    vcmp1, vcmp1b = k.sb("vcmp1", [128, 4, 4, 129], BF16)
    common_scratch(k)
    k.g_x, k.g_u = k.relu_t.items[0], k.relu_t.items[1]
    gsb = Rot([k.sb(f"gsb{i}", [128, 48], F32) for i in range(2)])
    a_pT = Rot([k.sb(f"pT{i}", [128, 512], BF16) for i in range(4)])
    a_rz = k.sb("rz", [128, 4], F32)
    a_otok = Rot([k.sb(f"otok{i}", [128, 4, 128], BF16) for i in range(2)])
    hidT, hidTb = k.sb("hidT", [128, 2, 512], BF16)
    w2sb, w2sbb = k.sb("w2sb", [128, 2, 128], BF16)
    peT, peTb = k.sb("peT", [128, 32], BF16)
    biasT, biasTb = k.sb("biasT", [128, 2], F32)
    gsc, gscb = k.sb("gates_sb", [128, 48], F32)
    sM, sMb = k.sb("sM", [128, 128], F32)
    sA, sAb = k.sb("sA", [128, 128], F32)
    cmk, cmkb = k.sb("cmk", [128, 4, 128], BF16)
    imp, impb = k.sb("imp", [128, 128], F32)
    scr, scrb = k.sb("scr", [128, 128], F32)
    m8, m8b = k.sb("m8", [128, 16], F32)
    selb, selbb = k.sb("selb", [128, 128], BF16)
    selT, selTb = k.sb("selT", [128, 128], BF16)
    otot, ototb = k.sb("otot", [128, 4, 128], F32)
    sc4, sc4b = k.sb("sc4", [128, 4], F32)
    mmb = [0, 1, 2, 3]
    NELE = 64 * TB

    def mk_attn(reg, kshape, vshape, nq):
        a = AttnRes()
        a.kslots = Rot([reg.take(f"kslot{i}", kshape) for i in range(3)])
        a.vslots = Rot([reg.take(f"vslot{i}", vshape) for i in range(3)])
        a.qs = Rot([reg.take(f"qs{i}", [128, nq, 128]) for i in range(1)])
        a.pT, a.rz, a.otok = a_pT, a_rz, a_otok
        a.sbanks = Rot([0, 1, 3])
        a.acc = (4, 5)
        a.region_bufs = [b for _, b in a.kslots.items + a.vslots.items + a.qs.items]
        return a
    regA = Region(k, big, NELE)
    aA = mk_attn(regA, [128, 4, 5 * 128], [128, 5, 4, 129], HA)
    aA.qs = Rot(aA.qs.items + [regA.take("qs_b", [128, HA, 128])])
    aA.region_bufs.append(aA.qs.items[1][1])
    assert regA.off <= 23552, regA.off
    regA2 = Region(k, big, NELE)
    vasm2 = [regA2.take(f"vasm2_{i}", [128, 2, 4, 129]) for i in range(4)]
    regP1 = Region(k, big, NELE)
    vasm = [regP1.take(f"vasm{i}", [128, HA, 129]) for i in range(4)]
    regA.off = 23552
    stage = Rot([regA.take(f"stage{i}", [128, D], F32) for i in range(2)])
    regB = Region(k, big, NELE)
    aB = mk_attn(regB, [128, 17 * 128], [128, 17, 129], 16)
    estd, estdb = regB.take("estd", [128, 8192])
    assert regB.off <= 23552, regB.off
    regC = Region(k, big, NELE)
    w1sb, w1sbb = regC.take("w1sb", [128, 32, 256])
    rawsb = Rot([regC.take(f"rawsb{i}", [128, S]) for i in range(2)])
    allreg = (aA.region_bufs + aB.region_bufs + [b for _, b in vasm + vasm2 + stage.items + rawsb.items]
              + [bigb, estdb, w1sbb])
    tabs = [cosTb, sinTb]
    tabsC = [cosCb, sinCb]
    k.store_reads = []

    def load_tabs(t0):
        P.dma("sp", cosT[:, :], cos_d[:, t0:t0 + TB], writes=[cosTb])
        P.dma("sp", sinT[:, :], sin_d[:, t0:t0 + TB], writes=[sinTb])

    def ep_add(ps, psb, ch):
        P.op("dve", lambda e: e.tensor_tensor(hT[:, ch, :], hT[:, ch, :], ps[:, :], ALU.add), [psb, hTb], [hTb])

    pre2, pre3 = [], []
    if 2 in phases and 1 in phases:
        for (w, K_, c0_, n_, mode) in ((a_w_out, 8 * DH, 0, D, "fm"), (w1[0], D, 0, DFF, "fm"), (w2[0], DFF, 0, D, "fm"),
                                       (wgate[0], D, 0, D, "fm"), (wproj[0], DPLE, 0, D, "fm"),
                                       (w_qg, D, 0, 16 * DH, "fm"), (w_qg, D, 16 * DH, 48, "tm"),
                                       (w_kv, D, 0, 512, "fm"), (w_kv, D, 512, 512, "fm"), (w_kv, D, 1024, 512, "fm"),
                                       (w_kv, D, 1536, 512, "tm"), (w_kv, D, 2048, 512, "fm"), (w_kv, D, 2560, 512, "tm")):
            pre2 += k.precast_list(w, K_, c0_, n_, mode)
    if 3 in phases and 2 in phases:
        for (w, K_, c0_, n_, mode) in ((b_w_out, D, 0, D, "fm"), (w1[1], D, 0, DFF, "fm"), (w2[1], DFF, 0, D, "fm"),
                                       (wgate[1], D, 0, D, "fm"), (wproj[1], DPLE, 0, D, "fm")):
            pre3 += k.precast_list(w, K_, c0_, n_, mode)

    def drain(lst, n):
        for _ in range(min(n, len(lst))):
            k.precast_issue(lst.pop(0))
    if 1 in phases:
        hn_alt, hn_altb = regP1.take("hn_alt", [128, KC, TB])
        assert regP1.off <= 23552, regP1.off
        hns = [(hnT, hnTb), (hn_alt, hn_altb)]
        fence(k, allreg + [hn_altb])

        def prep(tb):
            t0 = tb * TB
            k.load_transpose(xp, t0, KC, hT, hTb, k.ident, k.identb, stage)
            k.store(xT_s[:, :, t0:t0 + TB].rearrange("k p t -> p k t"), hT[:, :, :], hTb)
            hn, hnb = hns[tb % 2]
            k.rmsnorm(hT, hTb, g_attn0, g_attn0b, hn, hnb)
        prep(0)
        for tb in range(nbp):
            t0 = tb * TB
            hn, hnb = hns[tb % 2]
            if tb >= 1:
                drain(pre2, -(-len(pre2) // max(1, nbp - tb)))
            load_tabs(t0)

            def ep_qk(ps, psb, ch, t0=t0):
                dst = (qT_s if ch < HA else kT_s)[ch % HA, :, t0:t0 + TB]
                k.rope_head(ps, psb, cosT, sinT, tabs, 0, TB, dst)
            k.linear_fm(w_in, D, 0, 2 * HA * DH, hn, hnb, TB, ep_qk, mmb)
            if tb + 1 < nbp:
                prep(tb + 1)

            def ep_v(ps, psb, tile, c0, gc):
                vt, vtb = vasm[tile]
                h0 = c0 // 128
                P.op("act", lambda e: e.copy(vt[:, h0:h0 + gc // 128, 0:128],
                                             ps[:, :gc].rearrange("p (a b) -> p a b", a=gc // 128)), [psb], [vtb])
            k.linear_tm(w_in, D, 2 * HA * DH, HA * DH, hn, hnb, TB // 128, ep_v, mmb)
            for tile in range(4):
                vt, vtb = vasm[tile]
                pi_ = tb * 4 + tile
                P.op("dve", lambda e, vt=vt, pi_=pi_: e.tensor_copy(
                    vt[:, :, 128], validt[:, pi_:pi_ + 1].to_broadcast([128, HA])), [validtb], [vtb])
                k.store(v1_s[t0 + tile * 128:t0 + (tile + 1) * 128, :, :], vt[:, :, :], vtb)
        fence(k, allreg + [hn_altb])
        drain(pre2, len(pre2))
        k.phase_barrier()

    if 2 in phases:
        gains0 = (g_mlp0, g_mlp0b, g_ple0, g_ple0b)
        for tb in range(nbp):
            t0 = tb * TB
            fence(k, allreg)
            if tb >= 1:
                drain(pre3, -(-len(pre3) // max(1, nbp - tb)))
            if 1 in phases:
                P.dma("sp", hT[:, :, :], xT_s[:, :, t0:t0 + TB].rearrange("k p t -> p k t"), reads=[k.phaseb], writes=[hTb])
            else:
                k.load_transpose(xp, t0, KC, hT, hTb, k.ident, k.identb, stage)
            for qt in range(4):
                pq = tb * 4 + qt
                qs, qsb = aA.qs.next()
                P.dma("sp", qs[:, :, :], qT_s[:, :, pq * 128:(pq + 1) * 128].rearrange("h d t -> d h t"),
                      reads=[k.phaseb], writes=[qsb])
                items = []
                for quad in range(2):
                    qi = []
                    for g, (dil, nk) in enumerate(GROUPS_A):
                        lo = max(0, pq - nk)
                        for c_lo in range(lo, pq + 1, 5):
                            qi.append((quad, g, nk, c_lo, min(c_lo + 4, pq)))
                    items += [(it, j == 0, j == len(qi) - 1) for j, it in enumerate(qi)]
                loaded = []

                def issue(i):
                    (quad, g, nk, c_lo, c_hi), _, _ = items[i]
                    h0 = g * 8 + quad * 4
                    cnt = c_hi - c_lo + 1
                    ks, ksb = aA.kslots.next()
                    vs, vsb_ = aA.vslots.next()
                    P.dma("sp", ks[:, :, 0:cnt * 128],
                          kT_s[h0:h0 + 4, :, c_lo * 128:(c_hi + 1) * 128].rearrange("h d t -> d h t"),
                          reads=[k.phaseb], writes=[ksb])
                    P.dma("sp", vs[:, 0:cnt, :, :],
                          v1_s[c_lo * 128:(c_hi + 1) * 128, h0:h0 + 4, :].rearrange("(m l) h c -> l m h c", l=128),
                          reads=[k.phaseb], writes=[vsb_])
                    loaded.append((ks, ksb, vs, vsb_))
                pipe = Pipe(depth=2, nslots=3, issue=issue, nchunks=len(items))

                def finalize(quad, qt=qt):
                    rz, rzb = attn_rz(k, aA, tiny=1e-30)
                    ot, otb = aA.otok.next()
                    for h in range(4):
                        acc = k.ps[aA.acc[h // 2]]
                        P.op("dve", lambda e, h=h, acc=acc, ot=ot, rz=rz: e.tensor_scalar(
                            ot[:, h, :], acc[:, (h % 2) * 129:(h % 2) * 129 + 128], rz[:, h:h + 1], None, ALU.mult),
                            [k.psb[aA.acc[h // 2]], rzb], [otb])
                    transpose_heads(k, aA, ot, otb,
                                    lambda: oT[:, quad * 4:quad * 4 + 4, qt * 128:(qt + 1) * 128], oTb, 6)
                for i, ((quad, g, nk, c_lo, c_hi), first, last) in enumerate(items):
                    pipe.begin_chunk(i)
                    ks, ksb, vs, vsb_ = loaded[i]
                    h0 = g * 8 + quad * 4
                    for tile in range(c_lo, c_hi + 1):
                        j = pq - tile
                        kind = 0 if j == 0 else (2 if j == nk else 1)
                        mt = maskA[:, g * 3 + kind, :].unsqueeze(1).to_broadcast([128, 4, 128])
                        pipe.step(lambda ks=ks, ksb=ksb, vs=vs, vsb_=vsb_, tile=tile, c_lo=c_lo, h0=h0, mt=mt: attn_step(
                            k, aA, ks, ksb, tile - c_lo, False, qs, qsb, [h0 + h for h in range(4)], vs, vsb_,
                            (k.identbf[:], mt, [k.identbfb, maskAb]), defer=True),
                            first_hook=(lambda: attn_zero_acc(k, aA)) if (first and tile == c_lo) else None,
                            last_hook=(lambda quad=quad: finalize(quad)) if (last and tile == c_hi) else None, chunk=i)
                pipe.flush()
            k.linear_fm(a_w_out, 8 * DH, 0, D, oT, oTb, TB, ep_add, mmb)
            fence(k, allreg)
            mlp_ple(k, 0, hT, hTb, hnT, hnTb, big, bigb, gains0, w1[0], w2[0], wgate[0], wproj[0], p0p, t0, stage, mmb, allreg)
            fence(k, allreg)
            k.store(h_own[:, :, tb * 128:(tb + 1) * 128].rearrange("k p t -> p k t"), hT[:, :, 384:512], hTb)
            load_tabs(t0)
            k.rmsnorm(hT, hTb, g_attn1, g_attn1b, hnT, hnTb)
            hn_own = hnT[:, :, 384:512]

            def ep_q(ps, psb, ch, tb=tb):
                k.rope_head(ps, psb, cosT, sinT, tabs, 384, 128, qTB_s[ch, :, tb * 128:(tb + 1) * 128])
            k.linear_fm(w_qg, D, 0, 16 * DH, hn_own, hnTb, 128, ep_q, mmb)

            def ep_g(ps, psb, tile, c0, gc, tb=tb):
                gs, gsbuf = gsb.next()
                P.op("act", lambda e: e.activation(gs[:, :gc], ps[:, :gc], AF.Sigmoid), [psb], [gsbuf])
                k.store(gates_s[tb * 128:(tb + 1) * 128, :], gs[:, :gc], gsbuf)
            k.linear_tm(w_qg, D, 16 * DH, 48, hnT, hnTb, 4, ep_g, mmb, tiles=[3])
            k.rmsnorm(hT, hTb, g_kv, g_kvb, hnT, hnTb)

            def mk_ep_fm(slot, rope, t0=t0):
                def ep(ps, psb, ch):
                    dst = kvT_s[slot, ch, :, t0:t0 + TB]
                    if rope:
                        k.rope_head(ps, psb, cosT, sinT, tabs, 0, TB, dst)
                    else:
                        xb, xbb = k.xbf.next()
                        P.op("act", lambda e: e.copy(xb[:, :], ps[:, :]), [psb], [xbb])
                        k.store(dst, xb[:, :], xbb)
                return ep

            def mk_ep_tm(slot):
                def ep(ps, psb, tile, c0, gc):
                    vt, vtb = vasm2[tile]
                    g0 = c0 // 128
                    P.op("act", lambda e: e.copy(vt[:, slot, g0:g0 + gc // 128, 0:128],
                                                 ps[:, :gc].rearrange("p (a b) -> p a b", a=gc // 128)), [psb], [vtb])
                return ep
            k.linear_fm(w_kv, D, 0, 512, hnT, hnTb, TB, mk_ep_fm(0, False), mmb)
            k.linear_fm(w_kv, D, 512, 512, hnT, hnTb, TB, mk_ep_fm(1, False), mmb)
            k.linear_fm(w_kv, D, 1024, 512, hnT, hnTb, TB, mk_ep_fm(2, True), mmb)
            k.linear_tm(w_kv, D, 1536, 512, hnT, hnTb, 4, mk_ep_tm(0), mmb)
            k.linear_fm(w_kv, D, 2048, 512, hnT, hnTb, TB, mk_ep_fm(3, True), mmb)
            k.linear_tm(w_kv, D, 2560, 512, hnT, hnTb, 4, mk_ep_tm(1), mmb)
            for tile in range(4):
                vt, vtb = vasm2[tile]
                pi_ = tb * 4 + tile
                P.op("dve", lambda e, vt=vt, pi_=pi_: e.tensor_copy(
                    vt[:, :, :, 128], validt[:, pi_:pi_ + 1].unsqueeze(1).to_broadcast([128, 2, 4])), [validtb], [vtb])
                for slot in range(2):
                    k.store(vtm1_s[slot, t0 + tile * 128:t0 + (tile + 1) * 128, :, :], vt[:, slot, :, :], vtb)
        drain(pre3, len(pre3))
        k.phase_barrier()

    if 3 in phases:
        fence(k, allreg)
        P.op("dve", lambda e: e.memset(vcmp1[:, :, :, 128:129], 1.0), [], [vcmp1b])
        P.op("dve", lambda e: e.memset(hidT[:, :, :], 0.0), [], [hidTb])
        P.op("dve", lambda e: e.memset(kcmpT[:, :, :], 0.0), [], [kcmpTb])
        for kv in range(2):
            fence(k, [w1sbb, w2sbb, peTb])
            P.dma("pool", w1sb[:, :, :], cw1[kv].rearrange("(l d) h -> d l h", d=128), writes=[w1sbb])
            P.dma("pool", w2sb[:, :, :], cw2[kv].rearrange("(c p) d -> p c d", p=128), writes=[w2sbb])
            P.op("pool", lambda e, kv=kv: e.dma_start(out=peT[:, :], in_=cpe[kv].rearrange("l d -> d l"),
                                                      allow_slow_non_contiguous=True), [], [peTb], is_dma=True)
            ps7, ps7b = k.ps[7], k.psb[7]
            for hc in range(2):
                for l in range(32):
                    P.op("pe", lambda e, hc=hc, l=l: e.matmul(ps7[:, hc:hc + 1], w1sb[:, l, hc * 128:(hc + 1) * 128],
                                                              peT[:, l:l + 1], start=(l == 0), stop=(l == 31),
                                                              skip_group_check=True), [w1sbb, peTb], [ps7b])
            P.op("dve", lambda e: e.tensor_copy(biasT[:, :], ps7[:, 0:2]), [ps7b], [biasTb])
            for g in range(4):
                rs, rsb = rawsb.next()
                P.dma("sp", rs[:, :], kvT_s[kv, g, :, :], reads=[k.phaseb], writes=[rsb])
                for hc in range(2):
                    ps, psb = k.ps[mmb[hc]], k.psb[mmb[hc]]
                    for l in range(32):
                        P.op("pe", lambda e, ps=ps, rs=rs, hc=hc, l=l: e.matmul(
                            ps[:, 0:511], w1sb[:, l, hc * 128:(hc + 1) * 128], rs[:, l:l + 16 * 510 + 1:16],
                            start=(l == 0), stop=(l == 31)), [w1sbb, rsb], [psb])
                    gelu_tanh(k, hidT[:, hc, 0:511], hidTb, ps, psb, biasT[:, hc:hc + 1], biasTb, 511)
                if kv == 0:
                    ps, psb = k.ps[2], k.psb[2]
                    for hc in range(2):
                        P.op("pe", lambda e, ps=ps, hc=hc: e.matmul(ps[:, 0:512], w2sb[:, hc, :], hidT[:, hc, :],
                                                                    start=(hc == 0), stop=(hc == 1)), [w2sbb, hidTb], [psb])
                    xb, xbb = k.xbf.next()
                    P.op("act", lambda e, ps=ps, xb=xb: e.copy(xb[:, :], ps[:, :]), [psb], [xbb])
                    ps2, ps2b = k.ps[7], k.psb[7]
                    P.op("pe", lambda e, xb=xb: e.matmul(ps2[0:32, :], k.pm[0:32, 0:32], xb[0:32, :], start=True, stop=True),
                         [xbb, k.pmb], [ps2b])
                    t1, t1b = k.rt1.next()
                    t2, t2b = k.rt2.next()
                    P.op("dve", lambda e, ps=ps, t1=t1: e.tensor_tensor(t1[0:32, :], ps[0:32, :], cosC[0:32, :], ALU.mult), [psb] + tabsC, [t1b])
                    P.op("dve", lambda e, t2=t2: e.tensor_tensor(t2[0:32, :], ps2[0:32, :], sinC[0:32, :], ALU.mult), [ps2b] + tabsC, [t2b])
                    P.op("dve", lambda e, xb=xb, t1=t1, t2=t2: e.tensor_tensor(xb[0:32, :], t1[0:32, :], t2[0:32, :], ALU.add), [t1b, t2b, xbb], [xbb])
                    P.op("act", lambda e, xb=xb, g=g: e.copy(kcmpT[:, g, 0:511], xb[:, 0:511]), [xbb], [kcmpTb])
                else:
                    for ct in range(4):
                        ps, psb = k.ps[2 + ct % 2], k.psb[2 + ct % 2]
                        for hc in range(2):
                            P.op("pe", lambda e, ps=ps, hc=hc, ct=ct: e.matmul(ps[:, 0:128], hidT[:, hc, ct * 128:(ct + 1) * 128],
                                                                               w2sb[:, hc, :], start=(hc == 0), stop=(hc == 1)),
                                 [w2sbb, hidTb], [psb])
                        P.op("act", lambda e, ps=ps, ct=ct, g=g: e.copy(vcmp1[:, ct, g, 0:128], ps[:, 0:128]), [psb], [vcmp1b])
        gains1 = (g_mlp1, g_mlp1b, g_ple1, g_ple1b)
        for tb in range(nb3):
            t0 = tb * TB
            fence(k, allreg)
            P.dma("sp", estd[:, :], estd_d, writes=[estdb])
            P.dma("sp", hT[:, :, :], h_own[:, :, t0:t0 + TB].rearrange("k p t -> p k t"), reads=[k.phaseb], writes=[hTb])
            for qt in range(4):
                n = tb * 4 + qt
                qs, qsb = aB.qs.next()
                P.dma("sp", qs[:, :, :], qTB_s[:, :, n * 128:(n + 1) * 128].rearrange("h d t -> d h t"),
                      reads=[k.phaseb], writes=[qsb])
                P.dma("sp", gsc[:, :], gates_s[n * 128:(n + 1) * 128, :], reads=[k.phaseb], writes=[gscb])
                P.dma("sp", sM[:, :], scoreM_d[n], writes=[sMb])
                P.dma("sp", sA[:, :], scoreA_d[n], writes=[sAb])
                P.dma("sp", cmk[:, :, :], cmpmask_d[n], writes=[cmkb])
                for g in range(4):
                    nsa_group_f(k, aB, n, g, qs, qsb, gsc, gscb, sM, sMb, sA, sAb, cmk, cmkb, kcmpT, kcmpTb, vcmp1, vcmp1b,
                                ovp, ovpb, estd, estdb, maskB, maskBb, imp, impb, scr, scrb, m8, m8b, selb, selbb, selT, selTb,
                                otot, ototb, sc4, sc4b, kvT_s, vtm1_s, oT, oTb, qt)
            k.linear_fm(b_w_out, D, 0, D, oT, oTb, TB, ep_add, mmb)
            fence(k, allreg)
            mlp_ple(k, 1, hT, hTb, hnT, hnTb, big, bigb, gains1, w1[1], w2[1], wgate[1], wproj[1], p1o, t0, stage, mmb, allreg)
            fence(k, allreg)
            yT = big[:, 0:2 * KC * TB].bitcast(F32).rearrange("p (a b) -> p a b", a=KC)
            final_norm_store(k, hT, hTb, g_fin, g_finb, yT, bigb, stage, out_o, t0)
    return k.finish()


def nsa_group_f(k, a, n, g, qs, qsb, gsc, gscb, sM, sMb, sA, sAb, cmk, cmkb, kcmpT, kcmpTb, vcmp1, vcmp1b,
                ovp, ovpb, estd, estdb, maskB, maskBb, imp, impb, scr, scrb, m8, m8b, selb, selbb, selT, selTb,
                otot, ototb, sc4, sc4b, kvT_s, vtm1_s, oT, oTb, qt):
    P = k.P
    pq = 4 * n + 3
    qheads = [4 * g + h for h in range(4)]
    ip, ipb = k.ps[2], k.psb[2]

    def gate_scale(branch, tiny):
        rz, rzb = attn_rz(k, a, tiny=tiny)
        c0 = branch * 16 + 4 * g
        P.op("dve", lambda e: e.tensor_tensor(sc4[:, :], rz[:, :], gsc[:, c0:c0 + 4], ALU.mult), [rzb, gscb], [sc4b])
        return rz, rzb

    def accum_otot(first):
        for h in range(4):
            acc = k.ps[a.acc[h // 2]]
            accb = k.psb[a.acc[h // 2]]
            src = acc[:, (h % 2) * 129:(h % 2) * 129 + 128]
            if first:
                P.op("dve", lambda e, h=h, src=src: e.tensor_scalar(otot[:, h, :], src, sc4[:, h:h + 1], None, ALU.mult),
                     [accb, sc4b], [ototb])
            else:
                P.op("dve", lambda e, h=h, src=src: e.scalar_tensor_tensor(otot[:, h, :], src, sc4[:, h:h + 1], otot[:, h, :],
                                                                           ALU.mult, ALU.add), [accb, sc4b, ototb], [ototb])
    chunks = [("win", max(0, pq - 4), pq)] + [("slc", c_lo, min(c_lo + 16, pq)) for c_lo in range(0, pq + 1, 17)]
    loaded = []

    def issue(i):
        kind, c_lo, c_hi = chunks[i]
        cnt = c_hi - c_lo + 1
        ks, ksb = a.kslots.next()
        vs, vsb_ = a.vslots.next()
        ksl, vsl = (2, 0) if kind == "slc" else (3, 1)
        P.dma("sp", ks[:, 0:cnt * 128], kvT_s[ksl, g, :, c_lo * 128:(c_hi + 1) * 128], reads=[k.phaseb], writes=[ksb])
        P.dma("sp", vs[:, 0:cnt, :], vtm1_s[vsl, c_lo * 128:(c_hi + 1) * 128, g, :].rearrange("(m l) c -> l m c", l=128),
              reads=[k.phaseb], writes=[vsb_])
        loaded.append((ks, ksb, vs, vsb_))
    pipe = Pipe(depth=2, nslots=3, issue=issue, nchunks=len(chunks))
    pipe.begin_chunk(0)
    nct = min((16 * pq + 7) // 128 + 1, 4)

    def cmp_first():
        attn_zero_acc(k, a)
        P.op("dve", lambda e: e.memset(ip[:, :], 0.0), [], [ipb])

    def cmp_last():
        rz, rzb = gate_scale(0, 1e-20)
        P.op("dve", lambda e: e.tensor_scalar(imp[:, :], ip[:, 0:128], rz[:, 0:1], None, ALU.mult), [ipb, rzb], [impb])
        for h in range(1, 4):
            P.op("dve", lambda e, h=h: e.scalar_tensor_tensor(imp[:, :], ip[:, h * 128:(h + 1) * 128], rz[:, h:h + 1], imp[:, :],
                                                              ALU.mult, ALU.add), [ipb, rzb, impb], [impb])
        accum_otot(True)
        P.op("dve", lambda e: e.tensor_tensor(imp[:, :], imp[:, :], sM[:, :], ALU.mult), [impb, sMb], [impb])
        P.op("dve", lambda e: e.tensor_tensor(imp[:, :], imp[:, :], sA[:, :], ALU.add), [impb, sAb], [impb])
        P.op("dve", lambda e: e.max(m8[:, 0:8], imp[:, :]), [impb], [m8b])
        P.op("dve", lambda e: e.match_replace(scr[:, :], m8[:, 0:8], imp[:, :], -3.0e38), [impb, m8b], [scrb])
        P.op("dve", lambda e: e.max(m8[:, 8:16], scr[:, :]), [scrb], [m8b])
        P.op("dve", lambda e: e.tensor_scalar(scr[:, :], imp[:, :], m8[:, 15:16], 1.0, ALU.is_ge, ALU.subtract), [impb, m8b], [scrb])
        P.op("dve", lambda e: e.tensor_scalar(selb[:, :], scr[:, :], -NEG, None, ALU.mult), [scrb], [selbb])
        p6, p6b = k.ps[6], k.psb[6]
        P.op("pe", lambda e: e.transpose(p6[:, 0:128], selb[:, :], k.identbf[:]), [selbb, k.identbfb], [p6b])
        P.op("act", lambda e: e.copy(selT[:, :], p6[:, 0:128]), [p6b], [selTb])
    for ct in range(nct):
        def extra(pT, pTb, ct=ct):
            for h in range(4):
                P.op("pe", lambda e, h=h: e.matmul(ip[:, h * 128:(h + 1) * 128], pT[:, h * 128:(h + 1) * 128], ovp[:, ct, :],
                                                   start=False, stop=False, skip_group_check=True), [pTb, ovpb], [ipb])
        mt = cmk[:, ct, :].unsqueeze(1).to_broadcast([128, 4, 128])
        pipe.step(lambda ct=ct, mt=mt, extra=extra: attn_step(
            k, a, kcmpT[:, g, :], kcmpTb, ct, True, qs, qsb, qheads, None, vcmp1b,
            (k.identbf[:], mt, [k.identbfb, cmkb]), extra=extra, vsel=lambda h, ct=ct: vcmp1[:, ct, g, :], defer=True),
            first_hook=cmp_first if ct == 0 else None, last_hook=cmp_last if ct == nct - 1 else None)
    for i, (kind, c_lo, c_hi) in enumerate(chunks):
        pipe.begin_chunk(i)
        ks, ksb, vs, vsb_ = loaded[i]
        for tile in range(c_lo, c_hi + 1):
            s = tile - c_lo
            if tile == pq:
                mm = (k.identbf[:], maskB[:, 0, :].unsqueeze(1).to_broadcast([128, 4, 128]), [k.identbfb, maskBb])
            elif kind == "slc":
                mm = (estd[:, 128 * tile:128 * tile + 128], selT[:, :].unsqueeze(1).to_broadcast([128, 4, 128]), [estdb, selTb])
            elif tile == pq - 4:
                mm = (k.identbf[:], maskB[:, 1, :].unsqueeze(1).to_broadcast([128, 4, 128]), [k.identbfb, maskBb])
            else:
                mm = None
            is_first = (tile == c_lo) and (kind == "win" or c_lo == 0)
            is_last = (tile == pq)
            branch = 2 if kind == "win" else 1

            def last_hook(branch=branch):
                gate_scale(branch, 1e-30)
                accum_otot(False)
            pipe.step(lambda ks=ks, ksb=ksb, vs=vs, vsb_=vsb_, s=s, mm=mm: attn_step(
                k, a, ks, ksb, s, True, qs, qsb, qheads, None, vsb_, mm, vsel=lambda h, vs=vs, s=s: vs[:, s, :], defer=True),
                first_hook=(lambda: attn_zero_acc(k, a)) if is_first else None,
                last_hook=last_hook if is_last else None, chunk=i)
    pipe.flush()
    ot, otb = a.otok.next()
    P.op("act", lambda e: e.copy(ot[:, :, :], otot[:, :, :]), [ototb], [otb])
    transpose_heads(k, a, ot, otb, lambda: oT[:, 4 * g:4 * g + 4, qt * 128:(qt + 1) * 128], oTb, 6)


def nsa_private_consts(r):
    sh = r - 3
    jp = np.arange(128)
    j_of = jp + 2 * sh
    cp = np.arange(512)
    c_of = cp + 8 * sh
    cstart = c_of * 16
    ov = np.zeros((512, 128), np.float32)
    for q in range(128):
        j = j_of[q]
        if j < 0 or j >= 128:
            continue
        o = np.clip(np.minimum(cstart + 32, j * 64 + 64) - np.maximum(cstart, j * 64), 0, None) / 32.0
        ov[:, q] = np.where((c_of >= 0) & (c_of <= 510), o, 0.0)
    ov_perm = ov.reshape(4, 128, 128).transpose(1, 0, 2)
    scoreM = np.zeros((NT, 128, 128), np.float32)
    scoreA = np.zeros((NT, 128, 128), np.float32)
    cmpmask = np.zeros((NT, 128, 4, 128), np.float32)
    l = np.arange(128)
    for n in range(NT):
        t = (4 * n + r) * 128 + l
        cur = t // 64
        for q in range(128):
            j = j_of[q]
            if j < 0 or j >= 128:
                scoreA[n, :, q] = -1e30
                continue
            forced = (j == 0) | (j == cur) | (j == cur - 1)
            causal = (j * 64 <= t)
            scoreM[n, :, q] = np.where(forced | ~causal, 0.0, 1.0)
            scoreA[n, :, q] = np.where(forced, 1e9, np.where(causal, 0.0, -1e30))
        cend = (c_of * 16 + 31).reshape(4, 128)
        ok = ((c_of >= 0) & (c_of <= 510)).reshape(4, 128)
        valid = (cend.T[:, :, None] <= t[None, None, :]) & ok.T[:, :, None]
        cmpmask[n] = np.where(valid, 0.0, NEG)
    lk = np.arange(128)[:, None]
    lq = np.arange(128)[None, :]
    maskB = np.stack([np.where(lq >= lk, 0.0, NEG), np.where(lq < lk, 0.0, NEG)]).astype(np.float32)
    estd = (np.arange(8192)[None, :] // 64 == np.arange(128)[:, None]).astype(np.float32)
    cosC, sinC = rope_tables(np.maximum(c_of * 16 + 31, 0))
    bf = ml_dtypes.bfloat16
    return dict(ov_perm=np.ascontiguousarray(ov_perm).astype(bf), scoreM=scoreM, scoreA=scoreA,
                cmpmask=cmpmask.astype(bf), maskB=np.ascontiguousarray(maskB.transpose(1, 0, 2)),
                estd=estd.astype(bf), cosC=cosC, sinC=sinC)


def fused_maps(I):
    hc = host_consts()
    mA = np.ascontiguousarray(masks_A().transpose(1, 0, 2))
    maps = []
    for c in range(8):
        b, r = c // 4, c % 4
        gpos = private_positions(r)
        ok = (gpos >= 0) & (gpos < S)
        xp = np.zeros((TP, D), np.float32)
        xp[ok] = I["x"][b][gpos[ok]]
        p0p = np.zeros((TP, DPLE), np.float32)
        p0p[ok] = I["p"][0][b][gpos[ok]]
        pos = local_positions(r)
        cosP, sinP = rope_tables(np.maximum(gpos, 0))
        validt = np.ascontiguousarray(np.broadcast_to(ok.reshape(64, 128)[:, 0].astype(np.float32)[None, :], (128, 64)))
        maps.append(dict(
            xp=xp, p0p=p0p, p1o=np.ascontiguousarray(I["p"][1][b][pos]), a_w_in=I["a_w_in"][0], a_w_out=I["a_w_out"][0],
            mlp_w1_0=I["mlp_w1"][0], mlp_w1_1=I["mlp_w1"][1], mlp_w2_0=I["mlp_w2"][0], mlp_w2_1=I["mlp_w2"][1],
            ple_w_gate_0=I["ple_w_gate"][0], ple_w_gate_1=I["ple_w_gate"][1], ple_w_proj_0=I["ple_w_proj"][0],
            ple_w_proj_1=I["ple_w_proj"][1], b_w_qg=I["b_w_qg"][0], w_kv=I["w_kv_shared"], b_w_out=I["b_w_out"][0],
            cmp_w1_k=I["cmp_w1_k"], cmp_w1_v=I["cmp_w1_v"], cmp_w2_k=I["cmp_w2_k"], cmp_w2_v=I["cmp_w2_v"],
            cmp_pe_k=I["cmp_pe_k"], cmp_pe_v=I["cmp_pe_v"], cosP=cosP, sinP=sinP,
            g_attn0=gain_layout(I["attn_norm_g"][0]), g_mlp0=gain_layout(I["mlp_norm_g"][0]),
            g_ple0=gain_layout(I["ple_norm_g"][0]), g_attn1=gain_layout(I["attn_norm_g"][1]),
            g_kv=gain_layout(I["kv_norm_g"]), g_mlp1=gain_layout(I["mlp_norm_g"][1]),
            g_ple1=gain_layout(I["ple_norm_g"][1]), g_fin=gain_layout(I["final_norm_g"]),
            validt=validt, maskA=mA, **nsa_private_consts(r), **hc))
    return maps


def kernel(**inputs):
    I = {k_: np.asarray(v) for k_, v in inputs.items()}
    res = run_bass_kernel_spmd(_prog("fused", build_fused), fused_maps(I), core_ids=list(range(8))).results
    out = np.empty((B, S, D), np.float32)
    for c in range(8):
        b, r = c // 4, c % 4
        out[b][local_positions(r)] = np.asarray(res[c]["out"])
    return out
```

```python
import numpy as np
import ml_dtypes
import concourse.bass as bass
import concourse.mybir as mybir
from concourse.bass_utils import run_bass_kernel_spmd

F32 = mybir.dt.float32
BF16 = mybir.dt.bfloat16
AF = mybir.ActivationFunctionType
ALU = mybir.AluOpType

ENGS = ("pe", "act", "dve", "pool", "sp")
NDMA_SEMS = 12


class Buf:
    __slots__ = ("name", "last_w", "readers", "exclusive")

    def __init__(self, name, exclusive=False):
        self.name = name
        self.last_w = None
        self.readers = []
        self.exclusive = exclusive


class Op:
    __slots__ = ("eng", "emit", "deps", "needs_inc", "event", "waits", "is_dma")

    def __init__(self, eng, emit, is_dma=False):
        self.eng = eng
        self.emit = emit
        self.deps = []
        self.needs_inc = is_dma
        self.event = None
        self.waits = []
        self.is_dma = is_dma


class Prog:
    def __init__(self, nc):
        self.nc = nc
        self.ops = []
        self.nbuf = 0

    def buf(self, name=None, exclusive=False):
        self.nbuf += 1
        return Buf(name or f"b{self.nbuf}", exclusive)

    def op(self, eng, emit, reads=(), writes=(), is_dma=False, extra_deps=()):
        o = Op(eng, emit, is_dma)
        for d in extra_deps:
            o.deps.append(d)
            d.needs_inc = True
        xr = [b for b in reads if b.exclusive]
        if xr:
            reads = [b for b in reads if not b.exclusive]
            writes = list(writes) + [b for b in xr if b not in writes]
        deps = {}
        for b in reads:
            if b.last_w is not None:
                deps[id(b.last_w)] = b.last_w
        for b in writes:
            if b.last_w is not None:
                deps[id(b.last_w)] = b.last_w
            for r in b.readers:
                deps[id(r)] = r
        for d in deps.values():
            if d.eng == "pe" and eng == "pe" and not d.is_dma and not is_dma:
                continue
            o.deps.append(d)
            d.needs_inc = True
        for b in writes:
            b.last_w = o
            b.readers = []
        for b in reads:
            if b.last_w is not o:
                if not is_dma:
                    b.readers = [r for r in b.readers if r.is_dma or r.eng != eng]
                b.readers.append(o)
        self.ops.append(o)
        return o

    def dma(self, q, out, in_, reads=(), writes=(), extra_deps=()):
        return self.op(q, lambda e: e.dma_start(out=out, in_=in_), reads, writes, is_dma=True, extra_deps=extra_deps)

    def finalize(self, final_waits=()):
        nc = self.nc
        sems = {e: nc.alloc_semaphore(f"s_{e}") for e in ("pe", "act", "dve", "pool")}
        dsems = {q: [nc.alloc_semaphore(f"d_{q}{i}") for i in range(NDMA_SEMS)]
                 for q in ("sp", "pool", "act")}
        cnt = {e: 0 for e in sems}
        dcnt = {q: 0 for q in dsems}
        per_eng = {e: [] for e in ENGS}
        for o in self.ops:
            if o.is_dma:
                i = dcnt[o.eng]
                dcnt[o.eng] += 1
                sem = dsems[o.eng][i % NDMA_SEMS]
                o.event = (sem, 16 * (i // NDMA_SEMS + 1))
                if i >= NDMA_SEMS:
                    o.waits.append((sem, 16 * (i // NDMA_SEMS)))
            elif o.needs_inc:
                cnt[o.eng] += 1
                o.event = (sems[o.eng], cnt[o.eng])
            per_eng[o.eng].append(o)
        waited = {e: {} for e in ENGS}
        for o in self.ops:
            w = waited[o.eng]
            for s, v in o.waits:
                if w.get(id(s), 0) < v:
                    w[id(s)] = v
            for d in o.deps:
                s, v = d.event
                if w.get(id(s), 0) >= v:
                    continue
                w[id(s)] = v
                o.waits.append((s, v))
        self._per_eng = per_eng
        fin = {}
        for o in final_waits:
            sm, v = o.event
            if id(sm) not in fin or fin[id(sm)][1] < v:
                fin[id(sm)] = (sm, v)
        self._fin = list(fin.values())
        self.stats = {e: len(per_eng[e]) for e in ENGS}

    def emit(self):
        nc = self.nc
        per_eng = self._per_eng
        fin = self._fin

        def replay(eng_obj, ops, final=False):
            for o in ops:
                for s, v in o.waits:
                    eng_obj.wait_ge(s, v)
                ins = o.emit(eng_obj)
                if o.event is not None:
                    ins.then_inc(o.event[0], 16 if o.is_dma else 1)
            if final:
                for s, v in fin:
                    eng_obj.wait_ge(s, v)

        with nc.Block() as block:
            @block.tensor
            def _(e):
                replay(e, per_eng["pe"])

            @block.scalar
            def _(e):
                replay(e, per_eng["act"])

            @block.vector
            def _(e):
                replay(e, per_eng["dve"])

            @block.gpsimd
            def _(e):
                replay(e, per_eng["pool"])

            @block.sync
            def _(e):
                replay(e, per_eng["sp"], final=True)


class Rot:
    def __init__(self, items):
        self.items = list(items)
        self.i = 0

    def next(self):
        it = self.items[self.i % len(self.items)]
        self.i += 1
        return it


D = 2048
KC = 16
S = 8192
B = 2
NR = 4
TL = 2048
NT = 16
TB = 512
NB = TL // TB
DH = 128
EPS = 1e-6
SCALE = DH ** -0.5
NEG = -30000.0
DFF = 8192
DPLE = 256
DEBUG = False
HA = 24
GROUPS_A = ((1, 1), (4, 4), (16, 16))


class KB:
    def __init__(self, bf16_bank=None, wslot_elems=8192):
        self.wse = wslot_elems
        self.nc = bass.Bass("TRN2", target_bir_lowering=False)
        nc = self.nc
        self.P = Prog(nc)
        self.ps = [nc.alloc_psum_tensor(f"ps{i}", [128, 1024], BF16) if i == bf16_bank else
                   nc.alloc_psum_tensor(f"ps{i}", [128, 512], F32) for i in range(8)]
        self.psb = [self.P.buf(f"ps{i}", exclusive=True) for i in range(8)]
        self.wslots = Rot([self.sb(f"wslot{i}", [128, wslot_elems], BF16) for i in range(3)])
        self.out_dmas = []
        self.wcache = {}
        self.wcache_on = False
        self.phase_stores = []
        self.phaseb = self.P.buf("phase")
        self.dq = Rot(["sp", "act"])

    def sb(self, name, shape, dt):
        return self.nc.alloc_sbuf_tensor(name, list(shape), dt), self.P.buf(name)

    def din(self, name, shape, dt=F32):
        return self.nc.dram_tensor(name, list(shape), dt, kind="ExternalInput").ap()

    def dout(self, name, shape, dt=F32):
        return self.nc.dram_tensor(name, list(shape), dt, kind="ExternalOutput").ap()

    def load_const(self, name, shape, dt=F32, cast=False):
        d = self.din(name, shape, F32)
        t, b = self.sb(name + "_sb", shape, BF16 if cast else dt)
        self.P.dma("pool" if cast else "sp", t[:], d, writes=[b])
        return t, b

    def dump(self, name, in_ap, rbuf, shape, dt=F32):
        d = self.dout(name, shape, dt)
        self.store(d, in_ap, rbuf)

    def store(self, out_ap, in_ap, rbuf):
        o = self.P.dma("sp", out_ap, in_ap, reads=[rbuf] + list(getattr(self, "store_reads", [])))
        self.out_dmas.append(o)
        self.phase_stores.append(o)
        return o

    def phase_barrier(self):
        self.P.op("dve", lambda e: e.memset(self.fence_t[0][:, :], 0.0), [], [self.phaseb, self.fence_t[1]],
                  extra_deps=list(self.phase_stores))
        self.phase_stores = []

    def load_transpose(self, src, row0, ncol_chunks, dst, dstb, ident, identb, stage, ntiles=4, dst_dt_bf16=False):
        P = self.P
        for tile in range(ntiles):
            st, stb = stage.next()
            P.dma("sp", st[:, 0:ncol_chunks * 128], src[row0 + tile * 128: row0 + (tile + 1) * 128, :], writes=[stb])
            for k0 in range(0, ncol_chunks, 4):
                nk = min(4, ncol_chunks - k0)
                pi = self.trb.next()
                ps, psb = self.ps[pi], self.psb[pi]
                for j in range(nk):
                    P.op("pe", lambda e, ps=ps, st=st, j=j, k0=k0: e.transpose(
                        ps[:, j * 128:(j + 1) * 128], st[:, (k0 + j) * 128:(k0 + j + 1) * 128], ident[:]),
                        [stb, identb], [psb])
                P.op("act", lambda e, ps=ps, nk=nk, k0=k0, tile=tile: e.copy(
                    dst[:, k0:k0 + nk, tile * 128:(tile + 1) * 128],
                    ps[:, 0:nk * 128].rearrange("p (a b) -> p a b", a=nk)), [psb], [dstb])

    def rmsnorm(self, hT, hTb, g, gb, out, outb, T=TB, nkc=KC, reuse_stats=False):
        P = self.P
        pi = self.statb
        ps, psb = self.ps[pi], self.psb[pi]
        for kc in range(0 if reuse_stats else nkc):
            sq, sqb = self.sq.next()
            if kc % 2 == 0:
                P.op("act", lambda e, sq=sq, kc=kc: e.activation(sq[:, :T], hT[:, kc, :T], AF.Square), [hTb], [sqb])
            else:
                P.op("dve", lambda e, sq=sq, kc=kc: e.tensor_tensor(sq[:, :T], hT[:, kc, :T], hT[:, kc, :T], ALU.mult), [hTb], [sqb])
            P.op("pe", lambda e, sq=sq, kc=kc: e.matmul(ps[:, :T], self.ones[:], sq[:, :T],
                                                         start=(kc == 0), stop=(kc == nkc - 1)),
                 [sqb, self.onesb], [psb])
        rs, rsb = self.rstd
        if reuse_stats:
            pass
        elif getattr(self, "fast_rstd", False):
            P.op("act", lambda e: e.activation(rs[:, :T], ps[:, :T], AF.Ln, bias=EPS, scale=1.0 / (nkc * 128)), [psb], [rsb])
            P.op("act", lambda e: e.activation(rs[:, :T], rs[:, :T], AF.Exp, scale=-0.5), [rsb], [rsb])
        else:
            P.op("act", lambda e: e.activation(rs[:, :T], ps[:, :T], AF.Sqrt, bias=EPS, scale=1.0 / (nkc * 128)), [psb], [rsb])
            P.op("dve", lambda e: e.reciprocal(rs[:, :T], rs[:, :T]), [rsb], [rsb])
        for kc in range(nkc):
            P.op("dve", lambda e, kc=kc: e.scalar_tensor_tensor(
                out[:, kc, :T], hT[:, kc, :T], g[:, kc:kc + 1], rs[:, :T], ALU.mult, ALU.mult),
                [hTb, gb, rsb], [outb])

    def wsrc(self, key, i, nelem, slv, src_fp32):
        P = self.P
        if not getattr(self, "wcache_on", False):
            return None
        ent = self.wcache.setdefault(key, {})
        if i not in ent:
            self.wc_n = getattr(self, "wc_n", 0) + 1
            d = self.nc.dram_tensor(f"wc{self.wc_n}", [128, nelem], BF16).ap()
            ent[i] = (d, P.buf(f"wc_{key}_{i}"))
            return ("first", ent[i])
        return ("cached", ent[i])

    def wload(self, key, i, sl, slb, nelem, slv, src):
        P = self.P
        st = self.wsrc(key, i, nelem, slv, src)
        if st is None:
            P.dma("pool", slv, src, writes=[slb])
        elif st[0] == "first":
            d, db = st[1]
            P.dma("pool", slv, src, writes=[slb])
            P.dma("pool", d, sl[:, 0:nelem], reads=[slb], writes=[db])
        else:
            d, db = st[1]
            P.dma("pool", sl[:, 0:nelem], d, reads=[db], writes=[slb])

    def fm_groups(self, K, ncols):
        kct = K // 128
        if kct <= 16:
            kcs, nkq = kct, 1
            cps = min(ncols, (self.wse // kct) // 128 * 128)
        else:
            kcs, nkq = 16, kct // 16
            cps = min(ncols, self.wse // 16)
        return kcs, nkq, [(g0, min(cps, ncols - g0), kq) for g0 in range(0, ncols, cps) for kq in range(nkq)]

    def tm_groups(self, K, ncols):
        kct = K // 128
        cps = min(ncols, self.wse // kct)
        return kct, [(g0, min(cps, ncols - g0)) for g0 in range(0, ncols, cps)]

    def precast_list(self, w, K, c0, ncols, mode):
        wv = w.rearrange("(kc p) n -> p kc n", p=128)
        out = []
        if mode == "fm":
            kcs, nkq, groups = self.fm_groups(K, ncols)
            for i, (g0, gc, kq) in enumerate(groups):
                out.append(((w.tensor.name, "fm", c0, ncols), i, kcs * gc, kcs,
                            wv[:, kq * 16: kq * 16 + kcs, c0 + g0: c0 + g0 + gc]))
        else:
            kct, groups = self.tm_groups(K, ncols)
            for i, (g0, gc) in enumerate(groups):
                out.append(((w.tensor.name, "tm", c0, ncols), i, kct * gc, kct, wv[:, :, c0 + g0: c0 + g0 + gc]))
        return out

    def precast_issue(self, item):
        key, i, nelem, a, src = item
        ent = self.wcache.setdefault(key, {})
        assert i not in ent
        self.wc_n = getattr(self, "wc_n", 0) + 1
        d = self.nc.dram_tensor(f"wc{self.wc_n}", [128, nelem], BF16).ap()
        db = self.P.buf(f"wcb{self.wc_n}")
        ent[i] = (d, db)
        self.P.dma("pool", d.rearrange("p (a b) -> p a b", a=a), src, writes=[db])

    def linear_fm(self, w, K, c0, ncols, xT, xTb, T, epilogue, banks):
        P = self.P
        wv = w.rearrange("(kc p) n -> p kc n", p=128)
        kcs, nkq, groups = self.fm_groups(K, ncols)
        brot = Rot(banks)
        loads = []

        def issue(i):
            g0, gc, kq = groups[i]
            sl, slb = self.wslots.next()
            slv = sl[:, 0:kcs * gc].rearrange("p (a b) -> p a b", a=kcs)
            self.wload((w.tensor.name, "fm", c0, ncols), i, sl, slb, kcs * gc, slv,
                       wv[:, kq * 16: kq * 16 + kcs, c0 + g0: c0 + g0 + gc])
            loads.append((slv, slb))
        for i in range(min(2, len(groups))):
            issue(i)
        gbanks = None
        for i, (g0, gc, kq) in enumerate(groups):
            if i + 2 < len(groups):
                issue(i + 2)
            slv, slb = loads[i]
            nch = gc // 128
            if nkq > 1 and kq == 0:
                gbanks = [brot.next() for _ in range(nch)]
            for ch in range(nch):
                pi = gbanks[ch] if nkq > 1 else brot.next()
                ps, psb = self.ps[pi], self.psb[pi]
                for kc in range(kcs):
                    P.op("pe", lambda e, ps=ps, slv=slv, kc=kc, ch=ch, kq=kq: e.matmul(
                        ps[:, :T], slv[:, kc, ch * 128:(ch + 1) * 128], xT[:, kq * 16 + kc, :T],
                        start=(kq == 0 and kc == 0), stop=(kq == nkq - 1 and kc == kcs - 1)),
                        [slb, xTb], [psb])
                if kq == nkq - 1:
                    epilogue(ps, psb, (g0 // 128) + ch)
        self.rope_flush()

    def linear_tm(self, w, K, c0, ncols, xT, xTb, ntiles, epilogue, banks, tiles=None):
        P = self.P
        wv = w.rearrange("(kc p) n -> p kc n", p=128)
        kct, groups = self.tm_groups(K, ncols)
        brot = Rot(banks)
        loads = []

        def issue(i):
            g0, gc = groups[i]
            sl, slb = self.wslots.next()
            slv = sl[:, 0:kct * gc].rearrange("p (a b) -> p a b", a=kct)
            self.wload((w.tensor.name, "tm", c0, ncols), i, sl, slb, kct * gc, slv, wv[:, :, c0 + g0: c0 + g0 + gc])
            loads.append((slv, slb))
        for i in range(min(2, len(groups))):
            issue(i)
        for i, (g0, gc) in enumerate(groups):
            if i + 2 < len(groups):
                issue(i + 2)
            slv, slb = loads[i]
            for tile in (tiles if tiles is not None else range(ntiles)):
                pi = brot.next()
                ps, psb = self.ps[pi], self.psb[pi]
                for kc in range(kct):
                    P.op("pe", lambda e, ps=ps, slv=slv, kc=kc, tile=tile, gc=gc: e.matmul(
                        ps[:, :gc], xT[:, kc, tile * 128:(tile + 1) * 128], slv[:, kc, :],
                        start=(kc == 0), stop=(kc == kct - 1)), [slb, xTb], [psb])
                epilogue(ps, psb, tile, g0, gc)

    def rope_head(self, ps, psb, cosT, sinT, tabbs, t0, T, dst_dram):
        P = self.P
        xb, xbb = self.xbf.next()
        P.op("act", lambda e: e.copy(xb[:, :T], ps[:, :T]), [psb], [xbb])

        def part_b():
            pi = self.ropeb
            ps2, ps2b = self.ps[pi], self.psb[pi]
            P.op("pe", lambda e: e.matmul(ps2[0:32, :T], self.pm[0:32, 0:32], xb[0:32, :T], start=True, stop=True),
                 [xbb, self.pmb], [ps2b])
            t1, t1b = self.rt1.next()
            t2, t2b = self.rt2.next()
            P.op("dve", lambda e: e.tensor_tensor(t1[0:32, :T], ps[0:32, :T], cosT[0:32, t0:t0 + T], ALU.mult), [psb] + tabbs, [t1b])
            P.op("dve", lambda e: e.tensor_tensor(t2[0:32, :T], ps2[0:32, :T], sinT[0:32, t0:t0 + T], ALU.mult), [ps2b] + tabbs, [t2b])
            P.op("dve", lambda e: e.tensor_tensor(xb[0:32, :T], t1[0:32, :T], t2[0:32, :T], ALU.add), [t1b, t2b, xbb], [xbb])
            self.store(dst_dram, xb[:, :T], xbb)
        prev = getattr(self, "_rope_pending", None)
        self._rope_pending = part_b
        if prev is not None:
            prev()

    def rope_flush(self):
        prev = getattr(self, "_rope_pending", None)
        self._rope_pending = None
        if prev is not None:
            prev()

    def common_consts(self):
        self.ident, self.identb = self.load_const("ident", [128, 128])
        self.identbf, self.identbfb = self.load_const("identbf", [128, 128], cast=True)
        self.ones, self.onesb = self.load_const("ones", [128, 128], cast=True)
        self.pm, self.pmb = self.load_const("pm", [32, 32], cast=True)

    def finish(self):
        self.P.finalize(final_waits=self.out_dmas)
        self.P.emit()
        return self.nc


def host_consts():
    pm = np.zeros((32, 32), np.float32)
    for m in range(16):
        pm[m + 16, m] = -1.0
        pm[m, m + 16] = 1.0
    return dict(ident=np.eye(128, dtype=np.float32), identbf=np.eye(128, dtype=np.float32),
                ones=np.ones((128, 128), np.float32), pm=pm)


def rope_tables(pos):
    half = 16
    inv = (500000.0 ** (-np.arange(half, dtype=np.float32) * np.float32(2.0 / 32))).astype(np.float32)
    ang = pos.astype(np.float32)[None, :] * inv[:, None]
    c = np.cos(ang).astype(np.float32)
    s = np.sin(ang).astype(np.float32)
    return np.concatenate([c, c], 0), np.concatenate([s, s], 0)


def local_positions(r):
    n = np.arange(NT)[:, None]
    l = np.arange(128)[None, :]
    return ((4 * n + r) * 128 + l).reshape(-1)


def gain_layout(g):
    return np.ascontiguousarray(g.reshape(KC, 128).T)


def build_p1(nb=NB):
    k = KB()
    nc, P = k.nc, k.P
    x = k.din("x", [TL, D])
    w_in = k.din("w_in", [D, 3 * HA * DH])
    qT_o = k.dout("qT", [HA, DH, TL], BF16)
    kT_o = k.dout("kT", [HA, DH, TL], BF16)
    v_o = k.dout("v", [TL, HA * DH], BF16)
    k.common_consts()
    g, gb = k.load_const("g_attn", [128, KC])
    cosT, tabb = k.load_const("cosT", [32, TL])
    sinT, tabb2 = k.load_const("sinT", [32, TL])
    hT, hTb = k.sb("hT", [128, KC, TB], F32)
    hnT, hnTb = k.sb("hnT", [128, KC, TB], BF16)
    stage = Rot([k.sb(f"stage{i}", [128, D], F32) for i in range(2)])
    k.sq = Rot([k.sb(f"sq{i}", [128, TB], BF16) for i in range(2)])
    k.rstd = k.sb("rstd", [128, TB], F32)
    k.xbf = Rot([k.sb(f"xbf{i}", [128, TB], BF16) for i in range(3)])
    k.rt1 = Rot([k.sb(f"rt1_{i}", [32, TB], F32) for i in range(2)])
    k.rt2 = Rot([k.sb(f"rt2_{i}", [32, TB], F32) for i in range(2)])
    vsb = Rot([k.sb(f"vsb{i}", [128, 512], BF16) for i in range(3)])
    k.trb = Rot([6, 7])
    k.statb = 5
    k.ropeb = 4
    mmb = [0, 1, 2, 3]
    tabs = [tabb, tabb2]
    for tb in range(nb):
        t0 = tb * TB
        k.load_transpose(x, t0, KC, hT, hTb, k.ident, k.identb, stage)
        k.rmsnorm(hT, hTb, g, gb, hnT, hnTb)

        def ep_qk(ps, psb, ch, t0=t0):
            head = ch
            dst = (qT_o if head < HA else kT_o)[head % HA, :, t0:t0 + TB]
            k.rope_head(ps, psb, cosT, sinT, tabs, t0, TB, dst)
        k.linear_fm(w_in, D, 0, 2 * HA * DH, hnT, hnTb, TB, ep_qk, mmb)

        def ep_v(ps, psb, tile, c0, gc, t0=t0):
            vs, vsbuf = vsb.next()
            P.op("act", lambda e: e.copy(vs[:, :gc], ps[:, :gc]), [psb], [vsbuf])
            k.store(v_o[t0 + tile * 128: t0 + (tile + 1) * 128, c0:c0 + gc], vs[:, :gc], vsbuf)
        k.linear_tm(w_in, D, 2 * HA * DH, HA * DH, hnT, hnTb, TB // 128, ep_v, mmb)
    return k.finish()


def masks_A():
    lk = np.arange(128)[:, None]
    lq = np.arange(128)[None, :]
    out = []
    for dil in (1, 4, 16):
        res = ((lq - lk) % dil) == 0
        for kind in ("first", "mid", "last"):
            if kind == "first":
                ok = res & (lq >= lk)
            elif kind == "mid":
                ok = res
            else:
                ok = res & (lq <= lk)
            out.append(np.where(ok, 0.0, NEG).astype(np.float32))
    return np.stack(out)


def rel_arrange(arrs, r, tok_axis, pad_value=0.0):
    outs = []
    for rho in range(4):
        src = arrs[(r - rho) % 4]
        shift = 1 if r >= rho else 2
        shp = list(src.shape)
        shp[tok_axis] = 17 * 128
        o = np.zeros(shp, src.dtype)
        ntok = (17 - shift) * 128
        sl_o = [slice(None)] * len(shp)
        sl_i = [slice(None)] * len(shp)
        sl_o[tok_axis] = slice(shift * 128, 17 * 128)
        sl_i[tok_axis] = slice(0, ntok)
        o[tuple(sl_o)] = src[tuple(sl_i)]
        outs.append(o)
    return np.stack(outs)


class AttnRes:
    pass


class Region:
    def __init__(self, k, big, nelems):
        self.k, self.big, self.n, self.off = k, big, nelems, 0

    def take(self, name, shape, dt=BF16):
        ne = int(np.prod(shape[1:])) * (2 if dt == F32 else 1)
        assert self.off + ne <= self.n, (name, self.off, ne, self.n)
        ap = self.big[:, self.off:self.off + ne]
        self.off += ne
        if dt == F32:
            ap = ap.bitcast(F32)
        if len(shape) == 3:
            ap = ap.rearrange("p (a b) -> p a b", a=shape[1])
        elif len(shape) == 4:
            ap = ap.rearrange("p (a b c) -> p a b c", a=shape[1], b=shape[2])
        return ap, self.k.P.buf(name)


def fence(k, bufs):
    k.P.op("dve", lambda e: e.memset(k.fence_t[0][:, :], 0.0), [], list(bufs) + [k.fence_t[1]])


def attn_setup(k, reg, ntile_slot, nq_heads):
    a = AttnRes()
    a.kslots = Rot([reg.take(f"kslot{i}", [128, 4, ntile_slot * 128]) for i in range(3)])
    a.vslots = Rot([reg.take(f"vslot{i}", [128, ntile_slot, 4, 129]) for i in range(3)])
    a.pT = Rot([k.sb(f"pT{i}", [128, 512], BF16) for i in range(3)])
    a.qs = Rot([reg.take(f"qs{i}", [128, nq_heads, 128]) for i in range(2)])
    a.region_bufs = [b for _, b in a.kslots.items + a.vslots.items + a.qs.items]
    a.rz = k.sb("rz", [128, 4], F32)
    a.otok = Rot([k.sb(f"otok{i}", [128, 4, 128], BF16) for i in range(2)])
    a.sbanks = Rot([0, 1])
    a.acc = (4, 5)
    return a


class Pipe:
    def __init__(self, depth=2, nslots=3, issue=None, nchunks=0):
        self.depth, self.nslots, self.issue_fn, self.nchunks = depth, nslots, issue, nchunks
        self.pending = []
        self.next_issue = 0
        self.open_steps = {}

    def _can_issue(self, j):
        old = j - self.nslots
        return old < 0 or self.open_steps.get(old, 0) == 0

    def _issue_ahead(self, upto):
        while self.next_issue < self.nchunks and self.next_issue <= upto and self._can_issue(self.next_issue):
            self.issue_fn(self.next_issue)
            self.next_issue += 1

    def begin_chunk(self, j):
        while self.next_issue <= j:
            if self._can_issue(self.next_issue):
                self.issue_fn(self.next_issue)
                self.next_issue += 1
            else:
                self._flush_one()
        self._issue_ahead(j + self.nslots - 1)

    def _flush_one(self):
        pv, first_hook, last_hook, chunk = self.pending.pop(0)
        if first_hook is not None:
            first_hook()
        pv()
        if last_hook is not None:
            last_hook()
        if chunk is not None:
            self.open_steps[chunk] -= 1

    def flush(self):
        while self.pending:
            self._flush_one()

    def step(self, score_fn, first_hook=None, last_hook=None, chunk=None):
        pv = score_fn()
        if chunk is not None:
            self.open_steps[chunk] = self.open_steps.get(chunk, 0) + 1
        self.pending.append((pv, first_hook, last_hook, chunk))
        while len(self.pending) > self.depth:
            self._flush_one()
        if chunk is not None and self.issue_fn is not None:
            self._issue_ahead(chunk + self.nslots - 1)


def attn_step(k, a, kslot, kslotb, s, shared_k, qs, qsb, qheads, vslot, vslotb, mask_mm, extra=None, vsel=None, defer=False):
    P = k.P
    pi = a.sbanks.next()
    ps, psb = k.ps[pi], k.psb[pi]
    last = mask_mm is None
    if shared_k:
        q0 = qheads[0]
        P.op("pe", lambda e: e.matmul(ps[:, :], kslot[:, s * 128:(s + 1) * 128],
                                      qs[:, q0:q0 + 4, :].rearrange("p a b -> p (a b)"),
                                      start=True, stop=last, skip_group_check=True), [kslotb, qsb], [psb])
    else:
        for h in range(4):
            P.op("pe", lambda e, h=h: e.matmul(ps[:, h * 128:(h + 1) * 128], kslot[:, h, s * 128:(s + 1) * 128],
                                               qs[:, qheads[h], :], start=(h == 0), stop=(last and h == 3),
                                               skip_group_check=True), [kslotb, qsb], [psb])
    if mask_mm is not None:
        lt, rh, mb = mask_mm
        P.op("pe", lambda e: e.matmul(ps[:, :].rearrange("p (a b) -> p a b", a=4), lt, rh, start=False, stop=True,
                                      skip_group_check=True), list(mb), [psb])
    pT, pTb = a.pT.next()
    P.op("act", lambda e: e.activation(pT[:, :], ps[:, :], AF.Exp, scale=SCALE), [psb], [pTb])

    def pv_part():
        for h in range(4):
            acc = k.ps[a.acc[h // 2]]
            accb = k.psb[a.acc[h // 2]]
            vap = vslot[:, s, h, :] if vsel is None else vsel(h)
            P.op("pe", lambda e, h=h, acc=acc, vap=vap: e.matmul(acc[:, (h % 2) * 129:(h % 2) * 129 + 129],
                                                                 pT[:, h * 128:(h + 1) * 128], vap,
                                                                 start=False, stop=False, skip_group_check=True),
                 [pTb, vslotb], [accb])
        if extra is not None:
            extra(pT, pTb)
    if defer:
        return pv_part
    pv_part()


def attn_zero_acc(k, a):
    for pi in a.acc:
        k.P.op("dve", lambda e, pi=pi: e.memset(k.ps[pi][:, 0:258], 0.0), [], [k.psb[pi]])


def attn_rz(k, a, tiny=None):
    P = k.P
    rz, rzb = a.rz
    for j, pi in enumerate(a.acc):
        zin = k.ps[pi][:, 0:258].rearrange("p (a b) -> p a b", a=2)[:, :, 128]
        if tiny is None:
            P.op("dve", lambda e, zin=zin, j=j: e.reciprocal(rz[:, 2 * j:2 * j + 2], zin), [k.psb[pi]], [rzb])
        else:
            P.op("dve", lambda e, zin=zin, j=j: e.tensor_scalar_max(rz[:, 2 * j:2 * j + 2], zin, tiny), [k.psb[pi]], [rzb])
            P.op("dve", lambda e, j=j: e.reciprocal(rz[:, 2 * j:2 * j + 2], rz[:, 2 * j:2 * j + 2]), [rzb], [rzb])
    return rz, rzb


def transpose_heads(k, a, otok, otokb, dst_ap_fn, dstb, bank):
    P = k.P
    ps, psb = k.ps[bank], k.psb[bank]
    psv = ps
    for h in range(4):
        P.op("pe", lambda e, h=h: e.transpose(psv[:, h * 128:(h + 1) * 128], otok[:, h, :], k.identbf[:]),
             [otokb, k.identbfb], [psb])
    P.op("act", lambda e: e.copy(dst_ap_fn(), psv[:, 0:512].rearrange("p (a b) -> p a b", a=4)), [psb], [dstb])


def mlp_ple(k, li, hT, hTb, hnT, hnTb, big, bigb, gains, w1, w2, wgate, wproj, p_in, t0, stage_rot, mmb, allreg):
    P = k.P
    g_mlp, g_mlpb, g_ple, g_pleb = gains
    h1T = big[:, 0:64 * TB].rearrange("p (a b) -> p a b", a=64)
    k.rmsnorm(hT, hTb, g_mlp, g_mlpb, hnT, hnTb)

    def ep_w1(ps, psb, ch):
        r, rb = k.relu_t.next()
        P.op("act", lambda e: e.activation(r[:, :], ps[:, :], AF.Relu), [psb], [rb])
        P.op("dve", lambda e: e.tensor_tensor(h1T[:, ch, :], r[:, :], r[:, :], ALU.mult), [rb], [bigb])
    k.linear_fm(w1, D, 0, DFF, hnT, hnTb, TB, ep_w1, mmb)

    def ep_add(ps, psb, ch):
        P.op("dve", lambda e: e.tensor_tensor(hT[:, ch, :], hT[:, ch, :], ps[:, :], ALU.add), [psb, hTb], [hTb])
    k.linear_fm(w2, DFF, 0, D, h1T, bigb, TB, ep_add, mmb)
    fence(k, allreg)
    k.rmsnorm(hT, hTb, g_ple, g_pleb, hnT, hnTb)
    gateT = big[:, 0:KC * TB].rearrange("p (a b) -> p a b", a=KC)

    def ep_gate(ps, psb, ch):
        P.op("act", lambda e: e.activation(gateT[:, ch, :], ps[:, :], AF.Sigmoid), [psb], [bigb])
    k.linear_fm(wgate, D, 0, D, hnT, hnTb, TB, ep_gate, mmb)
    pT, pTb = k.pT_ple
    k.load_transpose(p_in, t0, 2, pT, pTb, k.ident, k.identb, stage_rot)

    def ep_proj(ps, psb, ch):
        r, rb = k.relu_t.next()
        P.op("dve", lambda e: e.tensor_tensor(r[:, :], ps[:, :], gateT[:, ch, :], ALU.mult), [psb, bigb], [rb])
        P.op("dve", lambda e: e.tensor_tensor(hT[:, ch, :], hT[:, ch, :], r[:, :], ALU.add), [rb, hTb], [hTb])
    k.linear_fm(wproj, DPLE, 0, D, pT, pTb, TB, ep_proj, mmb)


def common_scratch(k):
    k.sq = Rot([k.sb(f"sq{i}", [128, TB], BF16) for i in range(4)])
    k.rstd = k.sb("rstd", [128, TB], F32)
    k.xbf = Rot([k.sb(f"xbf{i}", [128, TB], BF16) for i in range(3)])
    k.rt1 = Rot([k.sb(f"rt1_{i}", [32, TB], F32) for i in range(1)])
    k.rt2 = Rot([k.sb(f"rt2_{i}", [32, TB], F32) for i in range(1)])
    k.relu_t = Rot([k.sb(f"relu{i}", [128, TB], F32) for i in range(2)])
    k.fence_t = k.sb("fence_t", [128, 2], F32)
    k.pT_ple = k.sb("pT_ple", [128, 2, TB], BF16)
    k.trb = Rot([7])
    k.statb = 7
    k.ropeb = 7


def build_p2(nb=NB, do_attn=True, do_mlp=True, do_proj=True):
    k = KB(bf16_bank=6)
    nc, P = k.nc, k.P
    x = k.din("x", [TL, D])
    p_in = k.din("p0", [TL, DPLE])
    qT_in = k.din("qT", [HA, DH, TL], BF16)
    kT_rel = k.din("kT_rel", [4, HA, DH, 17 * 128], BF16)
    v1_rel = k.din("v1_rel", [4, 17 * 128, HA, 129], BF16)
    w_out = k.din("a_w_out", [8 * DH, D])
    w1 = k.din("mlp_w1", [D, DFF])
    w2 = k.din("mlp_w2", [DFF, D])
    wgate = k.din("ple_w_gate", [D, D])
    wproj = k.din("ple_w_proj", [DPLE, D])
    w_qg = k.din("b_w_qg", [D, 2096])
    w_kv = k.din("w_kv", [D, 3072])
    cos_d = k.din("cosT", [32, TL])
    sin_d = k.din("sinT", [32, TL])
    h_o = k.dout("h_out", [KC, 128, TL], F32)
    qTB_o = k.dout("qTB", [16, DH, TL], BF16)
    gates_o = k.dout("gates", [TL, 48], F32)
    kvT_o = k.dout("kvT", [4, 4, DH, TL], BF16)
    vtm_o = k.dout("vtm", [2, TL, 512], BF16)
    k.common_consts()
    g_mlp, g_mlpb = k.load_const("g_mlp", [128, KC])
    g_ple, g_pleb = k.load_const("g_ple", [128, KC])
    g_attn1, g_attn1b = k.load_const("g_attn1", [128, KC])
    g_kv, g_kvb = k.load_const("g_kv", [128, KC])
    maskA, maskAb = k.load_const("maskA", [128, 9, 128], cast=True)
    hT, hTb = k.sb("hT", [128, KC, TB], F32)
    hnT, hnTb = k.sb("hnT", [128, KC, TB], BF16)
    big, bigb = k.sb("big", [128, 64 * TB], BF16)
    oT, oTb = k.sb("oT", [128, 8, TB], BF16)
    cosT, cosTb = k.sb("cosT_sb", [32, TB], F32)
    sinT, sinTb = k.sb("sinT_sb", [32, TB], F32)
    gsb = Rot([k.sb(f"gsb{i}", [128, 48], F32) for i in range(2)])
    vsb = Rot([k.sb(f"vsb{i}", [128, 512], BF16) for i in range(3)])
    common_scratch(k)
    reg = Region(k, big, 64 * TB)
    a = attn_setup(k, reg, 5, HA)
    reg.off = 24576
    stage = Rot([reg.take(f"stage{i}", [128, D], F32) for i in range(2)])
    allreg = a.region_bufs + [b for _, b in stage.items] + [bigb]
    mmb = [0, 1, 2, 3]
    gains = (g_mlp, g_mlpb, g_ple, g_pleb)
    for tb in range(nb):
        t0 = tb * TB
        fence(k, allreg)
        k.load_transpose(x, t0, KC, hT, hTb, k.ident, k.identb, stage)
        if do_attn:
            for qt in range(4):
                n = tb * 4 + qt
                qs, qsb = a.qs.next()
                P.dma("sp", qs[:, :, :], qT_in[:, :, n * 128:(n + 1) * 128].rearrange("h d t -> d h t"), writes=[qsb])
                for quad in range(2):
                    attn_zero_acc(k, a)
                    for g, (dil, nk) in enumerate(GROUPS_A):
                        for rho in range(4):
                            offs = [j for j in range(nk + 1) if j % 4 == rho and n - j // 4 >= 0]
                            if not offs:
                                continue
                            a_lo, a_hi = offs[0] // 4, offs[-1] // 4
                            m_lo, m_hi = n - a_hi + 1, n - a_lo + 1
                            cnt = m_hi - m_lo + 1
                            h0 = g * 8 + quad * 4
                            ks, ksb = a.kslots.next()
                            vs, vsb_ = a.vslots.next()
                            P.dma("sp", ks[:, :, 0:cnt * 128],
                                  kT_rel[rho, h0:h0 + 4, :, m_lo * 128:(m_hi + 1) * 128].rearrange("h d t -> d h t"),
                                  writes=[ksb])
                            P.dma("act", vs[:, 0:cnt, :, :],
                                  v1_rel[rho, m_lo * 128:(m_hi + 1) * 128, h0:h0 + 4, :].rearrange("(m l) h c -> l m h c", l=128),
                                  writes=[vsb_])
                            for j in offs:
                                aa = j // 4
                                s = (n - aa + 1) - m_lo
                                kind = 0 if j == 0 else (2 if j == nk else 1)
                                mt = maskA[:, g * 3 + kind, :].unsqueeze(1).to_broadcast([128, 4, 128])
                                attn_step(k, a, ks, ksb, s, False, qs, qsb, [h0 + h for h in range(4)], vs, vsb_,
                                          (k.identbf[:], mt, [k.identbfb, maskAb]))
                    rz, rzb = attn_rz(k, a)
                    if DEBUG and n == 0 and quad == 0:
                        dbg, dbgb = k.sb("dbg", [128, 258], F32)
                        P.op("dve", lambda e: e.tensor_copy(dbg[:, :], k.ps[4][:, 0:258]), [k.psb[4]], [dbgb])
                        k.dump("dbg_acc", dbg[:, :], dbgb, [128, 258])
                        k.dump("dbg_rz", rz[:, :], rzb, [128, 4])
                    ot, otb = a.otok.next()
                    for h in range(4):
                        acc = k.ps[a.acc[h // 2]]
                        P.op("dve", lambda e, h=h, acc=acc, ot=ot, rz=rz: e.tensor_scalar(
                            ot[:, h, :], acc[:, (h % 2) * 129:(h % 2) * 129 + 128], rz[:, h:h + 1], None, ALU.mult),
                            [k.psb[a.acc[h // 2]], rzb], [otb])
                    if DEBUG and n == 0 and quad == 0:
                        k.dump("dbg_ot", ot[:, :, :], otb, [128, 4, 128], BF16)
                    transpose_heads(k, a, ot, otb,
                                    lambda quad=quad, qt=qt: oT[:, quad * 4:quad * 4 + 4, qt * 128:(qt + 1) * 128], oTb, 6)

            def ep_add(ps, psb, ch):
                P.op("dve", lambda e: e.tensor_tensor(hT[:, ch, :], hT[:, ch, :], ps[:, :], ALU.add), [psb, hTb], [hTb])
            k.linear_fm(w_out, 8 * DH, 0, D, oT, oTb, TB, ep_add, mmb)
        if do_mlp:
            fence(k, allreg)
            mlp_ple(k, 0, hT, hTb, hnT, hnTb, big, bigb, gains, w1, w2, wgate, wproj, p_in, t0, stage, mmb, allreg)
        k.store(h_o[:, :, t0:t0 + TB].rearrange("k p t -> p k t"), hT[:, :, :], hTb)
        if do_proj:
            P.dma("sp", cosT[:, :], cos_d[:, t0:t0 + TB], writes=[cosTb])
            P.dma("sp", sinT[:, :], sin_d[:, t0:t0 + TB], writes=[sinTb])
            tabs = [cosTb, sinTb]
            k.rmsnorm(hT, hTb, g_attn1, g_attn1b, hnT, hnTb)

            def ep_q(ps, psb, ch, t0=t0):
                k.rope_head(ps, psb, cosT, sinT, tabs, 0, TB, qTB_o[ch, :, t0:t0 + TB])
            k.linear_fm(w_qg, D, 0, 16 * DH, hnT, hnTb, TB, ep_q, mmb)

            def ep_g(ps, psb, tile, c0, gc, t0=t0):
                gs, gsbuf = gsb.next()
                P.op("act", lambda e: e.activation(gs[:, :gc], ps[:, :gc], AF.Sigmoid), [psb], [gsbuf])
                k.store(gates_o[t0 + tile * 128:t0 + (tile + 1) * 128, :], gs[:, :gc], gsbuf)
            k.linear_tm(w_qg, D, 16 * DH, 48, hnT, hnTb, TB // 128, ep_g, mmb)
            k.rmsnorm(hT, hTb, g_kv, g_kvb, hnT, hnTb)

            def mk_ep_fm(slot, rope, t0=t0):
                def ep(ps, psb, ch):
                    dst = kvT_o[slot, ch, :, t0:t0 + TB]
                    if rope:
                        k.rope_head(ps, psb, cosT, sinT, tabs, 0, TB, dst)
                    else:
                        xb, xbb = k.xbf.next()
                        P.op("act", lambda e: e.copy(xb[:, :], ps[:, :]), [psb], [xbb])
                        k.store(dst, xb[:, :], xbb)
                return ep

            def mk_ep_tm(slot, t0=t0):
                def ep(ps, psb, tile, c0, gc):
                    vs, vsbuf = vsb.next()
                    P.op("act", lambda e: e.copy(vs[:, :gc], ps[:, :gc]), [psb], [vsbuf])
                    k.store(vtm_o[slot, t0 + tile * 128:t0 + (tile + 1) * 128, c0:c0 + gc], vs[:, :gc], vsbuf)
                return ep
            k.linear_fm(w_kv, D, 0, 512, hnT, hnTb, TB, mk_ep_fm(0, False), mmb)
            k.linear_fm(w_kv, D, 512, 512, hnT, hnTb, TB, mk_ep_fm(1, False), mmb)
            k.linear_fm(w_kv, D, 1024, 512, hnT, hnTb, TB, mk_ep_fm(2, True), mmb)
            k.linear_tm(w_kv, D, 1536, 512, hnT, hnTb, TB // 128, mk_ep_tm(0), mmb)
            k.linear_fm(w_kv, D, 2048, 512, hnT, hnTb, TB, mk_ep_fm(3, True), mmb)
            k.linear_tm(w_kv, D, 2560, 512, hnT, hnTb, TB // 128, mk_ep_tm(1), mmb)
    return k.finish()


def nsa_host_consts(r):
    n_cmp, n_slc = 511, 128
    j_of_jj = np.full(128, -1, np.int64)
    for rho in range(4):
        for m in range(1, 17):
            for half in range(2):
                j = 8 * (m - 1) + 2 * (r - rho) + half
                if 0 <= j < n_slc:
                    j_of_jj[32 * rho + 2 * (m - 1) + half] = j
    c = np.arange(512)
    cmp_start = c * 16
    ov = np.zeros((512, 128), np.float32)
    for jj in range(128):
        j = j_of_jj[jj]
        if j < 0:
            continue
        o = np.clip(np.minimum(cmp_start + 32, j * 64 + 64) - np.maximum(cmp_start, j * 64), 0, None) / 32.0
        ov[:, jj] = o
    ov[511] = 0.0
    ov_perm = ov.reshape(4, 128, 128).transpose(1, 0, 2)
    scoreM = np.zeros((NT, 128, 128), np.float32)
    scoreA = np.zeros((NT, 128, 128), np.float32)
    cmpmask = np.zeros((NT, 128, 4, 128), np.float32)
    l = np.arange(128)
    for n in range(NT):
        t = (4 * n + r) * 128 + l
        cur = t // 64
        for jj in range(128):
            j = j_of_jj[jj]
            if j < 0:
                scoreA[n, :, jj] = -1e30
                continue
            forced = (j == 0) | (j == cur) | (j == cur - 1)
            causal = (j * 64 <= t)
            scoreM[n, :, jj] = np.where(forced | ~causal, 0.0, 1.0)
            scoreA[n, :, jj] = np.where(forced, 1e9, np.where(causal, 0.0, -1e30))
        cend = (np.arange(512) * 16 + 31).reshape(4, 128)
        valid = cend.T[:, :, None] <= t[None, None, :]
        valid[127, 3, :] = False
        cmpmask[n] = np.where(valid, 0.0, NEG)
    lk = np.arange(128)[:, None]
    lq = np.arange(128)[None, :]
    maskB = np.stack([np.where(lq >= lk, 0.0, NEG), np.where(lq < lk, 0.0, NEG)]).astype(np.float32)
    estd = (np.arange(8192)[None, :] // 64 == np.arange(128)[:, None]).astype(np.float32)
    cpos = np.arange(512) * 16 + 31
    cosC, sinC = rope_tables(cpos)
    bf = ml_dtypes.bfloat16
    return dict(ov_perm=np.ascontiguousarray(ov_perm).astype(bf), scoreM=scoreM, scoreA=scoreA,
                cmpmask=cmpmask.astype(bf), maskB=np.ascontiguousarray(maskB.transpose(1, 0, 2)),
                estd=estd.astype(bf), cosC=cosC, sinC=sinC)


def gelu_tanh(k, out_ap, outb, ps, psb, bias_ap, biasb, n):
    P = k.P
    x, xb = k.g_x
    u, ub = k.g_u
    P.op("act", lambda e: e.activation(x[:, :n], ps[:, :n], AF.Identity, bias=bias_ap), [psb, biasb], [xb])
    P.op("act", lambda e: e.activation(u[:, :n], x[:, :n], AF.Square), [xb], [ub])
    P.op("dve", lambda e: e.tensor_scalar(u[:, :n], u[:, :n], 0.044715, 1.0, ALU.mult, ALU.add), [ub], [ub])
    P.op("dve", lambda e: e.tensor_tensor(u[:, :n], u[:, :n], x[:, :n], ALU.mult), [ub, xb], [ub])
    P.op("act", lambda e: e.activation(u[:, :n], u[:, :n], AF.Sigmoid, scale=1.5957691216057308), [ub], [ub])
    P.op("dve", lambda e: e.tensor_tensor(out_ap, u[:, :n], x[:, :n], ALU.mult), [ub, xb], [outb])


def build_p3(nb=NB, do_mlp=True, do_attn=True):
    k = KB(bf16_bank=6, wslot_elems=4096)
    nc, P = k.nc, k.P
    h_in = k.din("h_in", [KC, 128, TL])
    p_in = k.din("p1", [TL, DPLE])
    qT_in = k.din("qTB", [16, DH, TL], BF16)
    gates_in = k.din("gates", [TL, 48])
    rawT = k.din("rawT", [2, 4, DH, S], BF16)
    kslc_rel = k.din("kslc_rel", [4, 4, DH, 17 * 128], BF16)
    vslc_rel = k.din("vslc_rel", [4, 17 * 128, 4, 129], BF16)
    kwin_rel = k.din("kwin_rel", [4, 4, DH, 17 * 128], BF16)
    vwin_rel = k.din("vwin_rel", [4, 17 * 128, 4, 129], BF16)
    cw1 = [k.din("cmp_w1_k", [32 * DH, 256]), k.din("cmp_w1_v", [32 * DH, 256])]
    cw2 = [k.din("cmp_w2_k", [256, DH]), k.din("cmp_w2_v", [256, DH])]
    cpe = [k.din("cmp_pe_k", [32, DH]), k.din("cmp_pe_v", [32, DH])]
    w_out = k.din("b_w_out", [D, D])
    w1 = k.din("mlp_w1", [D, DFF])
    w2 = k.din("mlp_w2", [DFF, D])
    wgate = k.din("ple_w_gate", [D, D])
    wproj = k.din("ple_w_proj", [DPLE, D])
    scoreM_d = k.din("scoreM", [NT, 128, 128])
    scoreA_d = k.din("scoreA", [NT, 128, 128])
    cmpmask_d = k.din("cmpmask", [NT, 128, 4, 128], BF16)
    estd_d = k.din("estd", [128, 8192], BF16)
    ovp_d = k.din("ov_perm", [128, 4, 128], BF16)
    out_o = k.dout("out", [TL, D], F32)
    k.common_consts()
    g_mlp, g_mlpb = k.load_const("g_mlp", [128, KC])
    g_ple, g_pleb = k.load_const("g_ple", [128, KC])
    g_fin, g_finb = k.load_const("g_fin", [128, KC])
    maskB, maskBb = k.load_const("maskB", [128, 2, 128], cast=True)
    cosC, cosCb = k.load_const("cosC", [32, 512])
    sinC, sinCb = k.load_const("sinC", [32, 512])
    estd, estdb = k.sb("estd_sb", [128, 8192], BF16)
    P.dma("sp", estd[:, :], estd_d, writes=[estdb])
    ovp, ovpb = k.sb("ovp_sb", [128, 4, 128], BF16)
    P.dma("sp", ovp[:, :, :], ovp_d, writes=[ovpb])
    hT, hTb = k.sb("hT", [128, KC, TB], F32)
    hnT, hnTb = k.sb("hnT", [128, KC, TB], BF16)
    oT, oTb = hnT, hnTb
    big, bigb = k.sb("big", [128, 64 * TB], BF16)
    kcmpT, kcmpTb = k.sb("kcmpT", [128, 4, 512], BF16)
    vcmp1, vcmp1b = k.sb("vcmp1", [128, 4, 4, 129], BF16)
    common_scratch(k)
    k.g_x = k.sb("g_x", [128, 512], F32)
    k.g_u = k.sb("g_u", [128, 512], F32)
    reg = Region(k, big, 64 * TB)
    a = AttnRes()
    a.kslots = Rot([reg.take(f"kslot{i}", [128, 17 * 128]) for i in range(3)])
    a.vslots = Rot([reg.take(f"vslot{i}", [128, 17, 129]) for i in range(3)])
    a.qs = Rot([reg.take(f"qs{i}", [128, 16, 128]) for i in range(2)])
    a.pT = Rot([k.sb(f"pT{i}", [128, 512], BF16) for i in range(3)])
    a.rz = k.sb("rz", [128, 4], F32)
    a.otok = Rot([k.sb(f"otok{i}", [128, 4, 128], BF16) for i in range(2)])
    a.sbanks = Rot([0, 1])
    a.acc = (4, 5)
    a.region_bufs = [b for _, b in a.kslots.items + a.vslots.items + a.qs.items]
    assert reg.off <= 24576, reg.off
    reg.off = 24576
    stage = Rot([reg.take(f"stage{i}", [128, D], F32) for i in range(2)])
    reg2 = Region(k, big, 64 * TB)
    w1sb, w1sbb = reg2.take("w1sb", [128, 32, 256])
    rawsb = Rot([reg2.take(f"rawsb{i}", [128, S]) for i in range(2)])
    hidT, hidTb = k.sb("hidT", [128, 2, 512], BF16)
    w2sb, w2sbb = k.sb("w2sb", [128, 2, 128], BF16)
    peT, peTb = k.sb("peT", [128, 32], BF16)
    biasT, biasTb = k.sb("biasT", [128, 2], F32)
    allreg = a.region_bufs + [b for _, b in stage.items] + [bigb, w1sbb] + [b for _, b in rawsb.items]
    mmb = [0, 1, 2, 3]
    gains = (g_mlp, g_mlpb, g_ple, g_pleb)
    gsc, gscb = k.sb("gates_sb", [128, 48], F32)
    sM, sMb = k.sb("sM", [128, 128], F32)
    sA, sAb = k.sb("sA", [128, 128], F32)
    cmk, cmkb = k.sb("cmk", [128, 4, 128], BF16)
    imp, impb = k.sb("imp", [128, 128], F32)
    scr, scrb = k.sb("scr", [128, 128], F32)
    m8, m8b = k.sb("m8", [128, 16], F32)
    selb, selbb = k.sb("selb", [128, 128], BF16)
    selT, selTb = k.sb("selT", [128, 128], BF16)
    otot, ototb = k.sb("otot", [128, 4, 128], F32)
    sc4, sc4b = k.sb("sc4", [128, 4], F32)
    tabsC = [cosCb, sinCb]

    P.op("dve", lambda e: e.memset(vcmp1[:, :, :, 128:129], 1.0), [], [vcmp1b])
    P.op("dve", lambda e: e.memset(hidT[:, :, :], 0.0), [], [hidTb])
    P.op("dve", lambda e: e.memset(kcmpT[:, :, :], 0.0), [], [kcmpTb])
    for kv in range(2):
        fence(k, [w1sbb, w2sbb, peTb])
        P.dma("pool", w1sb[:, :, :], cw1[kv].rearrange("(l d) h -> d l h", d=128), writes=[w1sbb])
        P.dma("pool", w2sb[:, :, :], cw2[kv].rearrange("(c p) d -> p c d", p=128), writes=[w2sbb])
        P.op("pool", lambda e, kv=kv: e.dma_start(out=peT[:, :], in_=cpe[kv].rearrange("l d -> d l"),
                                                  allow_slow_non_contiguous=True), [], [peTb], is_dma=True)
        ps7, ps7b = k.ps[7], k.psb[7]
        for hc in range(2):
            for l in range(32):
                P.op("pe", lambda e, hc=hc, l=l: e.matmul(ps7[:, hc:hc + 1], w1sb[:, l, hc * 128:(hc + 1) * 128],
                                                          peT[:, l:l + 1], start=(l == 0), stop=(l == 31),
                                                          skip_group_check=True), [w1sbb, peTb], [ps7b])
        P.op("dve", lambda e: e.tensor_copy(biasT[:, :], ps7[:, 0:2]), [ps7b], [biasTb])
        for g in range(4):
            rs, rsb = rawsb.next()
            P.dma("sp", rs[:, :], rawT[kv, g, :, :], writes=[rsb])
            for hc in range(2):
                pi = mmb[hc]
                ps, psb = k.ps[pi], k.psb[pi]
                for l in range(32):
                    P.op("pe", lambda e, ps=ps, rs=rs, hc=hc, l=l: e.matmul(
                        ps[:, 0:511], w1sb[:, l, hc * 128:(hc + 1) * 128], rs[:, l:l + 16 * 510 + 1:16],
                        start=(l == 0), stop=(l == 31)), [w1sbb, rsb], [psb])
                gelu_tanh(k, hidT[:, hc, 0:511], hidTb, ps, psb, biasT[:, hc:hc + 1], biasTb, 511)
            if kv == 0:
                ps, psb = k.ps[2], k.psb[2]
                for hc in range(2):
                    P.op("pe", lambda e, ps=ps, hc=hc: e.matmul(ps[:, 0:512], w2sb[:, hc, :], hidT[:, hc, :],
                                                                start=(hc == 0), stop=(hc == 1)), [w2sbb, hidTb], [psb])
                xb, xbb = k.xbf.next()
                P.op("act", lambda e, ps=ps, xb=xb: e.copy(xb[:, :], ps[:, :]), [psb], [xbb])
                ps2, ps2b = k.ps[7], k.psb[7]
                P.op("pe", lambda e, xb=xb: e.matmul(ps2[0:32, :], k.pm[0:32, 0:32], xb[0:32, :], start=True, stop=True),
                     [xbb, k.pmb], [ps2b])
                t1, t1b = k.rt1.next()
                t2, t2b = k.rt2.next()
                P.op("dve", lambda e, ps=ps, t1=t1: e.tensor_tensor(t1[0:32, :], ps[0:32, :], cosC[0:32, :], ALU.mult), [psb] + tabsC, [t1b])
                P.op("dve", lambda e, t2=t2: e.tensor_tensor(t2[0:32, :], ps2[0:32, :], sinC[0:32, :], ALU.mult), [ps2b] + tabsC, [t2b])
                P.op("dve", lambda e, xb=xb, t1=t1, t2=t2: e.tensor_tensor(xb[0:32, :], t1[0:32, :], t2[0:32, :], ALU.add), [t1b, t2b, xbb], [xbb])
                P.op("act", lambda e, xb=xb, g=g: e.copy(kcmpT[:, g, 0:511], xb[:, 0:511]), [xbb], [kcmpTb])
            else:
                for ct in range(4):
                    ps, psb = k.ps[2 + ct % 2], k.psb[2 + ct % 2]
                    for hc in range(2):
                        P.op("pe", lambda e, ps=ps, hc=hc, ct=ct: e.matmul(ps[:, 0:128], hidT[:, hc, ct * 128:(ct + 1) * 128],
                                                                           w2sb[:, hc, :], start=(hc == 0), stop=(hc == 1)),
                             [w2sbb, hidTb], [psb])
                    P.op("act", lambda e, ps=ps, ct=ct, g=g: e.copy(vcmp1[:, ct, g, 0:128], ps[:, 0:128]), [psb], [vcmp1b])

    for tb in range(nb):
        t0 = tb * TB
        fence(k, allreg)
        P.dma("sp", hT[:, :, :], h_in[:, :, t0:t0 + TB].rearrange("k p t -> p k t"), writes=[hTb])
        if do_attn:
            for qt in range(4):
                n = tb * 4 + qt
                qs, qsb = a.qs.next()
                P.dma("sp", qs[:, :, :], qT_in[:, :, n * 128:(n + 1) * 128].rearrange("h d t -> d h t"), writes=[qsb])
                P.dma("sp", gsc[:, :], gates_in[n * 128:(n + 1) * 128, :], writes=[gscb])
                P.dma("sp", sM[:, :], scoreM_d[n], writes=[sMb])
                P.dma("sp", sA[:, :], scoreA_d[n], writes=[sAb])
                P.dma("sp", cmk[:, :, :], cmpmask_d[n], writes=[cmkb])
                for g in range(4):
                    nsa_group(k, a, n, g, qs, qsb, gsc, gscb, sM, sMb, sA, sAb, cmk, cmkb, kcmpT, kcmpTb, vcmp1, vcmp1b,
                              ovp, ovpb, estd, estdb, maskB, maskBb, imp, impb, scr, scrb, m8, m8b, selb, selbb, selT, selTb,
                              otot, ototb, sc4, sc4b, kslc_rel, vslc_rel, kwin_rel, vwin_rel, oT, oTb, qt)

            def ep_add(ps, psb, ch):
                P.op("dve", lambda e: e.tensor_tensor(hT[:, ch, :], hT[:, ch, :], ps[:, :], ALU.add), [psb, hTb], [hTb])
            k.linear_fm(w_out, D, 0, D, oT, oTb, TB, ep_add, mmb)
        if do_mlp:
            fence(k, allreg)
            mlp_ple(k, 1, hT, hTb, hnT, hnTb, big, bigb, gains, w1, w2, wgate, wproj, p_in, t0, stage, mmb, allreg)
        fence(k, allreg)
        yT = big[:, 0:2 * KC * TB].bitcast(F32).rearrange("p (a b) -> p a b", a=KC)
        final_norm_store(k, hT, hTb, g_fin, g_finb, yT, bigb, stage, out_o, t0)
    return k.finish()


def final_norm_store(k, hT, hTb, g, gb, yT, yTb, stage, out_o, t0):
    P = k.P
    T = TB
    pi = k.statb
    ps, psb = k.ps[pi], k.psb[pi]
    for kc in range(KC):
        sq, sqb = k.sq.next()
        P.op("act", lambda e, sq=sq, kc=kc: e.activation(sq[:, :T], hT[:, kc, :T], AF.Square), [hTb], [sqb])
        P.op("pe", lambda e, sq=sq, kc=kc: e.matmul(ps[:, :T], k.ones[:], sq[:, :T], start=(kc == 0), stop=(kc == KC - 1)),
             [sqb, k.onesb], [psb])
    rs, rsb = k.rstd
    P.op("act", lambda e: e.activation(rs[:, :T], ps[:, :T], AF.Sqrt, bias=EPS, scale=1.0 / D), [psb], [rsb])
    P.op("dve", lambda e: e.reciprocal(rs[:, :T], rs[:, :T]), [rsb], [rsb])
    for kc in range(KC):
        P.op("dve", lambda e, kc=kc: e.scalar_tensor_tensor(yT[:, kc, :T], hT[:, kc, :T], g[:, kc:kc + 1], rs[:, :T],
                                                            ALU.mult, ALU.mult), [hTb, gb, rsb], [yTb])
    for tile in range(4):
        st, stb = stage.next()
        for k0 in range(0, KC, 4):
            pj = [2, 3][(k0 // 4) % 2]
            pt, ptb = k.ps[pj], k.psb[pj]
            for j in range(4):
                P.op("pe", lambda e, pt=pt, j=j, k0=k0, tile=tile: e.transpose(
                    pt[:, j * 128:(j + 1) * 128], yT[:, k0 + j, tile * 128:(tile + 1) * 128], k.ident[:]),
                    [yTb, k.identb], [ptb])
            P.op("act", lambda e, pt=pt, st=st, k0=k0: e.copy(st[:, k0 * 128:(k0 + 4) * 128], pt[:, :]), [ptb], [stb])
        k.store(out_o[t0 + tile * 128:t0 + (tile + 1) * 128, :], st[:, :], stb)


def nsa_group(k, a, n, g, qs, qsb, gsc, gscb, sM, sMb, sA, sAb, cmk, cmkb, kcmpT, kcmpTb, vcmp1, vcmp1b,
              ovp, ovpb, estd, estdb, maskB, maskBb, imp, impb, scr, scrb, m8, m8b, selb, selbb, selT, selTb,
              otot, ototb, sc4, sc4b, kslc_rel, vslc_rel, kwin_rel, vwin_rel, oT, oTb, qt):
    P = k.P
    qheads = [4 * g + h for h in range(4)]
    ip, ipb = k.ps[2], k.psb[2]

    def gate_scale(branch, tiny):
        rz, rzb = attn_rz(k, a, tiny=tiny)
        c0 = branch * 16 + 4 * g
        P.op("dve", lambda e: e.tensor_tensor(sc4[:, :], rz[:, :], gsc[:, c0:c0 + 4], ALU.mult), [rzb, gscb], [sc4b])
        return rz, rzb

    def accum_otot(first):
        for h in range(4):
            acc = k.ps[a.acc[h // 2]]
            accb = k.psb[a.acc[h // 2]]
            src = acc[:, (h % 2) * 129:(h % 2) * 129 + 128]
            if first:
                P.op("dve", lambda e, h=h, src=src: e.tensor_scalar(otot[:, h, :], src, sc4[:, h:h + 1], None, ALU.mult),
                     [accb, sc4b], [ototb])
            else:
                P.op("dve", lambda e, h=h, src=src: e.scalar_tensor_tensor(otot[:, h, :], src, sc4[:, h:h + 1], otot[:, h, :],
                                                                           ALU.mult, ALU.add), [accb, sc4b, ototb], [ototb])

    attn_zero_acc(k, a)
    P.op("dve", lambda e: e.memset(ip[:, :], 0.0), [], [ipb])
    nct = n // 4 + 1
    for ct in range(nct):
        def extra(pT, pTb, ct=ct):
            for h in range(4):
                P.op("pe", lambda e, h=h: e.matmul(ip[:, h * 128:(h + 1) * 128], pT[:, h * 128:(h + 1) * 128], ovp[:, ct, :],
                                                   start=False, stop=False, skip_group_check=True), [pTb, ovpb], [ipb])
        mt = cmk[:, ct, :].unsqueeze(1).to_broadcast([128, 4, 128])
        attn_step(k, a, kcmpT[:, g, :], kcmpTb, ct, True, qs, qsb, qheads, None, vcmp1b,
                  (k.identbf[:], mt, [k.identbfb, cmkb]), extra=extra, vsel=lambda h, ct=ct: vcmp1[:, ct, g, :])
    rz, rzb = gate_scale(0, 1e-20)
    P.op("dve", lambda e: e.tensor_scalar(imp[:, :], ip[:, 0:128], rz[:, 0:1], None, ALU.mult), [ipb, rzb], [impb])
    for h in range(1, 4):
        P.op("dve", lambda e, h=h: e.scalar_tensor_tensor(imp[:, :], ip[:, h * 128:(h + 1) * 128], rz[:, h:h + 1], imp[:, :],
                                                          ALU.mult, ALU.add), [ipb, rzb, impb], [impb])
    accum_otot(True)
    P.op("dve", lambda e: e.tensor_tensor(imp[:, :], imp[:, :], sM[:, :], ALU.mult), [impb, sMb], [impb])
    P.op("dve", lambda e: e.tensor_tensor(imp[:, :], imp[:, :], sA[:, :], ALU.add), [impb, sAb], [impb])
    P.op("dve", lambda e: e.max(m8[:, 0:8], imp[:, :]), [impb], [m8b])
    P.op("dve", lambda e: e.match_replace(scr[:, :], m8[:, 0:8], imp[:, :], -3.0e38), [impb, m8b], [scrb])
    P.op("dve", lambda e: e.max(m8[:, 8:16], scr[:, :]), [scrb], [m8b])
    P.op("dve", lambda e: e.tensor_scalar(scr[:, :], imp[:, :], m8[:, 15:16], 1.0, ALU.is_ge, ALU.subtract), [impb, m8b], [scrb])
    P.op("dve", lambda e: e.tensor_scalar(selb[:, :], scr[:, :], -NEG, None, ALU.mult), [scrb], [selbb])
    p6, p6b = k.ps[6], k.psb[6]
    P.op("pe", lambda e: e.transpose(p6[:, 0:128], selb[:, :], k.identbf[:]), [selbb, k.identbfb], [p6b])
    P.op("act", lambda e: e.copy(selT[:, :], p6[:, 0:128]), [p6b], [selTb])
    attn_zero_acc(k, a)
    for rho in range(4):
        cnt = n + 1
        ks, ksb = a.kslots.next()
        vs, vsb_ = a.vslots.next()
        P.dma("sp", ks[:, 0:cnt * 128], kslc_rel[rho, g, :, 128:(cnt + 1) * 128], writes=[ksb])
        P.dma("act", vs[:, 0:cnt, :], vslc_rel[rho, 128:(cnt + 1) * 128, g, :].rearrange("(m l) c -> l m c", l=128),
              writes=[vsb_])
        for m in range(1, n + 2):
            s = m - 1
            if rho == 0 and m == n + 1:
                mm = (k.identbf[:], maskB[:, 0, :].unsqueeze(1).to_broadcast([128, 4, 128]), [k.identbfb, maskBb])
            else:
                e0 = 128 * (16 * rho + m - 1)
                mm = (estd[:, e0:e0 + 128], selT[:, :].unsqueeze(1).to_broadcast([128, 4, 128]), [estdb, selTb])
            attn_step(k, a, ks, ksb, s, True, qs, qsb, qheads, None, vsb_, mm, vsel=lambda h, vs=vs, s=s: vs[:, s, :])
    gate_scale(1, None)
    accum_otot(False)
    attn_zero_acc(k, a)
    for rho in range(4):
        ms = [n + 1] if rho else ([n, n + 1] if n >= 1 else [n + 1])
        m_lo, cnt = ms[0], len(ms)
        ks, ksb = a.kslots.next()
        vs, vsb_ = a.vslots.next()
        P.dma("sp", ks[:, 0:cnt * 128], kwin_rel[rho, g, :, m_lo * 128:(m_lo + cnt) * 128], writes=[ksb])
        P.dma("act", vs[:, 0:cnt, :], vwin_rel[rho, m_lo * 128:(m_lo + cnt) * 128, g, :].rearrange("(m l) c -> l m c", l=128),
              writes=[vsb_])
        for m in ms:
            s = m - m_lo
            if rho == 0 and m == n + 1:
                mm = (k.identbf[:], maskB[:, 0, :].unsqueeze(1).to_broadcast([128, 4, 128]), [k.identbfb, maskBb])
            elif rho == 0:
                mm = (k.identbf[:], maskB[:, 1, :].unsqueeze(1).to_broadcast([128, 4, 128]), [k.identbfb, maskBb])
            else:
                mm = None
            attn_step(k, a, ks, ksb, s, True, qs, qsb, qheads, None, vsb_, mm, vsel=lambda h, vs=vs, s=s: vs[:, s, :])
    gate_scale(2, None)
    accum_otot(False)
    ot, otb = a.otok.next()
    P.op("act", lambda e: e.copy(ot[:, :, :], otot[:, :, :]), [ototb], [otb])
    transpose_heads(k, a, ot, otb, lambda: oT[:, 4 * g:4 * g + 4, qt * 128:(qt + 1) * 128], oTb, 6)


_BF = ml_dtypes.bfloat16
_CACHE = {}


def _prog(name, fn):
    if name not in _CACHE:
        _CACHE[name] = fn()
    return _CACHE[name]


def p1_maps(I):
    hc = host_consts()
    maps = []
    for c in range(8):
        b, r = c // 4, c % 4
        pos = local_positions(r)
        cosT, sinT = rope_tables(pos)
        maps.append(dict(x=np.ascontiguousarray(I["x"][b][pos]), w_in=I["a_w_in"][0],
                         g_attn=gain_layout(I["attn_norm_g"][0]), cosT=cosT, sinT=sinT, **hc))
    return maps


def with_ones(v, nh):
    v = np.asarray(v).reshape(TL, nh, DH)
    return np.concatenate([v, np.ones((TL, nh, 1), _BF)], -1)


def p2_maps(I, r1):
    hc = host_consts()
    mA = np.ascontiguousarray(masks_A().transpose(1, 0, 2))
    maps = []
    for c in range(8):
        b, r = c // 4, c % 4
        pos = local_positions(r)
        cosT, sinT = rope_tables(pos)
        ks = [np.asarray(r1[4 * b + rr]["kT"]) for rr in range(4)]
        vs = [with_ones(r1[4 * b + rr]["v"], HA) for rr in range(4)]
        maps.append(dict(
            x=np.ascontiguousarray(I["x"][b][pos]), p0=np.ascontiguousarray(I["p"][0][b][pos]),
            qT=np.asarray(r1[c]["qT"]), kT_rel=rel_arrange(ks, r, 2), v1_rel=rel_arrange(vs, r, 0),
            a_w_out=I["a_w_out"][0], mlp_w1=I["mlp_w1"][0], mlp_w2=I["mlp_w2"][0], ple_w_gate=I["ple_w_gate"][0],
            ple_w_proj=I["ple_w_proj"][0], b_w_qg=I["b_w_qg"][0], w_kv=I["w_kv_shared"], cosT=cosT, sinT=sinT,
            g_mlp=gain_layout(I["mlp_norm_g"][0]), g_ple=gain_layout(I["ple_norm_g"][0]),
            g_attn1=gain_layout(I["attn_norm_g"][1]), g_kv=gain_layout(I["kv_norm_g"]), maskA=mA, **hc))
    return maps


def p3_maps(I, r2):
    hc = host_consts()
    maps = []
    for c in range(8):
        b, r = c // 4, c % 4
        pos = local_positions(r)
        kvT = [np.asarray(r2[4 * b + rr]["kvT"]) for rr in range(4)]
        vtm = [np.asarray(r2[4 * b + rr]["vtm"]) for rr in range(4)]
        raw = np.empty((2, 4, DH, S), _BF)
        for rr in range(4):
            gpos = local_positions(rr)
            raw[:, :, :, gpos] = kvT[rr][0:2]
        kslc = rel_arrange([kvT[rr][2] for rr in range(4)], r, 2)
        kwin = rel_arrange([kvT[rr][3] for rr in range(4)], r, 2)
        vslc = rel_arrange([with_ones(vtm[rr][0], 4) for rr in range(4)], r, 0)
        vwin = rel_arrange([with_ones(vtm[rr][1], 4) for rr in range(4)], r, 0)
        maps.append(dict(
            h_in=np.asarray(r2[c]["h_out"]), p1=np.ascontiguousarray(I["p"][1][b][pos]), qTB=np.asarray(r2[c]["qTB"]),
            gates=np.asarray(r2[c]["gates"]), rawT=raw, kslc_rel=kslc, vslc_rel=vslc, kwin_rel=kwin, vwin_rel=vwin,
            cmp_w1_k=I["cmp_w1_k"], cmp_w1_v=I["cmp_w1_v"], cmp_w2_k=I["cmp_w2_k"], cmp_w2_v=I["cmp_w2_v"],
            cmp_pe_k=I["cmp_pe_k"], cmp_pe_v=I["cmp_pe_v"], b_w_out=I["b_w_out"][0], mlp_w1=I["mlp_w1"][1],
            mlp_w2=I["mlp_w2"][1], ple_w_gate=I["ple_w_gate"][1], ple_w_proj=I["ple_w_proj"][1],
            g_mlp=gain_layout(I["mlp_norm_g"][1]), g_ple=gain_layout(I["ple_norm_g"][1]),
            g_fin=gain_layout(I["final_norm_g"]), **nsa_host_consts(r), **hc))
    return maps


def kernel_unfused(**inputs):
    I = {k_: np.asarray(v) for k_, v in inputs.items()}
    cores = list(range(8))
    r1 = run_bass_kernel_spmd(_prog("p1", build_p1), p1_maps(I), core_ids=cores).results
    r2 = run_bass_kernel_spmd(_prog("p2", build_p2), p2_maps(I, r1), core_ids=cores).results
    del r1
    r3 = run_bass_kernel_spmd(_prog("p3", build_p3), p3_maps(I, r2), core_ids=cores).results
    out = np.empty((B, S, D), np.float32)
    for c in range(8):
        b, r = c // 4, c % 4
        out[b][local_positions(r)] = np.asarray(r3[c]["out"])
    return out


TP = 8192
NBP = TP // TB


def private_positions(r):
    pi = np.arange(64)[:, None]
    l = np.arange(128)[None, :]
    return ((pi + r - 3) * 128 + l).reshape(-1)


def build_fused(nbp=NBP, nb3=NB, phases=(1, 2, 3)):
    k = KB(bf16_bank=6, wslot_elems=4096)
    k.wcache_on = True
    k.fast_rstd = True
    nc, P = k.nc, k.P
    xp = k.din("xp", [TP, D])
    p0p = k.din("p0p", [TP, DPLE])
    p1o = k.din("p1o", [TL, DPLE])
    w_in = k.din("a_w_in", [D, 3 * HA * DH])
    a_w_out = k.din("a_w_out", [8 * DH, D])
    w1 = [k.din("mlp_w1_0", [D, DFF]), k.din("mlp_w1_1", [D, DFF])]
    w2 = [k.din("mlp_w2_0", [DFF, D]), k.din("mlp_w2_1", [DFF, D])]
    wgate = [k.din("ple_w_gate_0", [D, D]), k.din("ple_w_gate_1", [D, D])]
    wproj = [k.din("ple_w_proj_0", [DPLE, D]), k.din("ple_w_proj_1", [DPLE, D])]
    w_qg = k.din("b_w_qg", [D, 2096])
    w_kv = k.din("w_kv", [D, 3072])
    b_w_out = k.din("b_w_out", [D, D])
    cw1 = [k.din("cmp_w1_k", [32 * DH, 256]), k.din("cmp_w1_v", [32 * DH, 256])]
    cw2 = [k.din("cmp_w2_k", [256, DH]), k.din("cmp_w2_v", [256, DH])]
    cpe = [k.din("cmp_pe_k", [32, DH]), k.din("cmp_pe_v", [32, DH])]
    cos_d = k.din("cosP", [32, TP])
    sin_d = k.din("sinP", [32, TP])
    scoreM_d = k.din("scoreM", [NT, 128, 128])
    scoreA_d = k.din("scoreA", [NT, 128, 128])
    cmpmask_d = k.din("cmpmask", [NT, 128, 4, 128], BF16)
    estd_d = k.din("estd", [128, 8192], BF16)
    ovp_d = k.din("ov_perm", [128, 4, 128], BF16)
    out_o = k.dout("out", [TL, D], F32)
    def scratch(name, shape, dt=BF16):
        return nc.dram_tensor(name, list(shape), dt).ap()
    qT_s = scratch("qT_s", [HA, DH, TP])
    kT_s = scratch("kT_s", [HA, DH, TP])
    v1_s = scratch("v1_s", [TP, HA, 129])
    h_own = scratch("h_own", [KC, 128, TL], F32)
    qTB_s = scratch("qTB_s", [16, DH, TL])
    gates_s = scratch("gates_s", [TL, 48], F32)
    kvT_s = scratch("kvT_s", [4, 4, DH, TP])
    vtm1_s = scratch("vtm1_s", [2, TP, 4, 129])
    xT_s = scratch("xT_s", [KC, 128, TP], F32)
    k.common_consts()
    g_attn0, g_attn0b = k.load_const("g_attn0", [128, KC])
    g_mlp0, g_mlp0b = k.load_const("g_mlp0", [128, KC])
    g_ple0, g_ple0b = k.load_const("g_ple0", [128, KC])
    g_attn1, g_attn1b = k.load_const("g_attn1", [128, KC])
    g_kv, g_kvb = k.load_const("g_kv", [128, KC])
    g_mlp1, g_mlp1b = k.load_const("g_mlp1", [128, KC])
    g_ple1, g_ple1b = k.load_const("g_ple1", [128, KC])
    g_fin, g_finb = k.load_const("g_fin", [128, KC])
    validt, validtb = k.load_const("validt", [128, 64], cast=True)
    maskA, maskAb = k.load_const("maskA", [128, 9, 128], cast=True)
    maskB, maskBb = k.load_const("maskB", [128, 2, 128], cast=True)
    cosC, cosCb = k.load_const("cosC", [32, 512])
    sinC, sinCb = k.load_const("sinC", [32, 512])
    ovp, ovpb = k.sb("ovp_sb", [128, 4, 128], BF16)
    P.dma("sp", ovp[:, :, :], ovp_d, writes=[ovpb])
    hT, hTb = k.sb("hT", [128, KC, TB], F32)
    hnT, hnTb = k.sb("hnT", [128, KC, TB], BF16)
    oT, oTb = hnT, hnTb
    big, bigb = k.sb("big", [128, 64 * TB], BF16)
    cosT, cosTb = k.sb("cosT_sb", [32, TB], F32)
    sinT, sinTb = k.sb("sinT_sb", [32, TB], F32)
    kcmpT, kcmpTb = k.sb("kcmpT", [128, 4, 512], BF16)
    vcmp1, vcmp1b = k.sb("vcmp1", [128, 4, 4, 129], BF16)
    common_scratch(k)
    k.g_x, k.g_u = k.relu_t.items[0], k.relu_t.items[1]
    gsb = Rot([k.sb(f"gsb{i}", [128, 48], F32) for i in range(2)])
    a_pT = Rot([k.sb(f"pT{i}", [128, 512], BF16) for i in range(4)])
    a_rz = k.sb("rz", [128, 4], F32)
    a_otok = Rot([k.sb(f"otok{i}", [128, 4, 128], BF16) for i in range(2)])
    hidT, hidTb = k.sb("hidT", [128, 2, 512], BF16)
    w2sb, w2sbb = k.sb("w2sb", [128, 2, 128], BF16)
    peT, peTb = k.sb("peT", [128, 32], BF16)
    biasT, biasTb = k.sb("biasT", [128, 2], F32)
    gsc, gscb = k.sb("gates_sb", [128, 48], F32)
    sM, sMb = k.sb("sM", [128, 128], F32)
    sA, sAb = k.sb("sA", [128, 128], F32)
    cmk, cmkb = k.sb("cmk", [128, 4, 128], BF16)
    imp, impb = k.sb("imp", [128, 128], F32)
    scr, scrb = k.sb("scr", [128, 128], F32)
    m8, m8b = k.sb("m8", [128, 16], F32)
    selb, selbb = k.sb("selb", [128, 128], BF16)
    selT, selTb = k.sb("selT", [128, 128], BF16)
    otot, ototb = k.sb("otot", [128, 4, 128], F32)
    sc4, sc4b = k.sb("sc4", [128, 4], F32)
    mmb = [0, 1, 2, 3]
    NELE = 64 * TB

    def mk_attn(reg, kshape, vshape, nq):
        a = AttnRes()
        a.kslots = Rot([reg.take(f"kslot{i}", kshape) for i in range(3)])
        a.vslots = Rot([reg.take(f"vslot{i}", vshape) for i in range(3)])
        a.qs = Rot([reg.take(f"qs{i}", [128, nq, 128]) for i in range(1)])
        a.pT, a.rz, a.otok = a_pT, a_rz, a_otok
        a.sbanks = Rot([0, 1, 3])
        a.acc = (4, 5)
        a.region_bufs = [b for _, b in a.kslots.items + a.vslots.items + a.qs.items]
        return a
    regA = Region(k, big, NELE)
    aA = mk_attn(regA, [128, 4, 5 * 128], [128, 5, 4, 129], HA)
    aA.qs = Rot(aA.qs.items + [regA.take("qs_b", [128, HA, 128])])
    aA.region_bufs.append(aA.qs.items[1][1])
    assert regA.off <= 23552, regA.off
    regA2 = Region(k, big, NELE)
    vasm2 = [regA2.take(f"vasm2_{i}", [128, 2, 4, 129]) for i in range(4)]
    regP1 = Region(k, big, NELE)
    vasm = [regP1.take(f"vasm{i}", [128, HA, 129]) for i in range(4)]
    regA.off = 23552
    stage = Rot([regA.take(f"stage{i}", [128, D], F32) for i in range(2)])
    regB = Region(k, big, NELE)
    aB = mk_attn(regB, [128, 17 * 128], [128, 17, 129], 16)
    estd, estdb = regB.take("estd", [128, 8192])
    assert regB.off <= 23552, regB.off
    regC = Region(k, big, NELE)
    w1sb, w1sbb = regC.take("w1sb", [128, 32, 256])
    rawsb = Rot([regC.take(f"rawsb{i}", [128, S]) for i in range(2)])
    allreg = (aA.region_bufs + aB.region_bufs + [b for _, b in vasm + vasm2 + stage.items + rawsb.items]
              + [bigb, estdb, w1sbb])
    tabs = [cosTb, sinTb]
    tabsC = [cosCb, sinCb]
    k.store_reads = []

    def load_tabs(t0):
        P.dma("sp", cosT[:, :], cos_d[:, t0:t0 + TB], writes=[cosTb])
        P.dma("sp", sinT[:, :], sin_d[:, t0:t0 + TB], writes=[sinTb])

    def ep_add(ps, psb, ch):
        P.op("dve", lambda e: e.tensor_tensor(hT[:, ch, :], hT[:, ch, :], ps[:, :], ALU.add), [psb, hTb], [hTb])

    pre2, pre3 = [], []
    if 2 in phases and 1 in phases:
        for (w, K_, c0_, n_, mode) in ((a_w_out, 8 * DH, 0, D, "fm"), (w1[0], D, 0, DFF, "fm"), (w2[0], DFF, 0, D, "fm"),
                                       (wgate[0], D, 0, D, "fm"), (wproj[0], DPLE, 0, D, "fm"),
                                       (w_qg, D, 0, 16 * DH, "fm"), (w_qg, D, 16 * DH, 48, "tm"),
                                       (w_kv, D, 0, 512, "fm"), (w_kv, D, 512, 512, "fm"), (w_kv, D, 1024, 512, "fm"),
                                       (w_kv, D, 1536, 512, "tm"), (w_kv, D, 2048, 512, "fm"), (w_kv, D, 2560, 512, "tm")):
            pre2 += k.precast_list(w, K_, c0_, n_, mode)
    if 3 in phases and 2 in phases:
        for (w, K_, c0_, n_, mode) in ((b_w_out, D, 0, D, "fm"), (w1[1], D, 0, DFF, "fm"), (w2[1], DFF, 0, D, "fm"),
                                       (wgate[1], D, 0, D, "fm"), (wproj[1], DPLE, 0, D, "fm")):
            pre3 += k.precast_list(w, K_, c0_, n_, mode)

    def drain(lst, n):
        for _ in range(min(n, len(lst))):
            k.precast_issue(lst.pop(0))
    if 1 in phases:
        for tb in range(nbp):
            t0 = tb * TB
            fence(k, allreg)
            if tb >= 1:
                drain(pre2, -(-len(pre2) // max(1, nbp - tb)))
            k.load_transpose(xp, t0, KC, hT, hTb, k.ident, k.identb, stage)
            k.store(xT_s[:, :, t0:t0 + TB].rearrange("k p t -> p k t"), hT[:, :, :], hTb)
            load_tabs(t0)
            k.rmsnorm(hT, hTb, g_attn0, g_attn0b, hnT, hnTb)

            def ep_qk(ps, psb, ch, t0=t0):
                dst = (qT_s if ch < HA else kT_s)[ch % HA, :, t0:t0 + TB]
                k.rope_head(ps, psb, cosT, sinT, tabs, 0, TB, dst)
            k.linear_fm(w_in, D, 0, 2 * HA * DH, hnT, hnTb, TB, ep_qk, mmb)

            def ep_v(ps, psb, tile, c0, gc):
                vt, vtb = vasm[tile]
                h0 = c0 // 128
                P.op("act", lambda e: e.copy(vt[:, h0:h0 + gc // 128, 0:128],
                                             ps[:, :gc].rearrange("p (a b) -> p a b", a=gc // 128)), [psb], [vtb])
            k.linear_tm(w_in, D, 2 * HA * DH, HA * DH, hnT, hnTb, TB // 128, ep_v, mmb)
            for tile in range(4):
                vt, vtb = vasm[tile]
                pi_ = tb * 4 + tile
                P.op("dve", lambda e, vt=vt, pi_=pi_: e.tensor_copy(
                    vt[:, :, 128], validt[:, pi_:pi_ + 1].to_broadcast([128, HA])), [validtb], [vtb])
                k.store(v1_s[t0 + tile * 128:t0 + (tile + 1) * 128, :, :], vt[:, :, :], vtb)
        drain(pre2, len(pre2))
        k.phase_barrier()

    if 2 in phases:
        gains0 = (g_mlp0, g_mlp0b, g_ple0, g_ple0b)
        for tb in range(nbp):
            t0 = tb * TB
            fence(k, allreg)
            if tb >= 1:
                drain(pre3, -(-len(pre3) // max(1, nbp - tb)))
            if 1 in phases:
                P.dma("sp", hT[:, :, :], xT_s[:, :, t0:t0 + TB].rearrange("k p t -> p k t"), reads=[k.phaseb], writes=[hTb])
            else:
                k.load_transpose(xp, t0, KC, hT, hTb, k.ident, k.identb, stage)
            for qt in range(4):
                pq = tb * 4 + qt
                qs, qsb = aA.qs.next()
                P.dma("sp", qs[:, :, :], qT_s[:, :, pq * 128:(pq + 1) * 128].rearrange("h d t -> d h t"),
                      reads=[k.phaseb], writes=[qsb])
                items = []
                for quad in range(2):
                    qi = []
                    for g, (dil, nk) in enumerate(GROUPS_A):
                        lo = max(0, pq - nk)
                        for c_lo in range(lo, pq + 1, 5):
                            qi.append((quad, g, nk, c_lo, min(c_lo + 4, pq)))
                    items += [(it, j == 0, j == len(qi) - 1) for j, it in enumerate(qi)]
                loaded = []

                def issue(i):
                    (quad, g, nk, c_lo, c_hi), _, _ = items[i]
                    h0 = g * 8 + quad * 4
                    cnt = c_hi - c_lo + 1
                    ks, ksb = aA.kslots.next()
                    vs, vsb_ = aA.vslots.next()
                    P.dma("sp", ks[:, :, 0:cnt * 128],
                          kT_s[h0:h0 + 4, :, c_lo * 128:(c_hi + 1) * 128].rearrange("h d t -> d h t"),
                          reads=[k.phaseb], writes=[ksb])
                    P.dma("sp", vs[:, 0:cnt, :, :],
                          v1_s[c_lo * 128:(c_hi + 1) * 128, h0:h0 + 4, :].rearrange("(m l) h c -> l m h c", l=128),
                          reads=[k.phaseb], writes=[vsb_])
                    loaded.append((ks, ksb, vs, vsb_))
                pipe = Pipe(depth=2, nslots=3, issue=issue, nchunks=len(items))

                def finalize(quad, qt=qt):
                    rz, rzb = attn_rz(k, aA, tiny=1e-30)
                    ot, otb = aA.otok.next()
                    for h in range(4):
                        acc = k.ps[aA.acc[h // 2]]
                        P.op("dve", lambda e, h=h, acc=acc, ot=ot, rz=rz: e.tensor_scalar(
                            ot[:, h, :], acc[:, (h % 2) * 129:(h % 2) * 129 + 128], rz[:, h:h + 1], None, ALU.mult),
                            [k.psb[aA.acc[h // 2]], rzb], [otb])
                    transpose_heads(k, aA, ot, otb,
                                    lambda: oT[:, quad * 4:quad * 4 + 4, qt * 128:(qt + 1) * 128], oTb, 6)
                for i, ((quad, g, nk, c_lo, c_hi), first, last) in enumerate(items):
                    pipe.begin_chunk(i)
                    ks, ksb, vs, vsb_ = loaded[i]
                    h0 = g * 8 + quad * 4
                    for tile in range(c_lo, c_hi + 1):
                        j = pq - tile
                        kind = 0 if j == 0 else (2 if j == nk else 1)
                        mt = maskA[:, g * 3 + kind, :].unsqueeze(1).to_broadcast([128, 4, 128])
                        pipe.step(lambda ks=ks, ksb=ksb, vs=vs, vsb_=vsb_, tile=tile, c_lo=c_lo, h0=h0, mt=mt: attn_step(
                            k, aA, ks, ksb, tile - c_lo, False, qs, qsb, [h0 + h for h in range(4)], vs, vsb_,
                            (k.identbf[:], mt, [k.identbfb, maskAb]), defer=True),
                            first_hook=(lambda: attn_zero_acc(k, aA)) if (first and tile == c_lo) else None,
                            last_hook=(lambda quad=quad: finalize(quad)) if (last and tile == c_hi) else None, chunk=i)
                pipe.flush()
            k.linear_fm(a_w_out, 8 * DH, 0, D, oT, oTb, TB, ep_add, mmb)
            fence(k, allreg)
            mlp_ple(k, 0, hT, hTb, hnT, hnTb, big, bigb, gains0, w1[0], w2[0], wgate[0], wproj[0], p0p, t0, stage, mmb, allreg)
            fence(k, allreg)
            k.store(h_own[:, :, tb * 128:(tb + 1) * 128].rearrange("k p t -> p k t"), hT[:, :, 384:512], hTb)
            load_tabs(t0)
            k.rmsnorm(hT, hTb, g_attn1, g_attn1b, hnT, hnTb)
            hn_own = hnT[:, :, 384:512]

            def ep_q(ps, psb, ch, tb=tb):
                k.rope_head(ps, psb, cosT, sinT, tabs, 384, 128, qTB_s[ch, :, tb * 128:(tb + 1) * 128])
            k.linear_fm(w_qg, D, 0, 16 * DH, hn_own, hnTb, 128, ep_q, mmb)

            def ep_g(ps, psb, tile, c0, gc, tb=tb):
                gs, gsbuf = gsb.next()
                P.op("act", lambda e: e.activation(gs[:, :gc], ps[:, :gc], AF.Sigmoid), [psb], [gsbuf])
                k.store(gates_s[tb * 128:(tb + 1) * 128, :], gs[:, :gc], gsbuf)
            k.linear_tm(w_qg, D, 16 * DH, 48, hnT, hnTb, 4, ep_g, mmb, tiles=[3])
            k.rmsnorm(hT, hTb, g_kv, g_kvb, hnT, hnTb, reuse_stats=True)

            def mk_ep_fm(slot, rope, t0=t0):
                def ep(ps, psb, ch):
                    dst = kvT_s[slot, ch, :, t0:t0 + TB]
                    if rope:
                        k.rope_head(ps, psb, cosT, sinT, tabs, 0, TB, dst)
                    else:
                        xb, xbb = k.xbf.next()
                        P.op("act", lambda e: e.copy(xb[:, :], ps[:, :]), [psb], [xbb])
                        k.store(dst, xb[:, :], xbb)
                return ep

            def mk_ep_tm(slot):
                def ep(ps, psb, tile, c0, gc):
                    vt, vtb = vasm2[tile]
                    g0 = c0 // 128
                    P.op("act", lambda e: e.copy(vt[:, slot, g0:g0 + gc // 128, 0:128],
                                                 ps[:, :gc].rearrange("p (a b) -> p a b", a=gc // 128)), [psb], [vtb])
                return ep
            k.linear_fm(w_kv, D, 0, 512, hnT, hnTb, TB, mk_ep_fm(0, False), mmb)
            k.linear_fm(w_kv, D, 512, 512, hnT, hnTb, TB, mk_ep_fm(1, False), mmb)
            k.linear_fm(w_kv, D, 1024, 512, hnT, hnTb, TB, mk_ep_fm(2, True), mmb)
            k.linear_tm(w_kv, D, 1536, 512, hnT, hnTb, 4, mk_ep_tm(0), mmb)
            k.linear_fm(w_kv, D, 2048, 512, hnT, hnTb, TB, mk_ep_fm(3, True), mmb)
            k.linear_tm(w_kv, D, 2560, 512, hnT, hnTb, 4, mk_ep_tm(1), mmb)
            for tile in range(4):
                vt, vtb = vasm2[tile]
                pi_ = tb * 4 + tile
                P.op("dve", lambda e, vt=vt, pi_=pi_: e.tensor_copy(
                    vt[:, :, :, 128], validt[:, pi_:pi_ + 1].unsqueeze(1).to_broadcast([128, 2, 4])), [validtb], [vtb])
                for slot in range(2):
                    k.store(vtm1_s[slot, t0 + tile * 128:t0 + (tile + 1) * 128, :, :], vt[:, slot, :, :], vtb)
        drain(pre3, len(pre3))
        k.phase_barrier()

    if 3 in phases:
        fence(k, allreg)
        P.op("dve", lambda e: e.memset(vcmp1[:, :, :, 128:129], 1.0), [], [vcmp1b])
        P.op("dve", lambda e: e.memset(hidT[:, :, :], 0.0), [], [hidTb])
        P.op("dve", lambda e: e.memset(kcmpT[:, :, :], 0.0), [], [kcmpTb])
        for kv in range(2):
            fence(k, [w1sbb, w2sbb, peTb])
            P.dma("pool", w1sb[:, :, :], cw1[kv].rearrange("(l d) h -> d l h", d=128), writes=[w1sbb])
            P.dma("pool", w2sb[:, :, :], cw2[kv].rearrange("(c p) d -> p c d", p=128), writes=[w2sbb])
            P.op("pool", lambda e, kv=kv: e.dma_start(out=peT[:, :], in_=cpe[kv].rearrange("l d -> d l"),
                                                      allow_slow_non_contiguous=True), [], [peTb], is_dma=True)
            ps7, ps7b = k.ps[7], k.psb[7]
            for hc in range(2):
                for l in range(32):
                    P.op("pe", lambda e, hc=hc, l=l: e.matmul(ps7[:, hc:hc + 1], w1sb[:, l, hc * 128:(hc + 1) * 128],
                                                              peT[:, l:l + 1], start=(l == 0), stop=(l == 31),
                                                              skip_group_check=True), [w1sbb, peTb], [ps7b])
            P.op("dve", lambda e: e.tensor_copy(biasT[:, :], ps7[:, 0:2]), [ps7b], [biasTb])
            for g in range(4):
                rs, rsb = rawsb.next()
                P.dma("sp", rs[:, :], kvT_s[kv, g, :, :], reads=[k.phaseb], writes=[rsb])
                for hc in range(2):
                    ps, psb = k.ps[mmb[hc]], k.psb[mmb[hc]]
                    for l in range(32):
                        P.op("pe", lambda e, ps=ps, rs=rs, hc=hc, l=l: e.matmul(
                            ps[:, 0:511], w1sb[:, l, hc * 128:(hc + 1) * 128], rs[:, l:l + 16 * 510 + 1:16],
                            start=(l == 0), stop=(l == 31)), [w1sbb, rsb], [psb])
                    gelu_tanh(k, hidT[:, hc, 0:511], hidTb, ps, psb, biasT[:, hc:hc + 1], biasTb, 511)
                if kv == 0:
                    ps, psb = k.ps[2], k.psb[2]
                    for hc in range(2):
                        P.op("pe", lambda e, ps=ps, hc=hc: e.matmul(ps[:, 0:512], w2sb[:, hc, :], hidT[:, hc, :],
                                                                    start=(hc == 0), stop=(hc == 1)), [w2sbb, hidTb], [psb])
                    xb, xbb = k.xbf.next()
                    P.op("act", lambda e, ps=ps, xb=xb: e.copy(xb[:, :], ps[:, :]), [psb], [xbb])
                    ps2, ps2b = k.ps[7], k.psb[7]
                    P.op("pe", lambda e, xb=xb: e.matmul(ps2[0:32, :], k.pm[0:32, 0:32], xb[0:32, :], start=True, stop=True),
                         [xbb, k.pmb], [ps2b])
                    t1, t1b = k.rt1.next()
                    t2, t2b = k.rt2.next()
                    P.op("dve", lambda e, ps=ps, t1=t1: e.tensor_tensor(t1[0:32, :], ps[0:32, :], cosC[0:32, :], ALU.mult), [psb] + tabsC, [t1b])
                    P.op("dve", lambda e, t2=t2: e.tensor_tensor(t2[0:32, :], ps2[0:32, :], sinC[0:32, :], ALU.mult), [ps2b] + tabsC, [t2b])
                    P.op("dve", lambda e, xb=xb, t1=t1, t2=t2: e.tensor_tensor(xb[0:32, :], t1[0:32, :], t2[0:32, :], ALU.add), [t1b, t2b, xbb], [xbb])
                    P.op("act", lambda e, xb=xb, g=g: e.copy(kcmpT[:, g, 0:511], xb[:, 0:511]), [xbb], [kcmpTb])
                else:
                    for ct in range(4):
                        ps, psb = k.ps[2 + ct % 2], k.psb[2 + ct % 2]
                        for hc in range(2):
                            P.op("pe", lambda e, ps=ps, hc=hc, ct=ct: e.matmul(ps[:, 0:128], hidT[:, hc, ct * 128:(ct + 1) * 128],
                                                                               w2sb[:, hc, :], start=(hc == 0), stop=(hc == 1)),
                                 [w2sbb, hidTb], [psb])
                        P.op("act", lambda e, ps=ps, ct=ct, g=g: e.copy(vcmp1[:, ct, g, 0:128], ps[:, 0:128]), [psb], [vcmp1b])
        gains1 = (g_mlp1, g_mlp1b, g_ple1, g_ple1b)
        for tb in range(nb3):
            t0 = tb * TB
            fence(k, allreg)
            P.dma("sp", estd[:, :], estd_d, writes=[estdb])
            P.dma("sp", hT[:, :, :], h_own[:, :, t0:t0 + TB].rearrange("k p t -> p k t"), reads=[k.phaseb], writes=[hTb])
            for qt in range(4):
                n = tb * 4 + qt
                qs, qsb = aB.qs.next()
                P.dma("sp", qs[:, :, :], qTB_s[:, :, n * 128:(n + 1) * 128].rearrange("h d t -> d h t"),
                      reads=[k.phaseb], writes=[qsb])
                P.dma("sp", gsc[:, :], gates_s[n * 128:(n + 1) * 128, :], reads=[k.phaseb], writes=[gscb])
                P.dma("sp", sM[:, :], scoreM_d[n], writes=[sMb])
                P.dma("sp", sA[:, :], scoreA_d[n], writes=[sAb])
                P.dma("sp", cmk[:, :, :], cmpmask_d[n], writes=[cmkb])
                for g in range(4):
                    nsa_group_f(k, aB, n, g, qs, qsb, gsc, gscb, sM, sMb, sA, sAb, cmk, cmkb, kcmpT, kcmpTb, vcmp1, vcmp1b,
                                ovp, ovpb, estd, estdb, maskB, maskBb, imp, impb, scr, scrb, m8, m8b, selb, selbb, selT, selTb,
                                otot, ototb, sc4, sc4b, kvT_s, vtm1_s, oT, oTb, qt)
            k.linear_fm(b_w_out, D, 0, D, oT, oTb, TB, ep_add, mmb)
            fence(k, allreg)
            mlp_ple(k, 1, hT, hTb, hnT, hnTb, big, bigb, gains1, w1[1], w2[1], wgate[1], wproj[1], p1o, t0, stage, mmb, allreg)
            fence(k, allreg)
            yT = big[:, 0:2 * KC * TB].bitcast(F32).rearrange("p (a b) -> p a b", a=KC)
            final_norm_store(k, hT, hTb, g_fin, g_finb, yT, bigb, stage, out_o, t0)
    return k.finish()


def nsa_group_f(k, a, n, g, qs, qsb, gsc, gscb, sM, sMb, sA, sAb, cmk, cmkb, kcmpT, kcmpTb, vcmp1, vcmp1b,
                ovp, ovpb, estd, estdb, maskB, maskBb, imp, impb, scr, scrb, m8, m8b, selb, selbb, selT, selTb,
                otot, ototb, sc4, sc4b, kvT_s, vtm1_s, oT, oTb, qt):
    P = k.P
    pq = 4 * n + 3
    qheads = [4 * g + h for h in range(4)]
    ip, ipb = k.ps[2], k.psb[2]

    def gate_scale(branch, tiny):
        rz, rzb = attn_rz(k, a, tiny=tiny)
        c0 = branch * 16 + 4 * g
        P.op("dve", lambda e: e.tensor_tensor(sc4[:, :], rz[:, :], gsc[:, c0:c0 + 4], ALU.mult), [rzb, gscb], [sc4b])
        return rz, rzb

    def accum_otot(first):
        for h in range(4):
            acc = k.ps[a.acc[h // 2]]
            accb = k.psb[a.acc[h // 2]]
            src = acc[:, (h % 2) * 129:(h % 2) * 129 + 128]
            if first:
                P.op("dve", lambda e, h=h, src=src: e.tensor_scalar(otot[:, h, :], src, sc4[:, h:h + 1], None, ALU.mult),
                     [accb, sc4b], [ototb])
            else:
                P.op("dve", lambda e, h=h, src=src: e.scalar_tensor_tensor(otot[:, h, :], src, sc4[:, h:h + 1], otot[:, h, :],
                                                                           ALU.mult, ALU.add), [accb, sc4b, ototb], [ototb])
    chunks = [("win", max(0, pq - 4), pq)] + [("slc", c_lo, min(c_lo + 16, pq)) for c_lo in range(0, pq + 1, 17)]
    loaded = []

    def issue(i):
        kind, c_lo, c_hi = chunks[i]
        cnt = c_hi - c_lo + 1
        ks, ksb = a.kslots.next()
        vs, vsb_ = a.vslots.next()
        ksl, vsl = (2, 0) if kind == "slc" else (3, 1)
        P.dma("sp", ks[:, 0:cnt * 128], kvT_s[ksl, g, :, c_lo * 128:(c_hi + 1) * 128], reads=[k.phaseb], writes=[ksb])
        P.dma("sp", vs[:, 0:cnt, :], vtm1_s[vsl, c_lo * 128:(c_hi + 1) * 128, g, :].rearrange("(m l) c -> l m c", l=128),
              reads=[k.phaseb], writes=[vsb_])
        loaded.append((ks, ksb, vs, vsb_))
    pipe = Pipe(depth=2, nslots=3, issue=issue, nchunks=len(chunks))
    pipe.begin_chunk(0)
    nct = min((16 * pq + 7) // 128 + 1, 4)

    def cmp_first():
        attn_zero_acc(k, a)
        P.op("dve", lambda e: e.memset(ip[:, :], 0.0), [], [ipb])

    def cmp_last():
        rz, rzb = gate_scale(0, 1e-20)
        P.op("dve", lambda e: e.tensor_scalar(imp[:, :], ip[:, 0:128], rz[:, 0:1], None, ALU.mult), [ipb, rzb], [impb])
        for h in range(1, 4):
            P.op("dve", lambda e, h=h: e.scalar_tensor_tensor(imp[:, :], ip[:, h * 128:(h + 1) * 128], rz[:, h:h + 1], imp[:, :],
                                                              ALU.mult, ALU.add), [ipb, rzb, impb], [impb])
        accum_otot(True)
        P.op("dve", lambda e: e.tensor_tensor(imp[:, :], imp[:, :], sM[:, :], ALU.mult), [impb, sMb], [impb])
        P.op("dve", lambda e: e.tensor_tensor(imp[:, :], imp[:, :], sA[:, :], ALU.add), [impb, sAb], [impb])
        P.op("dve", lambda e: e.max(m8[:, 0:8], imp[:, :]), [impb], [m8b])
        P.op("dve", lambda e: e.match_replace(scr[:, :], m8[:, 0:8], imp[:, :], -3.0e38), [impb, m8b], [scrb])
        P.op("dve", lambda e: e.max(m8[:, 8:16], scr[:, :]), [scrb], [m8b])
        P.op("dve", lambda e: e.tensor_scalar(scr[:, :], imp[:, :], m8[:, 15:16], 1.0, ALU.is_ge, ALU.subtract), [impb, m8b], [scrb])
        P.op("dve", lambda e: e.tensor_scalar(selb[:, :], scr[:, :], -NEG, None, ALU.mult), [scrb], [selbb])
        p6, p6b = k.ps[6], k.psb[6]
        P.op("pe", lambda e: e.transpose(p6[:, 0:128], selb[:, :], k.identbf[:]), [selbb, k.identbfb], [p6b])
        P.op("act", lambda e: e.copy(selT[:, :], p6[:, 0:128]), [p6b], [selTb])
    for ct in range(nct):
        def extra(pT, pTb, ct=ct):
            for h in range(4):
                P.op("pe", lambda e, h=h: e.matmul(ip[:, h * 128:(h + 1) * 128], pT[:, h * 128:(h + 1) * 128], ovp[:, ct, :],
                                                   start=False, stop=False, skip_group_check=True), [pTb, ovpb], [ipb])
        mt = cmk[:, ct, :].unsqueeze(1).to_broadcast([128, 4, 128])
        pipe.step(lambda ct=ct, mt=mt, extra=extra: attn_step(
            k, a, kcmpT[:, g, :], kcmpTb, ct, True, qs, qsb, qheads, None, vcmp1b,
            (k.identbf[:], mt, [k.identbfb, cmkb]), extra=extra, vsel=lambda h, ct=ct: vcmp1[:, ct, g, :], defer=True),
            first_hook=cmp_first if ct == 0 else None, last_hook=cmp_last if ct == nct - 1 else None)
    for i, (kind, c_lo, c_hi) in enumerate(chunks):
        pipe.begin_chunk(i)
        ks, ksb, vs, vsb_ = loaded[i]
        for tile in range(c_lo, c_hi + 1):
            s = tile - c_lo
            if tile == pq:
                mm = (k.identbf[:], maskB[:, 0, :].unsqueeze(1).to_broadcast([128, 4, 128]), [k.identbfb, maskBb])
            elif kind == "slc":
                mm = (estd[:, 128 * tile:128 * tile + 128], selT[:, :].unsqueeze(1).to_broadcast([128, 4, 128]), [estdb, selTb])
            elif tile == pq - 4:
                mm = (k.identbf[:], maskB[:, 1, :].unsqueeze(1).to_broadcast([128, 4, 128]), [k.identbfb, maskBb])
            else:
                mm = None
            is_first = (tile == c_lo) and (kind == "win" or c_lo == 0)
            is_last = (tile == pq)
            branch = 2 if kind == "win" else 1

            def last_hook(branch=branch):
                gate_scale(branch, 1e-30)
                accum_otot(False)
            pipe.step(lambda ks=ks, ksb=ksb, vs=vs, vsb_=vsb_, s=s, mm=mm: attn_step(
                k, a, ks, ksb, s, True, qs, qsb, qheads, None, vsb_, mm, vsel=lambda h, vs=vs, s=s: vs[:, s, :], defer=True),
                first_hook=(lambda: attn_zero_acc(k, a)) if is_first else None,
                last_hook=last_hook if is_last else None, chunk=i)
    pipe.flush()
    ot, otb = a.otok.next()
    P.op("act", lambda e: e.copy(ot[:, :, :], otot[:, :, :]), [ototb], [otb])
    transpose_heads(k, a, ot, otb, lambda: oT[:, 4 * g:4 * g + 4, qt * 128:(qt + 1) * 128], oTb, 6)


def nsa_private_consts(r):
    sh = r - 3
    jp = np.arange(128)
    j_of = jp + 2 * sh
    cp = np.arange(512)
    c_of = cp + 8 * sh
    cstart = c_of * 16
    ov = np.zeros((512, 128), np.float32)
    for q in range(128):
        j = j_of[q]
        if j < 0 or j >= 128:
            continue
        o = np.clip(np.minimum(cstart + 32, j * 64 + 64) - np.maximum(cstart, j * 64), 0, None) / 32.0
        ov[:, q] = np.where((c_of >= 0) & (c_of <= 510), o, 0.0)
    ov_perm = ov.reshape(4, 128, 128).transpose(1, 0, 2)
    scoreM = np.zeros((NT, 128, 128), np.float32)
    scoreA = np.zeros((NT, 128, 128), np.float32)
    cmpmask = np.zeros((NT, 128, 4, 128), np.float32)
    l = np.arange(128)
    for n in range(NT):
        t = (4 * n + r) * 128 + l
        cur = t // 64
        for q in range(128):
            j = j_of[q]
            if j < 0 or j >= 128:
                scoreA[n, :, q] = -1e30
                continue
            forced = (j == 0) | (j == cur) | (j == cur - 1)
            causal = (j * 64 <= t)
            scoreM[n, :, q] = np.where(forced | ~causal, 0.0, 1.0)
            scoreA[n, :, q] = np.where(forced, 1e9, np.where(causal, 0.0, -1e30))
        cend = (c_of * 16 + 31).reshape(4, 128)
        ok = ((c_of >= 0) & (c_of <= 510)).reshape(4, 128)
        valid = (cend.T[:, :, None] <= t[None, None, :]) & ok.T[:, :, None]
        cmpmask[n] = np.where(valid, 0.0, NEG)
    lk = np.arange(128)[:, None]
    lq = np.arange(128)[None, :]
    maskB = np.stack([np.where(lq >= lk, 0.0, NEG), np.where(lq < lk, 0.0, NEG)]).astype(np.float32)
    estd = (np.arange(8192)[None, :] // 64 == np.arange(128)[:, None]).astype(np.float32)
    cosC, sinC = rope_tables(np.maximum(c_of * 16 + 31, 0))
    bf = ml_dtypes.bfloat16
    return dict(ov_perm=np.ascontiguousarray(ov_perm).astype(bf), scoreM=scoreM, scoreA=scoreA,
                cmpmask=cmpmask.astype(bf), maskB=np.ascontiguousarray(maskB.transpose(1, 0, 2)),
                estd=estd.astype(bf), cosC=cosC, sinC=sinC)


def fused_maps(I):
    hc = host_consts()
    mA = np.ascontiguousarray(masks_A().transpose(1, 0, 2))
    maps = []
    for c in range(8):
        b, r = c // 4, c % 4
        gpos = private_positions(r)
        ok = (gpos >= 0) & (gpos < S)
        xp = np.zeros((TP, D), np.float32)
        xp[ok] = I["x"][b][gpos[ok]]
        p0p = np.zeros((TP, DPLE), np.float32)
        p0p[ok] = I["p"][0][b][gpos[ok]]
        pos = local_positions(r)
        cosP, sinP = rope_tables(np.maximum(gpos, 0))
        validt = np.ascontiguousarray(np.broadcast_to(ok.reshape(64, 128)[:, 0].astype(np.float32)[None, :], (128, 64)))
        maps.append(dict(
            xp=xp, p0p=p0p, p1o=np.ascontiguousarray(I["p"][1][b][pos]), a_w_in=I["a_w_in"][0], a_w_out=I["a_w_out"][0],
            mlp_w1_0=I["mlp_w1"][0], mlp_w1_1=I["mlp_w1"][1], mlp_w2_0=I["mlp_w2"][0], mlp_w2_1=I["mlp_w2"][1],
            ple_w_gate_0=I["ple_w_gate"][0], ple_w_gate_1=I["ple_w_gate"][1], ple_w_proj_0=I["ple_w_proj"][0],
            ple_w_proj_1=I["ple_w_proj"][1], b_w_qg=I["b_w_qg"][0], w_kv=I["w_kv_shared"], b_w_out=I["b_w_out"][0],
            cmp_w1_k=I["cmp_w1_k"], cmp_w1_v=I["cmp_w1_v"], cmp_w2_k=I["cmp_w2_k"], cmp_w2_v=I["cmp_w2_v"],
            cmp_pe_k=I["cmp_pe_k"], cmp_pe_v=I["cmp_pe_v"], cosP=cosP, sinP=sinP,
            g_attn0=gain_layout(I["attn_norm_g"][0]), g_mlp0=gain_layout(I["mlp_norm_g"][0]),
            g_ple0=gain_layout(I["ple_norm_g"][0]), g_attn1=gain_layout(I["attn_norm_g"][1]),
            g_kv=gain_layout(I["kv_norm_g"]), g_mlp1=gain_layout(I["mlp_norm_g"][1]),
            g_ple1=gain_layout(I["ple_norm_g"][1]), g_fin=gain_layout(I["final_norm_g"]),
            validt=validt, maskA=mA, **nsa_private_consts(r), **hc))
    return maps


def kernel(**inputs):
    I = {k_: np.asarray(v) for k_, v in inputs.items()}
    res = run_bass_kernel_spmd(_prog("fused", build_fused), fused_maps(I), core_ids=list(range(8))).results
    out = np.empty((B, S, D), np.float32)
    for c in range(8):
        b, r = c // 4, c % 4
        out[b][local_positions(r)] = np.asarray(res[c]["out"])
    return out
```
